# Optimizing a Trainium2 kernel written in Bass

```python
import math
import jax, jax.numpy as jnp
from jax import lax
import numpy as np

D_MODEL = 2048
BATCH = 4
SEQ = 2048
DEPTH = 2
DEC_BATCH = 128
DEC_SEQ = 4
PAST_LEN = 16384
PAGE_SIZE = 128

N_MIXERS = 2
N_A_LAYERS = (DEPTH + 1) // 2
N_B_LAYERS = DEPTH // 2
A_HEADS = 8
A_DV = D_MODEL // A_HEADS
A_DK = A_DV // 2
A_CHUNK = 64
A_PROJ = 2 * A_HEADS * A_DK + A_HEADS * A_DV + D_MODEL + 2 * A_HEADS
B_CHUNK = 128
B_GROUPS = 8
B_HALF = 3 * D_MODEL
B_GROUP_DIM = B_HALF // B_GROUPS
P_HEADS = 8
P_NKEYS = 128
P_EXPERTS = P_NKEYS * P_NKEYS
P_TOPK = 16
P_DKEY = 256
P_BLOCK = 128
ALPHA = float((2 * DEPTH) ** 0.25)
BETA = float((8 * DEPTH) ** -0.25)
LN_EPS = 1e-5

kernel_name = 'hybrid_mlstm_chunkgmlp_peer_step'


def layer_norm(x, g, b):
    xf = x.astype(jnp.float32)
    mu = jnp.mean(xf, axis=-1, keepdims=True)
    var = jnp.mean(jnp.square(xf - mu), axis=-1, keepdims=True)
    return ((xf - mu) * lax.rsqrt(var + LN_EPS) * g + b).astype(x.dtype)


def mlstm_chunkwise(q, k, v, it, lf, C0, n0, m0, chunk):
    B, T, H, _ = q.shape
    nc = T // chunk
    f32 = jnp.float32

    def to_chunks(a):
        a = a.astype(f32).reshape((B, nc, chunk, H) + a.shape[3:])
        return jnp.moveaxis(a, (1, 3), (0, 2))

    causal = jnp.tril(jnp.ones((chunk, chunk), dtype=bool))

    def step(carry, xs):
        C, n, m = carry
        qc, kc, vc, ic, fc = xs
        b = jnp.cumsum(fc, axis=-1)
        dlog = jnp.where(causal, b[..., :, None] - b[..., None, :] + ic[..., None, :], -jnp.inf)
        inter = b + m[..., None]
        m_t = jnp.maximum(inter, jnp.max(dlog, axis=-1))
        s = jnp.einsum('bhtd,bhsd->bhts', qc, kc) * jnp.exp(dlog - m_t[..., None])
        w_inter = jnp.exp(inter - m_t)
        num = jnp.einsum('bhts,bhse->bhte', s, vc) + w_inter[..., None] * jnp.einsum('bhtd,bhde->bhte', qc, C)
        den = jnp.sum(s, axis=-1) + w_inter * jnp.einsum('bhtd,bhd->bht', qc, n)
        h = num / jnp.maximum(jnp.abs(den), jnp.exp(-m_t))[..., None]
        b_end = b[..., -1]
        g_log = b_end[..., None] - b + ic
        m_new = jnp.maximum(b_end + m, jnp.max(g_log, axis=-1))
        w_k = jnp.exp(g_log - m_new[..., None])
        decay = jnp.exp(b_end + m - m_new)
        C_new = decay[..., None, None] * C + jnp.einsum('bhs,bhsd,bhse->bhde', w_k, kc, vc)
        n_new = decay[..., None] * n + jnp.einsum('bhs,bhsd->bhd', w_k, kc)
        return (C_new, n_new, m_new), h

    xs = (to_chunks(q), to_chunks(k), to_chunks(v), to_chunks(it), to_chunks(lf))
    init = (C0.astype(f32), n0.astype(f32), m0.astype(f32))
    (C, n, m), h = lax.scan(step, init, xs)
    h = jnp.moveaxis(h, (0, 2), (1, 3)).reshape(B, T, H, A_DV)
    return h, C, n, m


def mlstm_mixer(x, C0, n0, m0, w_in, b_gate, hn_gain, w_out):
    B, T, _ = x.shape
    HK, HV = A_HEADS * A_DK, A_HEADS * A_DV
    p = x @ w_in
    q = p[..., :HK].reshape(B, T, A_HEADS, A_DK)
    k = p[..., HK:2 * HK].reshape(B, T, A_HEADS, A_DK) * (A_DK ** -0.5)
    v = p[..., 2 * HK:2 * HK + HV].reshape(B, T, A_HEADS, A_DV)
    o = p[..., 2 * HK + HV:2 * HK + HV + D_MODEL]
    gates = p[..., 2 * HK + HV + D_MODEL:].astype(jnp.float32) + b_gate
    it = gates[..., :A_HEADS]
    lf = jax.nn.log_sigmoid(gates[..., A_HEADS:])
    chunk = math.gcd(A_CHUNK, T)
    h, C, n, m = mlstm_chunkwise(q, k, v, it, lf, C0, n0, m0, chunk)
    h = h * jax.nn.sigmoid(o.astype(jnp.float32)).reshape(B, T, A_HEADS, A_DV)
    mu = jnp.mean(h, axis=-1, keepdims=True)
    var = jnp.mean(jnp.square(h - mu), axis=-1, keepdims=True)
    h = (h - mu) * lax.rsqrt(var + LN_EPS) * hn_gain
    y = h.reshape(B, T, D_MODEL).astype(x.dtype) @ w_out
    return y, C, n, m


def chunk_gmlp_mixer(x, w_in, b_in, lnv_g, lnv_b, w_s, b_s, w_out):
    B, T, _ = x.shape
    z = jax.nn.gelu(x @ w_in + b_in)
    u, v = z[..., :B_HALF], z[..., B_HALF:]
    v = layer_norm(v, lnv_g, lnv_b)
    pad = (-T) % B_CHUNK
    nc = (T + pad) // B_CHUNK
    vc = jnp.pad(v, ((0, 0), (0, pad), (0, 0))).reshape(B, nc, B_CHUNK, B_GROUPS, B_GROUP_DIM)
    ws = jnp.where(jnp.tril(jnp.ones((B_CHUNK, B_CHUNK), dtype=bool)), w_s, jnp.zeros_like(w_s))
    mixed = jnp.einsum('gts,bcsgd->bctgd', ws, vc) + b_s.T[None, None, :, :, None]
    mixed = mixed.reshape(B, T + pad, B_HALF)[:, :T]
    y = (u * mixed) @ w_out
    return y, v


def peer_ffn(x, w_q, sub_keys, exp_u, exp_v):
    B, T, D = x.shape
    N = B * T
    pad = (-N) % P_BLOCK
    xb = jnp.pad(x.reshape(N, D), ((0, pad), (0, 0))).reshape(-1, P_BLOCK, D)
    sk = sub_keys.astype(jnp.float32)

    def block(xt):
        q = (xt @ w_q).astype(jnp.float32).reshape(P_BLOCK, P_HEADS, 2, P_DKEY // 2)
        s = jnp.einsum('bhpd,hpnd->bhpn', q, sk)
        sv, si = lax.top_k(s, P_TOPK)
        cand = (sv[:, :, 0, :, None] + sv[:, :, 1, None, :]).reshape(P_BLOCK, P_HEADS, P_TOPK * P_TOPK)
        cidx = (si[:, :, 0, :, None] * P_NKEYS + si[:, :, 1, None, :]).reshape(P_BLOCK, P_HEADS, P_TOPK * P_TOPK)
        top_s, top_pos = lax.top_k(cand, P_TOPK)
        idx = jnp.take_along_axis(cidx, top_pos, axis=-1)
        g = jax.nn.softmax(top_s, axis=-1)
        a = jnp.einsum('bhkd,bd->bhk', exp_u[idx], xt)
        w = (jax.nn.gelu(a.astype(jnp.float32)) * g).astype(x.dtype)
        return jnp.einsum('bhk,bhkd->bd', w, exp_v[idx])

    out = lax.map(block, xb).reshape(-1, D)[:N]
    return out.reshape(B, T, D)


def trunk(x, C0, n0, m0, w_in_a, b_gate_a, hn_gain_a, w_out_a, w_in_b, b_in_b, lnv_g_b, lnv_b_b,
          w_s_b, b_s_b, w_out_b, ln_mix_g, ln_mix_b, ln_ffn_g, ln_ffn_b,
          peer_w_q, peer_sub_keys, peer_u, peer_v):
    Cs, ns, ms, vs = [], [], [], []
    for i in range(DEPTH):
        j = i // N_MIXERS
        if i % N_MIXERS == 0:
            mix, C, n, m = mlstm_mixer(x, C0[j], n0[j], m0[j], w_in_a[j], b_gate_a[j], hn_gain_a[j], w_out_a[j])
            Cs.append(C)
            ns.append(n)
            ms.append(m)
        else:
            mix, v = chunk_gmlp_mixer(x, w_in_b[j], b_in_b[j], lnv_g_b[j], lnv_b_b[j], w_s_b[j], b_s_b[j], w_out_b[j])
            vs.append(v)
        x = layer_norm(ALPHA * x + mix, ln_mix_g[i], ln_mix_b[i])
        ffn = peer_ffn(x, peer_w_q[i], peer_sub_keys[i], peer_u[i], peer_v[i])
        x = layer_norm(ALPHA * x + ffn, ln_ffn_g[i], ln_ffn_b[i])
    return x, jnp.stack(Cs), jnp.stack(ns), jnp.stack(ms), jnp.stack(vs)


def setup_inputs(seed: int = 0) -> dict:
    key = jax.random.key(seed)
    ks = jax.random.split(key, 28)
    f32 = jnp.float32

    def nrm(k, shape, s):
        return s * jax.random.normal(k, shape, f32)

    b_gate_a = jnp.concatenate([
        nrm(ks[6], (N_A_LAYERS, A_HEADS), 0.1),
        jnp.linspace(3.0, 6.0, A_HEADS, dtype=f32)[None] + nrm(ks[7], (N_A_LAYERS, A_HEADS), 0.1)], axis=-1)
    return {
        'x_prompt': nrm(ks[0], (BATCH, SEQ, D_MODEL), 1.0),
        'x_sample': nrm(ks[1], (DEC_BATCH, DEC_SEQ, D_MODEL), 1.0),
        'state_mlstm_C': nrm(ks[2], (N_A_LAYERS, DEC_BATCH, A_HEADS, A_DK, A_DV), 0.3),
        'state_mlstm_n': nrm(ks[3], (N_A_LAYERS, DEC_BATCH, A_HEADS, A_DK), 0.3),
        'state_mlstm_m': nrm(ks[4], (N_A_LAYERS, DEC_BATCH, A_HEADS), 0.5),
        'w_in_a': nrm(ks[5], (N_A_LAYERS, D_MODEL, A_PROJ), D_MODEL ** -0.5),
        'b_gate_a': b_gate_a,
        'hn_gain_a': 1.0 + nrm(ks[8], (N_A_LAYERS, A_HEADS, A_DV), 0.02),
        'w_out_a': nrm(ks[9], (N_A_LAYERS, D_MODEL, D_MODEL), BETA * D_MODEL ** -0.5),
        'w_in_b': nrm(ks[10], (N_B_LAYERS, D_MODEL, 2 * B_HALF), D_MODEL ** -0.5),
        'b_in_b': nrm(ks[11], (N_B_LAYERS, 2 * B_HALF), 0.02),
        'lnv_g_b': 1.0 + nrm(ks[12], (N_B_LAYERS, B_HALF), 0.02),
        'lnv_b_b': nrm(ks[13], (N_B_LAYERS, B_HALF), 0.02),
        'w_s_b': nrm(ks[14], (N_B_LAYERS, B_GROUPS, B_CHUNK, B_CHUNK), B_CHUNK ** -0.5),
        'b_s_b': 1.0 + nrm(ks[15], (N_B_LAYERS, B_GROUPS, B_CHUNK), 0.02),
        'w_out_b': nrm(ks[16], (N_B_LAYERS, B_HALF, D_MODEL), BETA * B_HALF ** -0.5),
        'ln_mix_g': 1.0 + nrm(ks[17], (DEPTH, D_MODEL), 0.02),
        'ln_mix_b': nrm(ks[18], (DEPTH, D_MODEL), 0.02),
        'ln_ffn_g': 1.0 + nrm(ks[19], (DEPTH, D_MODEL), 0.02),
        'ln_ffn_b': nrm(ks[20], (DEPTH, D_MODEL), 0.02),
        'peer_w_q': nrm(ks[21], (DEPTH, D_MODEL, P_HEADS * P_DKEY), D_MODEL ** -0.5),
        'peer_sub_keys': nrm(ks[22], (DEPTH, P_HEADS, 2, P_NKEYS, P_DKEY // 2), (P_DKEY // 2) ** -0.5),
        'peer_u': nrm(ks[23], (DEPTH, P_EXPERTS, D_MODEL), D_MODEL ** -0.5),
        'peer_v': nrm(ks[24], (DEPTH, P_EXPERTS, D_MODEL), BETA * P_HEADS ** -0.5),
    }


def reference(x_prompt, x_sample, state_mlstm_C, state_mlstm_n, state_mlstm_m,
              w_in_a, b_gate_a, hn_gain_a, w_out_a, w_in_b, b_in_b, lnv_g_b, lnv_b_b,
              w_s_b, b_s_b, w_out_b, ln_mix_g, ln_mix_b, ln_ffn_g, ln_ffn_b,
              peer_w_q, peer_sub_keys, peer_u, peer_v):
    bp = x_prompt.shape[0]
    C0 = jnp.zeros((N_A_LAYERS, bp, A_HEADS, A_DK, A_DV), jnp.float32)
    n0 = jnp.zeros((N_A_LAYERS, bp, A_HEADS, A_DK), jnp.float32)
    m0 = jnp.zeros((N_A_LAYERS, bp, A_HEADS), jnp.float32)
    y_prompt, C_p, n_p, m_p, _ = trunk(
        x_prompt, C0, n0, m0, w_in_a, b_gate_a, hn_gain_a, w_out_a, w_in_b, b_in_b, lnv_g_b, lnv_b_b,
        w_s_b, b_s_b, w_out_b, ln_mix_g, ln_mix_b, ln_ffn_g, ln_ffn_b,
        peer_w_q, peer_sub_keys, peer_u, peer_v)
    y_sample, C_s, n_s, m_s, v_s = trunk(
        x_sample, state_mlstm_C, state_mlstm_n, state_mlstm_m, w_in_a, b_gate_a, hn_gain_a, w_out_a,
        w_in_b, b_in_b, lnv_g_b, lnv_b_b, w_s_b, b_s_b, w_out_b, ln_mix_g, ln_mix_b, ln_ffn_g, ln_ffn_b,
        peer_w_q, peer_sub_keys, peer_u, peer_v)
    return (y_prompt, y_sample, C_p, n_p, m_p, C_s, n_s, m_s, v_s)
```

```python
import contextlib
import numpy as np
import concourse.bass as bass
import concourse.mybir as mybir
from concourse.bass_utils import run_bass_kernel_spmd

F32 = mybir.dt.float32
BF16 = mybir.dt.bfloat16
I32 = mybir.dt.int32
U32 = mybir.dt.uint32
AF = mybir.ActivationFunctionType
ALU = mybir.AluOpType
AX = mybir.AxisListType

NCORES = 8
D = 2048
NT = 1088
NPV = 1024
NALL = NT + NPV
ALPHA = float(4 ** 0.25)
LN_EPS = 1e-5
NEG = -1.0e30
TILES = [(i * 128, 128) for i in range(8)] + [(1024, 64)]
PTILES = [(NT + i * 128, 128) for i in range(8)]


class Op:
    __slots__ = ("eng", "fn", "reads", "writes", "is_dma", "deps", "signal", "sig", "semkey", "idx")

    def __init__(self, eng, fn, reads, writes, is_dma):
        self.eng = eng
        self.fn = fn
        self.reads = tuple(reads)
        self.writes = tuple(writes)
        self.is_dma = is_dma
        self.deps = []
        self.signal = False
        self.sig = None
        self.semkey = None


class Sched:
    ENGS = ("pe", "act", "dve", "pool", "sp")
    SEM_WRAP = 20000

    def __init__(self, nc):
        self.nc = nc
        self.ops = []
        self.allkeys = set()

    def op(self, eng, fn, reads=(), writes=()):
        o = Op(eng, fn, reads, writes, False)
        self.ops.append(o)
        self.allkeys.update(o.reads)
        self.allkeys.update(o.writes)
        return o

    def dma(self, eng, fn, reads=(), writes=(), semkey=None):
        o = Op(eng, fn, reads, writes, True)
        o.semkey = semkey if semkey is not None else o.writes[0]
        self.ops.append(o)
        self.allkeys.update(o.reads)
        self.allkeys.update(o.writes)
        return o

    def barrier(self):
        self.ops.append("BARRIER")
        self.op("sp", lambda e: e.nop(), reads=(), writes=tuple(self.allkeys) + ("__bar",))
        for e in ("pe", "act", "dve", "pool"):
            self.op(e, None, reads=("__bar",))

    def finalize(self):
        nc = self.nc
        last_w = {}
        readers = {}
        phase = 0
        ops2 = []
        self.dma_slot = {}
        slots_in_phase = {}
        for o in self.ops:
            if isinstance(o, str):
                phase += 1
                slots_in_phase = {}
                continue
            if o.is_dma:
                cls = "sw" if o.eng == "pool" else "hw"
                sk = (cls, o.semkey)
                if sk not in slots_in_phase:
                    slots_in_phase[sk] = (cls, sum(1 for k in slots_in_phase if k[0] == cls))
                self.dma_slot[id(o)] = slots_in_phase[sk]
            ops2.append(o)
        self.ops = ops2
        for i, o in enumerate(self.ops):
            o.idx = i
            deps = {}
            for k in o.reads:
                for p in last_w.get(k, ()):
                    deps[p.idx] = p
            for k in o.writes:
                grp = last_w.get(k, ())
                rd = readers.get(k)
                join = o.is_dma and grp and all(p.is_dma for p in grp) and not rd
                if not join:
                    for p in grp:
                        deps[p.idx] = p
                    for r in rd or ():
                        deps[r.idx] = r
            deps.pop(i, None)
            for p in deps.values():
                if p.is_dma:
                    o.deps.append(p)
                elif p.eng == "pe" and o.eng == "pe" and not o.is_dma:
                    continue
                else:
                    if p.fn is None:
                        raise RuntimeError("dependency on a wait-only op")
                    p.signal = True
                    o.deps.append(p)
            for k in o.reads:
                if o.fn is not None:
                    readers.setdefault(k, []).append(o)
            for k in o.writes:
                grp = last_w.get(k, ())
                rd = readers.get(k)
                if o.is_dma and grp and all(p.is_dma for p in grp) and not rd:
                    last_w[k] = [p for p in grp if p.semkey != o.semkey] + [o]
                else:
                    last_w[k] = [o]
                readers[k] = []
        self._stack = contextlib.ExitStack()
        sem_ctr = [0]

        def newsem(name):
            sem_ctr[0] += 1
            return self._stack.enter_context(nc.semaphore(f"{name}{sem_ctr[0]}"))

        eng_sem, eng_cnt, dma_sem, dma_cnt = {}, {}, {}, {}
        for o in self.ops:
            if o.is_dma:
                k = self.dma_slot[id(o)]
                if k not in dma_sem:
                    dma_sem[k] = newsem("d")
                    dma_cnt[k] = 0
                dma_cnt[k] += 16
                o.sig = (dma_sem[k], dma_cnt[k])
            elif o.signal:
                e = o.eng
                if e not in eng_sem or eng_cnt[e] >= self.SEM_WRAP:
                    eng_sem[e] = newsem("e")
                    eng_cnt[e] = 0
                eng_cnt[e] += 1
                o.sig = (eng_sem[e], eng_cnt[e])
        self.n_sems = sem_ctr[0]
        per = {e: [o for o in self.ops if o.eng == e] for e in self.ENGS}
        handles = {"pe": "tensor", "act": "scalar", "dve": "vector", "pool": "gpsimd", "sp": "sync"}

        def emit_stream(eng, h):
            waited = {}
            for o in per[eng]:
                need = {}
                for p in o.deps:
                    s, v = p.sig
                    key = id(s)
                    if key not in need or need[key][1] < v:
                        need[key] = (s, v)
                for key, (s, v) in need.items():
                    if waited.get(key, 0) >= v:
                        continue
                    h.wait_ge(s, v)
                    waited[key] = v
                if o.fn is not None:
                    ins = o.fn(h)
                    if o.sig is not None:
                        ins.then_inc(o.sig[0], 16 if o.is_dma else 1)

        with nc.Block() as block:
            for e in self.ENGS:
                if not per[e]:
                    continue

                def mk(e):
                    def f(h):
                        emit_stream(e, h)
                    return f
                getattr(block, handles[e])(mk(e))
        self._stack.close()


class Arena:
    def __init__(self, ap):
        self.ap = ap
        self.cap = ap.shape[1]
        self.off = 0

    def reset(self):
        self.off = 0

    def _take(self, nwords):
        assert self.off + nwords <= self.cap, f"arena overflow {self.off}+{nwords}>{self.cap}"
        a = self.ap[:, self.off:self.off + nwords]
        self.off += nwords
        return a

    def f32(self, n):
        return self._take(n)

    def i32(self, n):
        return self._take(n).bitcast(I32)

    def u32(self, n):
        return self._take(n).bitcast(U32)

    def bf16(self, n):
        return self._take((n + 1) // 2).bitcast(BF16)[:, :n]


def _consts():
    c = {}
    c["ident"] = np.eye(128, dtype=np.float32)
    s = np.arange(64)
    tri = (s[:, None] <= s[None, :]).astype(np.float32)
    same = (s[:, None] // 4 == s[None, :] // 4).astype(np.float32)
    t64 = np.zeros((128, 64), np.float32)
    t64[:64] = tri
    c["tri"] = t64
    tb = np.zeros((128, 64), np.float32)
    tb[:64] = tri * same
    c["trib"] = tb
    ss = np.zeros((128, 16), np.float32)
    ss[:64] = (s[:, None] // 4 == np.arange(16)[None, :]).astype(np.float32)
    c["seqsel"] = ss
    cm = np.zeros((128, 16, 64), np.float32)
    cm[:] = (np.arange(16)[:, None] == (s[None, :] // 4)).astype(np.float32)[None]
    c["colmask"] = cm.reshape(128, 1024)
    c["ones"] = np.ones((128, 128), np.float32)
    c["iota16"] = np.tile(np.arange(16, dtype=np.float32)[None, :], (128, 1))
    c["thr16"] = np.tile(16.0 * (np.arange(16, dtype=np.float32)[None, :] + 1.0), (128, 1))
    s128 = np.arange(128)
    c["tri128"] = (s128[:, None] <= s128[None, :]).astype(np.float32)
    c["iota128"] = np.tile(np.arange(128, dtype=np.float32)[None, :], (128, 1))
    offs, o = {}, 0
    for k, v in c.items():
        offs[k] = (o, v.shape[1])
        o += v.shape[1]
    pack = np.concatenate([c[k] for k in c], axis=1)
    return pack, offs


CONST_PACK, CONST_OFFS = _consts()


class Builder:
    def __init__(self, debug=False, stop_after=None, dense=True):
        self.debug = debug
        self.dense = dense
        self.stop_after = stop_after
        self.nc = bass.Bass("TRN2", target_bir_lowering=False)
        self.S = Sched(self.nc)
        self.st = contextlib.ExitStack()
        self.uid = 0
        self.store_q = "sp"

    def key(self, name):
        self.uid += 1
        return f"{name}#{self.uid}"

    def dname(self, ap):
        return "dram_" + str(ap.name)

    def din(self, name, shape, dt=F32):
        return self.nc.dram_tensor(name, list(shape), dt, kind="ExternalInput").ap()

    def dout(self, name, shape, dt=F32):
        return self.nc.dram_tensor(name, list(shape), dt, kind="ExternalOutput").ap()

    def dscr(self, name, shape, dt=F32):
        kind = "ExternalOutput" if self.debug else "Internal"
        return self.nc.dram_tensor(name, list(shape), dt, kind=kind).ap()

    def V(self, fn, r=(), w=()):
        return self.S.op("dve", fn, r, w)

    def A(self, fn, r=(), w=()):
        return self.S.op("act", fn, r, w)

    def T(self, fn, r=(), w=()):
        return self.S.op("pe", fn, r, w)

    def G(self, fn, r=(), w=()):
        return self.S.op("pool", fn, r, w)

    def ld(self, out, in_, r=(), w=(), q="sp"):
        return self.S.dma(q, lambda e: e.dma_start(out=out, in_=in_), r, w)

    def ldc(self, out, in_, r=(), w=()):
        return self.S.dma("pool", lambda e: e.dma_start(out=out, in_=in_), r, w)

    def stq(self, out, in_, r, w, q=None):
        q = q or self.store_q
        return self.S.dma(q, lambda e: e.dma_start(out=out, in_=in_), r, w, semkey=("st", r[0]))

    def build(self):
        nc, st = self.nc, self.st
        dbg = self.debug
        x_own = self.din("x_own", [NT, D])
        x_prev = self.din("x_prev", [NPV, D])
        flag = self.din("flag", [128, 1])
        consts_d = self.din("consts", list(CONST_PACK.shape))
        C0s = self.din("C0s", [16 * 8 * 128, 256])
        n0sT = self.din("n0sT", [128, 128])
        m0sT = self.din("m0sT", [8, 16])
        w_in_a = self.din("w_in_a", [D, 6160])
        b_gate = self.din("b_gate", [1, 16])
        hn_gain = self.din("hn_gain", [1, D])
        w_out_a = self.din("w_out_a", [D, D])
        ln_g = {}
        for nm in ("ln_mix_g", "ln_mix_b", "ln_ffn_g", "ln_ffn_b"):
            ln_g[nm] = self.din(nm, [2, D])
        self.ln = ln_g
        self.w_in_b = self.din("w_in_b", [D, 12288])
        self.b_in_b = self.din("b_in_b", [1, 12288])
        self.lnv_g = self.din("lnv_g", [1, 6144])
        self.lnv_b = self.din("lnv_b", [1, 6144])
        self.wsT = self.din("wsT", [128, 8, 128])
        self.wsT_s = self.din("wsT_s", [64, 8, 64])
        self.b_s_t = self.din("b_s_t", [128, 8])
        self.b_s_s = self.din("b_s_s", [64, 8])
        self.w_out_b = self.din("w_out_b", [6144, D])
        self.peer_wq = self.din("peer_wq", [2, D, D])
        self.skT = self.din("skT", [2, 128, 16, 128])
        self.peer_u = self.din("peer_u", [2 * 128 * D, 128])
        self.peer_v = self.din("peer_v", [2 * 16384, D])

        y_own = self.dout("y_own", [NT, D])
        Cp_o = self.dout("Cp", [8 * 128, 256])
        np_o = self.dout("npT", [128, 8])
        mp_o = self.dout("mpT", [8, 1])
        Cs_o = self.dout("Cs", [16 * 8 * 128, 256])
        ns_o = self.dout("nsT", [128, 128])
        ms_o = self.dout("msT", [8, 16])
        vs_o = self.dout("vs", [64, 6144])
        self.outs = dict(y_own=y_own, Cp=Cp_o, npT=np_o, mpT=mp_o, Cs=Cs_o, nsT=ns_o, msT=ms_o, vs=vs_o)

        sc = {}
        sc["qT"] = self.dscr("s_qT", [8, 128, NT], BF16)
        sc["kT"] = self.dscr("s_kT", [8, 128, NT], BF16)
        sc["k"] = self.dscr("s_k", [NALL, 1024], BF16)
        sc["v"] = self.dscr("s_v", [NALL, 2048], BF16)
        sc["og"] = self.dscr("s_og", [NT, 2048], F32)
        sc["gi"] = self.dscr("s_gi", [NALL, 8], F32)
        sc["lf"] = self.dscr("s_lf", [NALL, 8], F32)
        sc["hn"] = self.dscr("s_hn", [NT, D], BF16)
        sc["x1"] = self.dscr("s_x1", [NT, D], F32)
        sc["x2"] = self.dscr("s_x2", [NT, D], F32)
        sc["x3"] = self.dscr("s_x3", [NT, D], F32)
        sc["vraw"] = self.dscr("s_vraw", [NT, 6144], F32)
        sc["mixed"] = self.dscr("s_mixed", [NT, 6144], F32)
        sc["ymix"] = self.dscr("s_ymix", [NT, D], F32)
        sc["G"] = self.nc.dram_tensor("s_G", [9, 128, 128, 128], BF16, kind="Internal").ap()
        self.sc = sc

        ARENA_WORDS = 49 * 1024
        arena_t = st.enter_context(nc.sbuf_tensor("arena", [128, ARENA_WORDS], F32))
        self.ar = Arena(arena_t[:, :])
        cst = st.enter_context(nc.sbuf_tensor("cst", [128, CONST_PACK.shape[1]], F32))
        identb = st.enter_context(nc.sbuf_tensor("identb", [128, 128], BF16))
        colmb = st.enter_context(nc.sbuf_tensor("colmb", [128, 1024], BF16))
        self.ps = [st.enter_context(nc.psum_tensor(f"ps{i}", [128, 512], F32)) for i in range(8)]
        self.psk = [f"ps{i}" for i in range(8)]

        def cv(name):
            o, n = CONST_OFFS[name]
            return cst[:, o:o + n]
        self.ident = cv("ident")
        self.identb = identb[:, :]
        self.tri = cv("tri")
        self.trib = cv("trib")
        self.seqsel = cv("seqsel")
        self.colmb = colmb[:, :].rearrange("p (b c) -> p b c", b=16)
        self.ones = cv("ones")
        self.iota16 = cv("iota16")
        self.thr16 = cv("thr16")
        self.tri128 = cv("tri128")
        self.gelu_native = True
        self.iota128 = cv("iota128")
        self.ld(cst[:, :], consts_d, w=["cst"])
        self.V(lambda e: e.tensor_copy(identb[:, :], cv("ident")), ["cst"], ["identb"])
        self.V(lambda e: e.tensor_copy(colmb[:, :], cv("colmask")), ["cst"], ["colmb"])
        self.CK = ["cst", "identb", "colmb"]

        self.phase_A(x_own, x_prev, w_in_a, b_gate)
        self.S.barrier()
        if self.stop_after != "A":
            self.phase_B(flag, C0s, n0sT, m0sT, hn_gain)
            self.S.barrier()
        if self.stop_after not in ("A", "B"):
            self.phase_C(x_own, w_out_a)
            self.S.barrier()
        srcs = {k: sc[k] for k in ("x1", "x2", "x3")}
        if dbg:
            for k in ("x1", "x2", "x3"):
                srcs[k] = self.din("dbg_" + k, [NT, D])
        if self.stop_after not in ("A", "B", "C"):
            (self.phase_peer_dense if self.dense else self.phase_peer)(srcs["x1"], 0, sc["x2"], "D_")
            self.S.barrier()
        if self.stop_after not in ("A", "B", "C", "D"):
            self.phase_gmlp(srcs["x2"], sc["x3"])
            self.S.barrier()
        if self.stop_after not in ("A", "B", "C", "D", "E"):
            (self.phase_peer_dense if self.dense else self.phase_peer)(srcs["x3"], 1, y_own, "F_")
            self.S.barrier()
        self.S.op("sp", None, reads=tuple(self.S.allkeys))
        self.S.finalize()
        st.close()
        return nc

    def load_xT(self, src, tiles, xT, xTk, col0_of, stg, src_is_bf16=False, kc=16):
        ps, psk = self.ps, self.psk
        for ti, (row0, rows, col0) in enumerate(tiles):
            buf, bk = stg[ti % len(stg)]
            self.ld(buf[:rows, :], src[row0:row0 + rows, :], w=[bk])
            for g4 in range(kc // 4):
                b = (ti * (kc // 4) + g4) % 2
                if src_is_bf16:
                    pt = ps[b][:, 0:256].bitcast(BF16)
                    idn = self.identb
                else:
                    pt = ps[b][:, :]
                    idn = self.ident
                for j in range(4):
                    c = g4 * 4 + j
                    self.T(lambda e, pt=pt, j=j, c=c, buf=buf, rows=rows, idn=idn: e.transpose(
                        pt[:, j * 128:j * 128 + rows], buf[:rows, c * 128:(c + 1) * 128], idn[:rows, :rows]),
                        [bk] + self.CK, [psk[b]])
                src_v = pt.rearrange("p (a b) -> p a b", a=4)[:, :, :rows]
                dst_v = xT[:, g4 * 4:(g4 + 1) * 4, col0:col0 + rows]
                if g4 % 2 == 0:
                    self.V(lambda e, d=dst_v, s=src_v: e.tensor_copy(d, s), [psk[b]], [xTk])
                else:
                    self.A(lambda e, d=dst_v, s=src_v: e.copy(d, s), [psk[b]], [xTk])

    def layer_norm_rows(self, xin, rows, gk, bk_, gtile, btile, out, keys_in, key_out, scr):
        stats, mv, rstd = scr
        for j in range(4):
            self.V(lambda e, j=j: e.bn_stats(stats[:rows, j * 6:(j + 1) * 6], xin[:rows, j * 512:(j + 1) * 512]),
                   keys_in, [key_out + "_st"])
        self.V(lambda e: e.bn_aggr(mv[:rows, :], stats[:rows, :]), [key_out + "_st"], [key_out + "_mv"])
        self.A(lambda e: e.activation(out=rstd[:rows, :], in_=mv[:rows, 1:2], func=AF.Sqrt, bias=self.eps_t[:rows, :], scale=1.0),
               [key_out + "_mv", "eps"], [key_out + "_rs"])
        self.V(lambda e: e.reciprocal(rstd[:rows, :], rstd[:rows, :]), [key_out + "_rs"], [key_out + "_rs"])
        self.V(lambda e: e.tensor_scalar(out[:rows, :], xin[:rows, :], mv[:rows, 0:1], rstd[:rows, 0:1], ALU.subtract, ALU.mult),
               keys_in + [key_out + "_mv", key_out + "_rs"], [key_out])
        self.V(lambda e: e.tensor_tensor(out[:rows, :], out[:rows, :], gtile[:rows, :], ALU.mult), [key_out, gk], [key_out])
        self.V(lambda e: e.tensor_tensor(out[:rows, :], out[:rows, :], btile[:rows, :], ALU.add), [key_out, bk_], [key_out])

    def phase_A(self, x_own, x_prev, w_in_a, b_gate):
        ar, sc, ps, psk = self.ar, self.sc, self.ps, self.psk
        ar.reset()
        xT = ar.bf16(16 * NALL).rearrange("p (c n) -> p c n", c=16)
        xTk = "A_xT"
        stg = [(ar.f32(D), f"A_xs{i}") for i in range(2)]
        wts = [(ar.bf16(16 * 512).rearrange("p (c n) -> p c n", c=16), f"A_w{i}") for i in range(2)]
        ost = [(ar.f32(512), f"A_o{i}") for i in range(4)]
        bg = ar.f32(16)
        sm = [ar.f32(8) for _ in range(4)]
        self.ld(bg, b_gate.partition_broadcast(128).rearrange("p o n -> p (o n)"), w=["A_bg"])
        self.load_xT(x_own, [(r0, rw, r0) for (r0, rw) in TILES], xT, xTk, None, stg)
        self.load_xT(x_prev, [(i * 128, 128, NT + i * 128) for i in range(8)], xT, xTk, None, stg)
        alltiles = TILES + PTILES
        oi = [0]

        def nxt_ost():
            oi[0] += 1
            return ost[oi[0] % 4]

        pb = [0]

        def nxt_bank():
            pb[0] += 1
            return 2 + pb[0] % 6

        for g in range(13):
            ncol = 512 if g < 12 else 16
            wt, wk = wts[g % 2]
            self.ldc(wt[:, :, :ncol], w_in_a[:, g * 512:g * 512 + ncol].rearrange("(c p) n -> p c n", p=128), w=[wk])
            if g < 4:
                dst = sc["qT"] if g < 2 else sc["kT"]
                scale = 1.0 if g < 2 else float(128 ** -0.5)
                for j in range(4):
                    h = (g % 2) * 4 + j
                    for (t0, tn) in ((0, 512), (512, 512), (1024, 64)):
                        b = nxt_bank()
                        for c in range(16):
                            self.T(lambda e, b=b, c=c, j=j, t0=t0, tn=tn, wt=wt: e.matmul(
                                ps[b][:, :tn], lhsT=wt[:, c, j * 128:(j + 1) * 128], rhs=xT[:, c, t0:t0 + tn],
                                start=(c == 0), stop=(c == 15)), [wk, xTk], [psk[b]])
                        ob, ok = nxt_ost()
                        obb = ob.bitcast(BF16)
                        self.A(lambda e, b=b, tn=tn, obb=obb, scale=scale: e.activation(
                            out=obb[:, :tn], in_=ps[b][:, :tn], func=AF.Copy, scale=scale), [psk[b]], [ok])
                        self.stq(dst[h, :, t0:t0 + tn], obb[:, :tn], [ok], [self.dname(dst)])
            if g >= 2:
                tiles = alltiles if (g < 8 or g == 12) else TILES
                for (r0, rw) in tiles:
                    b = nxt_bank()
                    for c in range(16):
                        self.T(lambda e, b=b, c=c, r0=r0, rw=rw, wt=wt, ncol=ncol: e.matmul(
                            ps[b][:rw, :ncol], lhsT=xT[:, c, r0:r0 + rw], rhs=wt[:, c, :ncol],
                            start=(c == 0), stop=(c == 15)), [wk, xTk], [psk[b]])
                    ob, ok = nxt_ost()
                    if g < 4:
                        obb = ob.bitcast(BF16)
                        self.V(lambda e, b=b, rw=rw, obb=obb: e.tensor_scalar(
                            obb[:rw, :512], ps[b][:rw, :], float(128 ** -0.5), None, ALU.mult), [psk[b]], [ok])
                        self.stq(sc["k"][r0:r0 + rw, (g - 2) * 512:(g - 1) * 512], obb[:rw, :512], [ok], ["s_k"])
                    elif g < 8:
                        obb = ob.bitcast(BF16)
                        self.V(lambda e, b=b, rw=rw, obb=obb: e.tensor_copy(obb[:rw, :512], ps[b][:rw, :]), [psk[b]], [ok])
                        self.stq(sc["v"][r0:r0 + rw, (g - 4) * 512:(g - 3) * 512], obb[:rw, :512], [ok], ["s_v"])
                    elif g < 12:
                        self.A(lambda e, b=b, rw=rw, ob=ob: e.activation(out=ob[:rw, :], in_=ps[b][:rw, :], func=AF.Sigmoid),
                               [psk[b]], [ok])
                        self.stq(sc["og"][r0:r0 + rw, (g - 8) * 512:(g - 7) * 512], ob[:rw, :], [ok], ["s_og"])
                    else:
                        gt = ob[:, 0:16]
                        self.V(lambda e, b=b, rw=rw, gt=gt: e.tensor_tensor(gt[:rw, :], ps[b][:rw, :16], bg[:rw, :], ALU.add),
                               [psk[b], "A_bg"], [ok])
                        self.stq(sc["gi"][r0:r0 + rw, :], gt[:rw, 0:8], [ok], ["s_gi"])
                        ob2, ok2 = nxt_ost()
                        xx = gt[:, 8:16]
                        ab, ex, ln_, mn = ob2[:, 0:8], ob2[:, 8:16], ob2[:, 16:24], ob2[:, 24:32]
                        self.A(lambda e, rw=rw, ab=ab, xx=xx: e.activation(out=ab[:rw, :], in_=xx[:rw, :], func=AF.Abs), [ok], [ok2])
                        self.A(lambda e, rw=rw, ab=ab, ex=ex: e.activation(out=ex[:rw, :], in_=ab[:rw, :], func=AF.Exp, scale=-1.0), [ok2], [ok2])
                        self.V(lambda e, rw=rw, ex=ex: e.tensor_scalar_add(ex[:rw, :], ex[:rw, :], 1.0), [ok2], [ok2])
                        self.A(lambda e, rw=rw, ex=ex, ln_=ln_: e.activation(out=ln_[:rw, :], in_=ex[:rw, :], func=AF.Ln), [ok2], [ok2])
                        self.V(lambda e, rw=rw, mn=mn, xx=xx: e.tensor_scalar_min(mn[:rw, :], xx[:rw, :], 0.0), [ok, ok2], [ok2])
                        self.V(lambda e, rw=rw, mn=mn, ln_=ln_: e.tensor_tensor(mn[:rw, :], mn[:rw, :], ln_[:rw, :], ALU.subtract), [ok2], [ok2])
                        self.stq(sc["lf"][r0:r0 + rw, :], mn[:rw, :], [ok2], ["s_lf"])

    def phase_B(self, flag, C0s, n0sT, m0sT, hn_gain):
        ar, sc, ps, psk = self.ar, self.sc, self.ps, self.psk
        self.store_q = "pool"
        ar.reset()
        V, A, T = self.V, self.A, self.T
        CK = self.CK
        ident, tri, trib, seqsel, ones = self.ident, self.tri, self.trib, self.seqsel, self.ones
        CN = ar.f32(8 * 257).rearrange("p (h n) -> p h n", h=8)
        mT = ar.f32(1)
        flg = ar.f32(1)
        gain = ar.f32(D)
        eps_t = ar.f32(1)
        self.eps_t = eps_t
        self.G(lambda e: e.memset(eps_t, LN_EPS), [], ["eps"])
        self.ld(flg, flag, w=["B_flag"])
        self.ld(gain, hn_gain.partition_broadcast(128).rearrange("p o n -> p (o n)"), w=["B_gain"])
        CNK = [f"CN{h}" for h in range(8)]
        V(lambda e: e.memset(CN, 0.0), [], CNK)
        V(lambda e: e.memset(mT, 0.0), [], ["mT"])
        NB = 2

        def mk(n, f):
            return [f() for _ in range(n)]
        gi_b = mk(NB, lambda: ar.f32(8))
        lf_b = mk(NB, lambda: ar.f32(8))
        k_b = mk(NB, lambda: ar.bf16(1024))
        v_b = mk(NB, lambda: ar.bf16(2048))
        qT_b = mk(NB, lambda: ar.bf16(8 * 64).rearrange("p (h n) -> p h n", h=8))
        kT_b = mk(NB, lambda: ar.bf16(8 * 64).rearrange("p (h n) -> p h n", h=8))
        og_b = mk(NB, lambda: ar.f32(2048))
        sca = ar.f32(64)
        a_t, e_t, cl_t, bM_t = sca[:, 0:8], sca[:, 8:16], sca[:, 16:24], sca[:, 24:32]
        hm = ar.f32(160)
        vpp = ar.bf16(8 * 257).rearrange("p (h n) -> p h n", h=8)
        stm = ar.bf16(8 * 64).rearrange("p (h n) -> p h n", h=8)
        cns = mk(2, lambda: ar.bf16(257))
        Mf = ar.f32(256)
        Mtok = ar.f32(8)
        hg = ar.f32(2048)
        hnb = mk(2, lambda: ar.bf16(2048))
        stt = ar.f32(8 * 6)
        mvv = ar.f32(16)
        rr = ar.f32(16)
        den = ar.f32(8)
        Rm = ar.f32(256)
        Mx = ar.f32(128)
        NCS = 6
        cst_s = mk(NCS, lambda: ar.f32(257))
        kz_all = ar.bf16(16 * 1024).rearrange("p (b n) -> p b n", b=16)
        qz_all = ar.bf16(8 * 16 * 64).rearrange("p (h b n) -> p h b n", h=8, b=16)
        nsT = ar.f32(128)
        n0t = ar.f32(128)
        m0t = ar.f32(16)
        self.ld(n0t, n0sT, w=["B_n0"])
        self.ld(m0t[:8, :], m0sT, w=["B_m0"])

        def chunk(ci, tok0, mode):
            sl = ci % NB
            ks = f"B{sl}"
            samp = mode == "sample"
            NS, LS = (16, 4) if samp else (1, 64)
            trim = trib if samp else tri
            full = mode != "prefix"
            gi, lf, kk, vv, qT, kT, og = gi_b[sl], lf_b[sl], k_b[sl], v_b[sl], qT_b[sl], kT_b[sl], og_b[sl]
            self.ld(gi[:64, :], sc["gi"][tok0:tok0 + 64, :], w=[ks + "gi"])
            self.ld(lf[:64, :], sc["lf"][tok0:tok0 + 64, :], w=[ks + "lf"])
            self.ld(kk[:64, :], sc["k"][tok0:tok0 + 64, :], w=[ks + "k"])
            self.ld(vv[:64, :], sc["v"][tok0:tok0 + 64, :], w=[ks + "v"])
            if full:
                self.ld(qT, sc["qT"][:, :, tok0:tok0 + 64].rearrange("h p n -> p h n"), w=[ks + "qT"])
                self.ld(kT, sc["kT"][:, :, tok0:tok0 + 64].rearrange("h p n -> p h n"), w=[ks + "kT"])
                self.ld(og[:64, :], sc["og"][tok0:tok0 + 64, :], w=[ks + "og"])
            T(lambda e: e.matmul(ps[0][:64, 0:8], lhsT=trim[:64, :], rhs=lf[:64, :], start=True, stop=True), [ks + "lf"] + CK, ["ps0"])
            V(lambda e: e.tensor_tensor(a_t[:64, :], gi[:64, :], ps[0][:64, 0:8], ALU.subtract), [ks + "gi", "ps0"], ["a_t"])
            T(lambda e: e.transpose(ps[0][:8, 64:128], a_t[:64, :], ident[:64, :64]), ["a_t"] + CK, ["ps0"])
            amax = hm[:8, 0:NS]
            V(lambda e: e.tensor_reduce(amax, ps[0][:8, 64:128].rearrange("p (b t) -> p b t", b=NS), AX.X, ALU.max), ["ps0"], ["amax"])
            m_old = m0t[:8, 0:16] if samp else mT[:8, 0:1]
            mk_old = "B_m0" if samp else "mT"
            MT = hm[:8, 16:16 + NS]
            fT = hm[:8, 32:32 + NS]
            V(lambda e: e.tensor_tensor(MT, amax, m_old, ALU.max), ["amax", mk_old], ["MT"])
            V(lambda e: e.tensor_tensor(fT, m_old, MT, ALU.subtract), ["MT", mk_old], ["fT"])
            A(lambda e: e.activation(out=fT, in_=fT, func=AF.Exp), ["fT"], ["fT"])
            selm = seqsel[:64, 0:16] if samp else ones[:64, 0:1]
            T(lambda e: e.matmul(ps[0][:8, 128:128 + NS], lhsT=lf[:64, :], rhs=selm, start=True, stop=True), [ks + "lf"] + CK, ["ps0"])
            mnew = hm[:8, 48:48 + NS]
            V(lambda e: e.tensor_tensor(mnew, ps[0][:8, 128:128 + NS], MT, ALU.add), ["ps0", "MT"], ["mnew"])
            V(lambda e: e.tensor_copy(Mx[:8, 0:64].rearrange("p (b t) -> p b t", b=NS), MT.unsqueeze(2).broadcast_to([8, NS, LS])), ["MT"], ["Mx"])
            T(lambda e: e.transpose(ps[0][:64, 160:168], Mx[:8, 0:64], ident[:8, :8]), ["Mx"] + CK, ["ps0"])
            V(lambda e: e.tensor_copy(Mtok[:64, :], ps[0][:64, 160:168]), ["ps0"], ["Mtok"])
            V(lambda e: e.tensor_tensor(e_t[:64, :], a_t[:64, :], Mtok[:64, :], ALU.subtract), ["a_t", "Mtok"], ["e_t"])
            A(lambda e: e.activation(out=e_t[:64, :], in_=e_t[:64, :], func=AF.Exp), ["e_t"], ["e_t"])
            if full:
                V(lambda e: e.tensor_tensor(bM_t[:64, :], gi[:64, :], a_t[:64, :], ALU.subtract), [ks + "gi", "a_t"], ["bM"])
                V(lambda e: e.tensor_tensor(bM_t[:64, :], bM_t[:64, :], Mtok[:64, :], ALU.add), ["bM", "Mtok"], ["bM"])
                A(lambda e: e.activation(out=cl_t[:64, :], in_=bM_t[:64, :], func=AF.Exp, scale=-1.0), ["bM"], ["cl_t"])
            Rv = Rm[:8, 0:NS * 8].rearrange("p (b h) -> p b h", b=NS)
            V(lambda e: e.tensor_tensor(Rv, fT.unsqueeze(2).broadcast_to([8, NS, 8]),
                                        ident[:8, 0:8].unsqueeze(1).broadcast_to([8, NS, 8]), ALU.mult), ["fT"] + CK, ["Rm"])
            T(lambda e: e.matmul(ps[0][:, 256:256 + NS * 8], lhsT=ones[:8, :], rhs=Rm[:8, 0:NS * 8], start=True, stop=True), ["Rm"] + CK, ["ps0"])
            V(lambda e: e.tensor_copy(Mf[:, 0:NS * 8], ps[0][:, 256:256 + NS * 8]), ["ps0"], ["Mf"])
            V(lambda e: e.tensor_tensor(vpp[:64, :, 0:256], vv[:64, :].rearrange("p (h n) -> p h n", h=8),
                                        e_t[:64, :].unsqueeze(2).broadcast_to([64, 8, 256]), ALU.mult), [ks + "v", "e_t"], ["vpp"])
            V(lambda e: e.tensor_copy(vpp[:64, :, 256:257], e_t[:64, :].unsqueeze(2)), ["e_t", "vpp"], ["vpp"])
            if full:
                for h in range(8):
                    T(lambda e, h=h: e.matmul(ps[1][:64, h * 64:(h + 1) * 64], lhsT=kT[:, h, :], rhs=qT[:, h, :], start=True, stop=True),
                      [ks + "kT", ks + "qT"], ["ps1"])
                V(lambda e: e.tensor_tensor(stm[:64, :, :], ps[1][:64, :].rearrange("p (h n) -> p h n", h=8),
                                            trim[:64, :].unsqueeze(1).broadcast_to([64, 8, 64]), ALU.mult), ["ps1"] + CK, ["stm"])
            spend = []
            if samp:
                for b in range(16):
                    V(lambda e, b=b: e.tensor_scalar(kz_all[:64, b, :], kk[:64, :], seqsel[:64, b:b + 1], None, ALU.mult), [ks + "k"] + CK, ["kz_all"])
                for h in range(8):
                    V(lambda e, h=h: e.tensor_tensor(qz_all[:, h], qT[:, h, :].unsqueeze(1).broadcast_to([128, 16, 64]), self.colmb, ALU.mult), [ks + "qT"] + CK, ["qz_all"])
            for h in range(8):
                nb = 2 + h % 2
                nk = f"psn{h % 2}"
                if full:
                    T(lambda e, h=h, nb=nb: e.matmul(ps[nb][:64, 0:257], lhsT=stm[:64, h, :], rhs=vpp[:64, h, :], start=True, stop=False),
                      ["stm", "vpp"], [nk])
                if not samp:
                    cb, ckk = cns[h % 2], f"cns{h % 2}"
                    if full:
                        A(lambda e, h=h, cb=cb: e.activation(out=cb, in_=CN[:, h, :], func=AF.Copy, scale=Mf[:, h:h + 1]), [f"CN{h}", "Mf"], [ckk])
                        T(lambda e, h=h, nb=nb, cb=cb: e.matmul(ps[nb][:64, 0:257], lhsT=qT[:, h, :], rhs=cb, start=False, stop=True),
                          [ckk, ks + "qT"], [nk])
                    kb = 4 + h % 2
                    T(lambda e, h=h, kb=kb: e.matmul(ps[kb][:, 0:257], lhsT=kk[:64, h * 128:(h + 1) * 128], rhs=vpp[:64, h, :], start=True, stop=True),
                      [ks + "k", "vpp"], [f"psk{h % 2}"])
                    V(lambda e, h=h, kb=kb: e.scalar_tensor_tensor(out=CN[:, h, :], in0=CN[:, h, :], scalar=Mf[:, h:h + 1], in1=ps[kb][:, 0:257],
                                                                    op0=ALU.mult, op1=ALU.add), [f"CN{h}", "Mf", f"psk{h % 2}"], [f"CN{h}"])
                else:
                    for b in range(16):
                        i = h * 16 + b
                        cs_, csk = cst_s[i % NCS], f"cs{i % NCS}"
                        row = (b * 8 + h) * 128
                        self.ld(cs_[:, 0:256], C0s[row:row + 128, :], w=[csk])
                        V(lambda e, cs_=cs_, b=b, h=h: e.tensor_copy(cs_[:, 256:257], n0t[:, b * 8 + h:b * 8 + h + 1]), ["B_n0", csk], [csk])
                        cb, ckk = cns[i % 2], f"cns{i % 2}"
                        A(lambda e, cb=cb, cs_=cs_, b=b, h=h: e.activation(out=cb, in_=cs_, func=AF.Copy, scale=Mf[:, b * 8 + h:b * 8 + h + 1]),
                          [csk, "Mf"], [ckk])
                        T(lambda e, nb=nb, cb=cb, b=b, h=h: e.matmul(ps[nb][:64, 0:257], lhsT=qz_all[:, h, b, :], rhs=cb, start=False, stop=(b == 15)),
                          ["qz_all", ckk], [nk])
                        kb = 4 + i % 2
                        T(lambda e, kb=kb, h=h, b=b: e.matmul(ps[kb][:, 0:257], lhsT=kz_all[:64, b, h * 128:(h + 1) * 128], rhs=vpp[:64, h, :], start=True, stop=True),
                          ["kz_all", "vpp"], [f"psk{i % 2}"])
                        def back_(cs_=cs_, csk=csk, kb=kb, b=b, h=h, i=i, row=row):
                            V(lambda e: e.scalar_tensor_tensor(out=cs_, in0=cs_, scalar=Mf[:, b * 8 + h:b * 8 + h + 1], in1=ps[kb][:, 0:257],
                                                               op0=ALU.mult, op1=ALU.add), [csk, "Mf", f"psk{i % 2}"], [csk])
                            self.stq(self.outs["Cs"][row:row + 128, :], cs_[:, 0:256], [csk], ["Cs"])
                            V(lambda e: e.tensor_copy(nsT[:, b * 8 + h:b * 8 + h + 1], cs_[:, 256:257]), [csk, "nsT"], ["nsT"])
                        if spend:
                            spend.pop(0)()
                        spend.append(back_)
                if full:
                    A(lambda e, h=h, nb=nb: e.activation(out=den[:64, h:h + 1], in_=ps[nb][:64, 256:257], func=AF.Abs), [nk], ["den"])
                    V(lambda e, h=h: e.tensor_tensor(den[:64, h:h + 1], den[:64, h:h + 1], cl_t[:64, h:h + 1], ALU.max), ["den", "cl_t"], ["den"])
                    V(lambda e, h=h: e.reciprocal(den[:64, h:h + 1], den[:64, h:h + 1]), ["den"], ["den"])
                    V(lambda e, h=h, nb=nb: e.scalar_tensor_tensor(out=hg[:64, h * 256:(h + 1) * 256], in0=ps[nb][:64, 0:256], scalar=den[:64, h:h + 1],
                                                                    in1=og[:64, h * 256:(h + 1) * 256], op0=ALU.mult, op1=ALU.mult),
                      [nk, "den", ks + "og"], ["hg"])
                    V(lambda e, h=h: e.bn_stats(stt[:64, h * 6:(h + 1) * 6], hg[:64, h * 256:(h + 1) * 256]), ["hg"], ["stt"])
                    V(lambda e, h=h: e.bn_aggr(mvv[:64, h * 2:(h + 1) * 2], stt[:64, h * 6:(h + 1) * 6]), ["stt"], ["mvv"])
            if full:
                mv3 = mvv[:64, :].rearrange("p (h t) -> p h t", t=2)
                A(lambda e: e.activation(out=rr[:64, 0:8], in_=mv3[:, :, 1], func=AF.Sqrt, bias=eps_t[:64, :], scale=1.0), ["mvv", "eps"], ["rr"])
                V(lambda e: e.reciprocal(rr[:64, 0:8], rr[:64, 0:8]), ["rr"], ["rr"])
                for h in range(8):
                    V(lambda e, h=h: e.tensor_scalar(hg[:64, h * 256:(h + 1) * 256], hg[:64, h * 256:(h + 1) * 256], mvv[:64, 2 * h:2 * h + 1], rr[:64, h:h + 1],
                                                     ALU.subtract, ALU.mult), ["hg", "mvv", "rr"], ["hg"])
                ob, obk = hnb[ci % 2], f"hnb{ci % 2}"
                V(lambda e, ob=ob: e.tensor_tensor(ob[:64, :], hg[:64, :], gain[:64, :], ALU.mult), ["hg", "B_gain"], [obk])
                self.stq(sc["hn"][tok0:tok0 + 64, :], ob[:64, :], [obk], ["s_hn"])
            while spend:
                spend.pop(0)()
            if not samp:
                V(lambda e: e.tensor_copy(mT[:8, :], mnew), ["mnew"], ["mT"])
            else:
                V(lambda e: e.tensor_copy(hm[:8, 64:80], mnew), ["mnew"], ["ms_out"])
                self.stq(self.outs["msT"], hm[:8, 64:80], ["ms_out"], ["msT"])
                self.stq(self.outs["nsT"], nsT, ["nsT"], ["nsTo"])

        for ci in range(16):
            chunk(ci, NT + ci * 64, "prefix")
        V(lambda e: e.tensor_scalar(CN, CN, flg[:, 0:1], None, ALU.mult), CNK + ["B_flag"], CNK)
        V(lambda e: e.tensor_scalar(mT[:8, :], mT[:8, :], flg[:8, 0:1], None, ALU.mult), ["mT", "B_flag"], ["mT"])
        for ci in range(16):
            chunk(ci, ci * 64, "own")
        for h in range(8):
            self.stq(self.outs["Cp"][h * 128:(h + 1) * 128, :], CN[:, h, 0:256], [f"CN{h}"], ["Cp"])
        V(lambda e: e.tensor_copy(Rm[:, 128:136], CN[:, :, 256]), CNK + ["Rm"], ["npo"])
        self.stq(self.outs["npT"], Rm[:, 128:136], ["npo"], ["npT"])
        self.stq(self.outs["mpT"], mT[:8, :], ["mT"], ["mpT"])
        chunk(16, 1024, "sample")
        self.store_q = "sp"

    def out_proj_ln(self, src_bf16, resid_src, w_dram, kc, li, which, dst, prefix):
        ar, ps, psk = self.ar, self.ps, self.psk
        V, A, T = self.V, self.A, self.T
        ar.reset()
        eps_t = ar.f32(1)
        self.eps_t = eps_t
        self.G(lambda e: e.memset(eps_t, LN_EPS), [], ["eps"])
        xT = ar.bf16(kc * NT).rearrange("p (c n) -> p c n", c=kc)
        xTk = prefix + "xT"
        stg = [(ar.bf16(kc * 128), f"{prefix}xs{i}") for i in range(2)]
        self.load_xT(src_bf16, [(r0, rw, r0) for (r0, rw) in TILES], xT, xTk, None, stg, src_is_bf16=True, kc=kc)
        wt = ar.bf16(kc * D).rearrange("p (c n) -> p c n", c=kc)
        for q4 in range(4):
            self.ldc(wt[:, :, q4 * 512:(q4 + 1) * 512], w_dram[:, q4 * 512:(q4 + 1) * 512].rearrange("(c p) n -> p c n", p=128), w=[prefix + "w"])
        gt, bt = ar.f32(D), ar.f32(D)
        self.ld(gt, self.ln[f"ln_{which}_g"][li:li + 1, :].partition_broadcast(128).rearrange("p o n -> p (o n)"), w=[prefix + "g"])
        self.ld(bt, self.ln[f"ln_{which}_b"][li:li + 1, :].partition_broadcast(128).rearrange("p o n -> p (o n)"), w=[prefix + "b"])
        xr = [(ar.f32(D), f"{prefix}xr{i}") for i in range(2)]
        pre = [(ar.f32(D), f"{prefix}pre{i}") for i in range(2)]
        scr = (ar.f32(24), ar.f32(2), ar.f32(1))
        for ti, (r0, rw) in enumerate(TILES):
            xb, xk = xr[ti % 2]
            pb, pk = pre[ti % 2]
            self.ld(xb[:rw, :], resid_src[r0:r0 + rw, :], w=[xk])
            for q4 in range(4):
                b = 2 + (ti * 4 + q4) % 6
                for c in range(kc):
                    T(lambda e, b=b, c=c, q4=q4, r0=r0, rw=rw: e.matmul(ps[b][:rw, :], lhsT=xT[:, c, r0:r0 + rw], rhs=wt[:, c, q4 * 512:(q4 + 1) * 512],
                                                                         start=(c == 0), stop=(c == kc - 1)), [xTk, prefix + "w"], [psk[b]])
                V(lambda e, b=b, q4=q4, rw=rw, xb=xb, pb=pb: e.scalar_tensor_tensor(out=pb[:rw, q4 * 512:(q4 + 1) * 512], in0=xb[:rw, q4 * 512:(q4 + 1) * 512],
                                                                                     scalar=ALPHA, in1=ps[b][:rw, :], op0=ALU.mult, op1=ALU.add),
                  [xk, psk[b]], [pk])
            self.layer_norm_rows(pb, rw, prefix + "g", prefix + "b", gt, bt, pb, [pk], pk, scr)
            self.stq(dst[r0:r0 + rw, :], pb[:rw, :], [pk], [self.dname(dst)], q="pool")

    def phase_C(self, x_own, w_out_a):
        self.out_proj_ln(self.sc["hn"], x_own, w_out_a, 16, 0, "mix", self.sc["x1"], "C_")


    def gelu_tanh(self, out, in_, tmp, rows, kin, kout, ktmp):
        V, A = self.V, self.A
        if self.gelu_native:
            A(lambda e: e.activation(out=out, in_=in_, func=AF.Gelu_apprx_tanh), kin, [kout])
            return
        A(lambda e: e.activation(out=tmp, in_=in_, func=AF.Square), kin, [ktmp])
        V(lambda e: e.tensor_scalar(tmp, tmp, 0.044715, 1.0, ALU.mult, ALU.add), [ktmp], [ktmp])
        V(lambda e: e.tensor_tensor(tmp, tmp, in_, ALU.mult), [ktmp] + kin, [ktmp])
        A(lambda e: e.activation(out=tmp, in_=tmp, func=AF.Tanh, scale=0.7978845608028654), [ktmp], [ktmp])
        V(lambda e: e.tensor_scalar(tmp, tmp, 1.0, 0.5, ALU.add, ALU.mult), [ktmp], [ktmp])
        V(lambda e: e.tensor_tensor(out, tmp, in_, ALU.mult), [ktmp] + kin, [kout])

    def phase_peer(self, src, li, dst, pf):
        ar, ps, psk = self.ar, self.ps, self.psk
        V, A, T = self.V, self.A, self.T
        ar.reset()
        eps_t = ar.f32(1)
        self.eps_t = eps_t
        self.G(lambda e: e.memset(eps_t, LN_EPS), [], ["eps"])
        qT = ar.bf16(16 * NT).rearrange("p (c n) -> p c n", c=16)
        skt = ar.bf16(16 * 128).rearrange("p (c n) -> p c n", c=16)
        gt, bt = ar.f32(D), ar.f32(D)
        self.ld(gt, self.ln["ln_ffn_g"][li:li + 1, :].partition_broadcast(128).rearrange("p o n -> p (o n)"), w=[pf + "g"])
        self.ld(bt, self.ln["ln_ffn_b"][li:li + 1, :].partition_broadcast(128).rearrange("p o n -> p (o n)"), w=[pf + "b"])
        self.ldc(skt, self.skT[li], w=[pf + "sk"])
        mark = ar.off
        xT = ar.bf16(16 * NT).rearrange("p (c n) -> p c n", c=16)
        stg = [(ar.f32(D), f"{pf}xs{i}") for i in range(2)]
        wq = ar.bf16(16 * D).rearrange("p (c n) -> p c n", c=16)
        for q4 in range(4):
            self.ldc(wq[:, :, q4 * 512:(q4 + 1) * 512], self.peer_wq[li, :, q4 * 512:(q4 + 1) * 512].rearrange("(c p) n -> p c n", p=128), w=[pf + "wq"])
        self.load_xT(src, [(r0, rw, r0) for (r0, rw) in TILES], xT, pf + "xT", None, stg)
        n = 0
        for cg in range(16):
            for (t0, tn) in ((0, 512), (512, 512), (1024, 64)):
                b = 2 + n % 6
                n += 1
                for c in range(16):
                    T(lambda e, b=b, c=c, cg=cg, t0=t0, tn=tn: e.matmul(ps[b][:, :tn], lhsT=wq[:, c, cg * 128:(cg + 1) * 128], rhs=xT[:, c, t0:t0 + tn],
                                                                       start=(c == 0), stop=(c == 15)), [pf + "wq", pf + "xT"], [psk[b]])
                if n % 2:
                    V(lambda e, b=b, cg=cg, t0=t0, tn=tn: e.tensor_copy(qT[:, cg, t0:t0 + tn], ps[b][:, :tn]), [psk[b]], [pf + "qT"])
                else:
                    A(lambda e, b=b, cg=cg, t0=t0, tn=tn: e.copy(qT[:, cg, t0:t0 + tn], ps[b][:, :tn]), [psk[b]], [pf + "qT"])
        self.S.barrier()
        ar.off = mark
        def two(f):
            return [f(), f()]
        xt = two(lambda: ar.f32(D))
        idx = two(lambda: ar.i32(128))
        gw = two(lambda: ar.f32(128))
        Sc = ar.f32(2048).rearrange("p (c n) -> p c n", c=16)
        Wk = ar.f32(2048).rearrange("p (c n) -> p c n", c=16)
        sv = ar.f32(256).rearrange("p (c n) -> p c n", c=16)
        si = ar.u32(256).rearrange("p (c n) -> p c n", c=16)
        sif = ar.f32(256).rearrange("p (h t n) -> p h t n", h=8, t=2)
        cand = ar.f32(2048).rearrange("p (h n) -> p h n", h=8)
        cw = ar.f32(2048).rearrange("p (h n) -> p h n", h=8)
        cs = ar.f32(128).rearrange("p (h n) -> p h n", h=8)
        cp = ar.u32(128).rearrange("p (h n) -> p h n", h=8)
        ci = ar.u32(128).rearrange("p (h n) -> p h n", h=8)
        cj = ar.u32(128).rearrange("p (h n) -> p h n", h=8)
        cif = ar.f32(128).rearrange("p (h n) -> p h n", h=8)
        cjf = ar.f32(128).rearrange("p (h n) -> p h n", h=8)
        oh = ar.f32(2048).rearrange("p (h k n) -> p h k n", h=8, k=16)
        n0 = ar.f32(128).rearrange("p (h n) -> p h n", h=8)
        n1 = ar.f32(128).rearrange("p (h n) -> p h n", h=8)
        zz = ar.f32(16)
        av = ar.f32(128)
        wv = ar.f32(128)
        gtmp = ar.f32(128)
        NG = 4
        gb = [ar.f32(D) for _ in range(NG)]
        junk = ar.bf16(D)
        acc = two(lambda: ar.f32(D))
        scr = (ar.f32(24), ar.f32(2), ar.f32(1))
        sv4 = sv.rearrange("p (h t) n -> p h t n", t=2)
        for i in range(2):
            V(lambda e, i=i: e.memset(idx[i], 0), [], [f"{pf}idx{i}"])

        def topk(ti):
            r0, rw = TILES[ti]
            s2 = ti % 2
            xk, ik, gk = f"{pf}xt{s2}", f"{pf}idx{s2}", f"{pf}gw{s2}"
            self.ld(xt[s2][:rw, :], src[r0:r0 + rw, :], w=[xk])
            for g4 in range(4):
                b = 2 + (ti * 4 + g4) % 6
                for j in range(4):
                    cg = g4 * 4 + j
                    T(lambda e, b=b, j=j, cg=cg: e.matmul(ps[b][:rw, j * 128:(j + 1) * 128], lhsT=qT[:, cg, r0:r0 + rw], rhs=skt[:, cg, :], start=True, stop=True),
                      [pf + "qT", pf + "sk"], [psk[b]])
                dstv = Sc[:rw, g4 * 4:(g4 + 1) * 4, :]
                srcv = ps[b][:rw, :].rearrange("p (a n) -> p a n", a=4)
                wkeys = [f"{pf}S{g4 * 4 + j}" for j in range(4)]
                if g4 % 2:
                    A(lambda e, d=dstv, s=srcv: e.copy(d, s), [psk[b]], wkeys)
                else:
                    V(lambda e, d=dstv, s=srcv: e.tensor_copy(d, s), [psk[b]], wkeys)
            for cg in range(16):
                V(lambda e, cg=cg: e.max(out=sv[:rw, cg, 0:8], in_=Sc[:rw, cg, :]), [f"{pf}S{cg}"], [f"{pf}sv{cg}"])
            for cg in range(16):
                V(lambda e, cg=cg: e.max_index(out=si[:rw, cg, 0:8], in_max=sv[:rw, cg, 0:8], in_values=Sc[:rw, cg, :]), [f"{pf}S{cg}", f"{pf}sv{cg}"], [f"{pf}si{cg}"])
            for cg in range(16):
                V(lambda e, cg=cg: e.match_replace(out=Wk[:rw, cg, :], in_to_replace=sv[:rw, cg, 0:8], in_values=Sc[:rw, cg, :], imm_value=NEG),
                  [f"{pf}S{cg}", f"{pf}sv{cg}"], [f"{pf}W{cg}"])
            for cg in range(16):
                V(lambda e, cg=cg: e.max(out=sv[:rw, cg, 8:16], in_=Wk[:rw, cg, :]), [f"{pf}W{cg}"], [f"{pf}sv{cg}"])
            for cg in range(16):
                V(lambda e, cg=cg: e.max_index(out=si[:rw, cg, 8:16], in_max=sv[:rw, cg, 8:16], in_values=Wk[:rw, cg, :]), [f"{pf}W{cg}", f"{pf}sv{cg}"], [f"{pf}si{cg}"])
            svk = [f"{pf}sv{cg}" for cg in range(16)]
            sik = [f"{pf}si{cg}" for cg in range(16)]
            V(lambda e: e.tensor_copy(sif[:rw].rearrange("p h t n -> p (h t) n"), si[:rw]), sik, [pf + "sif"])
            V(lambda e: e.tensor_tensor(cand[:rw].rearrange("p h (i j) -> p h i j", i=16),
                                        sv4[:rw, :, 0, :].unsqueeze(3).broadcast_to([rw, 8, 16, 16]),
                                        sv4[:rw, :, 1, :].unsqueeze(2).broadcast_to([rw, 8, 16, 16]), ALU.add), svk, [pf + "cand"])
            ck = [f"{pf}c{h}" for h in range(8)]
            for h in range(8):
                V(lambda e, h=h: e.max(out=cs[:rw, h, 0:8], in_=cand[:rw, h, :]), [pf + "cand"], [ck[h] + "s"])
            for h in range(8):
                V(lambda e, h=h: e.max_index(out=cp[:rw, h, 0:8], in_max=cs[:rw, h, 0:8], in_values=cand[:rw, h, :]), [pf + "cand", ck[h] + "s"], [ck[h] + "p"])
            for h in range(8):
                V(lambda e, h=h: e.match_replace(out=cw[:rw, h, :], in_to_replace=cs[:rw, h, 0:8], in_values=cand[:rw, h, :], imm_value=NEG),
                  [pf + "cand", ck[h] + "s"], [ck[h] + "w"])
            for h in range(8):
                V(lambda e, h=h: e.max(out=cs[:rw, h, 8:16], in_=cw[:rw, h, :]), [ck[h] + "w"], [ck[h] + "s"])
            for h in range(8):
                V(lambda e, h=h: e.max_index(out=cp[:rw, h, 8:16], in_max=cs[:rw, h, 8:16], in_values=cw[:rw, h, :]), [ck[h] + "w", ck[h] + "s"], [ck[h] + "p"])
            csk = [c + "s" for c in ck]
            cpk = [c + "p" for c in ck]
            cpf = n1
            V(lambda e: e.tensor_copy(cpf[:rw], cp[:rw]), cpk, [pf + "cpf"])
            th4 = self.thr16[:rw, :].unsqueeze(1).unsqueeze(1).broadcast_to([rw, 8, 16, 16])
            V(lambda e: e.tensor_tensor(oh[:rw], cpf[:rw].unsqueeze(3).broadcast_to([rw, 8, 16, 16]), th4, ALU.is_ge), [pf + "cpf"] + self.CK, [pf + "oh"])
            V(lambda e: e.tensor_reduce(cif[:rw], oh[:rw], AX.X, ALU.add), [pf + "oh"], [pf + "cif"])
            V(lambda e: e.scalar_tensor_tensor(out=cjf[:rw], in0=cif[:rw], scalar=-16.0, in1=cpf[:rw], op0=ALU.mult, op1=ALU.add), [pf + "cif", pf + "cpf"], [pf + "cjf"])
            io4 = self.iota16[:rw, :].unsqueeze(1).unsqueeze(1).broadcast_to([rw, 8, 16, 16])
            for (cf, half, nn, kk) in ((cif, 0, n0, "n0"), (cjf, 1, n1, "n1")):
                V(lambda e, cf=cf: e.tensor_tensor(oh[:rw], cf[:rw].unsqueeze(3).broadcast_to([rw, 8, 16, 16]), io4, ALU.is_equal),
                  [pf + "cif", pf + "cjf"] + self.CK, [pf + "oh"])
                V(lambda e, half=half: e.tensor_tensor(oh[:rw], oh[:rw], sif[:rw, :, half, :].unsqueeze(2).broadcast_to([rw, 8, 16, 16]), ALU.mult),
                  [pf + "oh", pf + "sif"], [pf + "oh"])
                V(lambda e, nn=nn: e.tensor_reduce(nn[:rw], oh[:rw], AX.X, ALU.add), [pf + "oh"], [pf + kk])
            V(lambda e: e.scalar_tensor_tensor(out=n0[:rw], in0=n0[:rw], scalar=128.0, in1=n1[:rw], op0=ALU.mult, op1=ALU.add), [pf + "n0", pf + "n1"], [pf + "n0"])
            V(lambda e: e.tensor_scalar_add(n0[:rw], n0[:rw], float(li * 16384)), [pf + "n0"], [pf + "n0"])
            V(lambda e: e.tensor_copy(idx[s2][:rw, :].rearrange("p (h n) -> p h n", h=8), n0[:rw]), [pf + "n0"], [ik])
            g3 = gw[s2][:rw, :].rearrange("p (h n) -> p h n", h=8)
            V(lambda e: e.tensor_tensor(g3, cs[:rw], cs[:rw, :, 0:1].broadcast_to([rw, 8, 16]), ALU.subtract), csk, [gk])
            A(lambda e: e.activation(out=g3, in_=g3, func=AF.Exp), [gk], [gk])
            V(lambda e: e.tensor_reduce(zz[:rw, 0:8], g3, AX.X, ALU.add), [gk], [pf + "zz"])
            V(lambda e: e.reciprocal(zz[:rw, 0:8], zz[:rw, 0:8]), [pf + "zz"], [pf + "zz"])
            V(lambda e: e.tensor_tensor(g3, g3, zz[:rw, 0:8].unsqueeze(2).broadcast_to([rw, 8, 16]), ALU.mult), [gk, pf + "zz"], [gk])

        gctr = [0]

        def gather(tab, ti, hk):
            r0, rw = TILES[ti]
            s2 = ti % 2
            sl = gctr[0] % NG
            gctr[0] += 1
            buf, bk = gb[sl], f"{pf}gb{sl}"
            self.S.dma("pool", lambda e, buf=buf: e.indirect_dma_start(
                out=buf[:rw, :], out_offset=None, in_=tab,
                in_offset=bass.IndirectOffsetOnAxis(ap=idx[s2][:rw, hk:hk + 1], axis=0)), [f"{pf}idx{s2}"], [bk])
            return buf, bk

        def udots(ti):
            r0, rw = TILES[ti]
            s2 = ti % 2
            for hk in range(128):
                buf, bk = gather(self.peer_u, ti, hk)
                V(lambda e, buf=buf, hk=hk: e.scalar_tensor_tensor(out=junk[:rw, :], in0=buf[:rw, :], scalar=1.0, in1=xt[s2][:rw, :],
                                                                   op0=ALU.mult, op1=ALU.mult, accum_out=av[:rw, hk:hk + 1]),
                  [bk, f"{pf}xt{s2}"], [f"{pf}junk{hk % 4}", f"{pf}av{hk % 8}"])
            avk = [f"{pf}av{i}" for i in range(8)]
            self.gelu_tanh(wv[:rw, :], av[:rw, :], gtmp[:rw, :], rw, avk, pf + "wv", pf + "gtmp")
            V(lambda e: e.tensor_tensor(wv[:rw, :], wv[:rw, :], gw[s2][:rw, :], ALU.mult), [pf + "wv", f"{pf}gw{s2}"], [pf + "wv"])

        def vacc(ti):
            r0, rw = TILES[ti]
            s2 = ti % 2
            for hk in range(128):
                buf, bk = gather(self.peer_v, ti, hk)
                a_, ak = acc[hk % 2], f"{pf}acc{hk % 2}"
                if hk < 2:
                    V(lambda e, buf=buf, hk=hk, a_=a_: e.tensor_scalar(a_[:rw, :], buf[:rw, :], wv[:rw, hk:hk + 1], None, ALU.mult), [bk, pf + "wv"], [ak])
                else:
                    V(lambda e, buf=buf, hk=hk, a_=a_: e.scalar_tensor_tensor(out=a_[:rw, :], in0=buf[:rw, :], scalar=wv[:rw, hk:hk + 1], in1=a_[:rw, :],
                                                                               op0=ALU.mult, op1=ALU.add), [bk, pf + "wv", ak], [ak])
            a0, a1 = acc
            V(lambda e: e.tensor_tensor(a0[:rw, :], a0[:rw, :], a1[:rw, :], ALU.add), [pf + "acc0", pf + "acc1"], [pf + "acc0"])
            V(lambda e: e.scalar_tensor_tensor(out=a0[:rw, :], in0=xt[s2][:rw, :], scalar=ALPHA, in1=a0[:rw, :], op0=ALU.mult, op1=ALU.add),
              [pf + "acc0", f"{pf}xt{s2}"], [pf + "acc0"])
            self.layer_norm_rows(a0, rw, pf + "g", pf + "b", gt, bt, a1, [pf + "acc0"], pf + "acc1", scr)
            self.stq(dst[r0:r0 + rw, :], a1[:rw, :], [pf + "acc1"], [self.dname(dst)])

        topk(0)
        for ti in range(len(TILES)):
            udots(ti)
            if ti + 1 < len(TILES):
                topk(ti + 1)
            vacc(ti)

    def phase_gmlp(self, src, dst):
        ar, sc, ps, psk = self.ar, self.sc, self.ps, self.psk
        V, A, T = self.V, self.A, self.T
        pf = "E_"
        ar.reset()
        eps_t = ar.f32(1)
        self.eps_t = eps_t
        self.G(lambda e: e.memset(eps_t, LN_EPS), [], ["eps"])
        um_off = ar.off
        umT = ar.bf16(48 * NT).rearrange("p (c n) -> p c n", c=48)
        ar2 = Arena(ar.ap[:, um_off:ar.off])
        mark0 = ar.off
        xT = ar.bf16(16 * NT).rearrange("p (c n) -> p c n", c=16)
        stats = ar.f32(9 * 72).rearrange("p (t s) -> p t s", t=9)
        mark1 = ar.off
        stg = [(ar2.f32(D), f"{pf}xs{i}") for i in range(2)]
        self.load_xT(src, [(r0, rw, r0) for (r0, rw) in TILES], xT, pf + "xT", None, stg)
        wts = [(ar2.bf16(16 * 512).rearrange("p (c n) -> p c n", c=16), f"{pf}w{i}") for i in range(2)]
        bts = [(ar2.f32(512), f"{pf}bi{i}") for i in range(2)]
        ost = [(ar2.f32(512), f"{pf}o{i}") for i in range(4)]
        n = 0
        for g in range(12):
            c0 = 6144 + g * 512
            wt, wk = wts[g % 2]
            bt_, bk_ = bts[g % 2]
            self.ldc(wt, self.w_in_b[:, c0:c0 + 512].rearrange("(c p) n -> p c n", p=128), w=[wk])
            self.ld(bt_, self.b_in_b[0:1, c0:c0 + 512].partition_broadcast(128).rearrange("p o n -> p (o n)"), w=[bk_])
            for ti, (r0, rw) in enumerate(TILES):
                b = 2 + n % 6
                ob, ok = ost[n % 4]
                n += 1
                for c in range(16):
                    T(lambda e, b=b, c=c, r0=r0, rw=rw, wt=wt: e.matmul(ps[b][:rw, :], lhsT=xT[:, c, r0:r0 + rw], rhs=wt[:, c, :], start=(c == 0), stop=(c == 15)),
                      [wk, pf + "xT"], [psk[b]])
                V(lambda e, b=b, rw=rw, ob=ob, bt_=bt_: e.tensor_tensor(ob[:rw, :], ps[b][:rw, :], bt_[:rw, :], ALU.add), [psk[b], bk_], [ok])
                A(lambda e, rw=rw, ob=ob: e.activation(out=ob[:rw, :], in_=ob[:rw, :], func=AF.Gelu_apprx_tanh), [ok], [ok])
                V(lambda e, rw=rw, ob=ob, ti=ti, g=g: e.bn_stats(stats[:rw, ti, g * 6:(g + 1) * 6], ob[:rw, :]), [ok], [pf + "stats"])
                self.stq(sc["vraw"][r0:r0 + rw, g * 512:(g + 1) * 512], ob[:rw, :], [ok], ["s_vraw"])
        self.S.barrier()
        ar.off = mark1
        ar2.reset()
        vts = [(ar2.f32(6144), pf + "vt0"), (ar.f32(6144), pf + "vt1")]
        lg, lb = ar2.f32(6144), ar2.f32(6144)
        vnb = ar2.bf16(6144)
        mxs = [(ar.f32(768), f"{pf}mx{i}") for i in range(6)]
        wsf = ar.f32(1024).rearrange("p (g t) -> p g t", g=8)
        wsm = ar.bf16(1024).rearrange("p (g t) -> p g t", g=8)
        wsf_s = ar.f32(512).rearrange("p (g t) -> p g t", g=8)
        wsm_s = ar.bf16(512).rearrange("p (g t) -> p g t", g=8)
        bst, bss = ar.f32(8), ar.f32(8)
        self.ld(lg, self.lnv_g.partition_broadcast(128).rearrange("p o n -> p (o n)"), w=[pf + "lg"])
        self.ld(lb, self.lnv_b.partition_broadcast(128).rearrange("p o n -> p (o n)"), w=[pf + "lb"])
        self.ld(wsf, self.wsT, w=[pf + "wsf"])
        self.ld(wsf_s[:64], self.wsT_s, w=[pf + "wsfs"])
        self.ld(bst, self.b_s_t, w=[pf + "bst"])
        self.ld(bss[:64], self.b_s_s, w=[pf + "bss"])
        V(lambda e: e.tensor_tensor(wsm, wsf, self.tri128.unsqueeze(1).broadcast_to([128, 8, 128]), ALU.mult), [pf + "wsf"] + self.CK, [pf + "wsm"])
        V(lambda e: e.tensor_tensor(wsm_s[:64], wsf_s[:64], self.tri[:64, :].unsqueeze(1).broadcast_to([64, 8, 64]), ALU.mult), [pf + "wsfs"] + self.CK, [pf + "wsms"])
        n = 0
        vnb2 = [(vnb, pf + "vnb0"), (ar2.bf16(6144), pf + "vnb1")]
        sm2 = [(ar.f32(2), ar.f32(1), ar.f32(1), f"{pf}sm{i}") for i in range(2)]

        def e3_stage1(ti):
            r0, rw = TILES[ti]
            vt, vtk = vts[ti % 2]
            vb, vbk = vnb2[ti % 2]
            mv, rs, nmr, smk = sm2[ti % 2]
            self.ld(vt[:rw, :], sc["vraw"][r0:r0 + rw, :], w=[vtk] + [f"{vtk}_{c}" for c in range(4)])
            V(lambda e: e.bn_aggr(mv[:rw, :], stats[:rw, ti, :]), [pf + "stats"], [smk])
            A(lambda e: e.activation(out=rs[:rw, :], in_=mv[:rw, 1:2], func=AF.Sqrt, bias=eps_t[:rw, :], scale=1.0), [smk, "eps"], [smk])
            V(lambda e: e.reciprocal(rs[:rw, :], rs[:rw, :]), [smk], [smk])
            V(lambda e: e.scalar_tensor_tensor(out=nmr[:rw, :], in0=mv[:rw, 0:1], scalar=-1.0, in1=rs[:rw, :], op0=ALU.mult, op1=ALU.mult), [smk], [smk])
            for cb4 in range(4):
                cs_ = slice(cb4 * 1536, (cb4 + 1) * 1536)
                kq = f"{vtk}_{cb4}"
                A(lambda e, cs_=cs_: e.activation(out=vt[:rw, cs_], in_=vt[:rw, cs_], func=AF.Identity, bias=nmr[:rw, :], scale=rs[:rw, 0:1]), [vtk, smk], [kq])
            for cb4 in range(4):
                cs_ = slice(cb4 * 1536, (cb4 + 1) * 1536)
                kq = f"{vtk}_{cb4}"
                V(lambda e, cs_=cs_: e.tensor_tensor(vt[:rw, cs_], vt[:rw, cs_], lg[:rw, cs_], ALU.mult), [kq, pf + "lg"], [kq])
                V(lambda e, cs_=cs_: e.tensor_tensor(vt[:rw, cs_], vt[:rw, cs_], lb[:rw, cs_], ALU.add), [kq, pf + "lb"], [kq])
                A(lambda e, cs_=cs_: e.copy(vb[:rw, cs_], vt[:rw, cs_]), [kq], [f"{vbk}_{cb4}"])
            if rw == 64:
                self.stq(self.outs["vs"], vt[:64, :], [f"{vtk}_{c}" for c in range(4)], ["vs"], q="pool")

        def e3_stage2(ti):
            nonlocal n
            r0, rw = TILES[ti]
            samp = rw == 64
            vb, vbk = vnb2[ti % 2]
            wm = wsm_s if samp else wsm
            wmk = pf + ("wsms" if samp else "wsm")
            bs_ = bss if samp else bst
            bsk = pf + ("bss" if samp else "bst")
            for g8 in range(8):
                mx, mk_ = mxs[(ti * 8 + g8) % 6]
                for (c0, cn) in ((0, 512), (512, 256)):
                    b = 2 + n % 6
                    n += 1
                    T(lambda e, b=b, g8=g8, c0=c0, cn=cn: e.matmul(ps[b][:rw, :cn], lhsT=wm[:rw, g8, :rw], rhs=vb[:rw, g8 * 768 + c0:g8 * 768 + c0 + cn],
                                                                   start=True, stop=True), [wmk, f"{vbk}_{g8 // 2}"], [psk[b]])
                    if c0 == 0:
                        V(lambda e, b=b, g8=g8, c0=c0, cn=cn, mx=mx: e.tensor_scalar(mx[:rw, c0:c0 + cn], ps[b][:rw, :cn], bs_[:rw, g8:g8 + 1], None, ALU.add),
                          [psk[b], bsk], [mk_])
                    else:
                        A(lambda e, b=b, g8=g8, c0=c0, cn=cn, mx=mx: e.activation(out=mx[:rw, c0:c0 + cn], in_=ps[b][:rw, :cn], func=AF.Identity, bias=bs_[:rw, g8:g8 + 1], scale=1.0),
                          [psk[b], bsk], [mk_])
                self.stq(sc["mixed"][r0:r0 + rw, g8 * 768:(g8 + 1) * 768], mx[:rw, :], [mk_], ["s_mixed"], q="pool")

        e3_stage1(0)
        for ti in range(len(TILES)):
            if ti + 1 < len(TILES):
                e3_stage1(ti + 1)
            e3_stage2(ti)
        self.S.barrier()
        ar.off = mark1
        wt1s = [(ar.bf16(16 * 512).rearrange("p (c n) -> p c n", c=16), f"{pf}w1_{i}") for i in range(2)]
        bts = [(ar.f32(512), f"{pf}bj{i}") for i in range(2)]
        ust = [(ar.f32(512), f"{pf}u{i}") for i in range(3)]
        mst = [(ar.f32(512), f"{pf}m{i}") for i in range(3)]
        umb = [(ar.bf16(512), f"{pf}um{i}") for i in range(3)]
        n = 0
        pend = []
        for g in range(12):
            c0 = g * 512
            bt_, bk_ = bts[g % 2]
            wt1, w1k = wt1s[g % 2]
            self.ldc(wt1, self.w_in_b[:, c0:c0 + 512].rearrange("(c p) n -> p c n", p=128), w=[w1k])
            self.ld(bt_, self.b_in_b[0:1, c0:c0 + 512].partition_broadcast(128).rearrange("p o n -> p (o n)"), w=[bk_])
            for ti, (r0, rw) in enumerate(TILES):
                b = 2 + n % 6
                ub, uk = ust[n % 3]
                mb_, mk_ = mst[n % 3]
                qb, qk = umb[n % 3]
                n += 1
                self.ld(mb_[:rw, :], sc["mixed"][r0:r0 + rw, c0:c0 + 512], w=[mk_])
                for c in range(16):
                    T(lambda e, b=b, c=c, r0=r0, rw=rw, wt1=wt1: e.matmul(ps[b][:rw, :], lhsT=xT[:, c, r0:r0 + rw], rhs=wt1[:, c, :], start=(c == 0), stop=(c == 15)),
                      [w1k, pf + "xT"], [psk[b]])
                V(lambda e, b=b, rw=rw, ub=ub, bt_=bt_: e.tensor_tensor(ub[:rw, :], ps[b][:rw, :], bt_[:rw, :], ALU.add), [psk[b], bk_], [uk])
                A(lambda e, rw=rw, ub=ub: e.activation(out=ub[:rw, :], in_=ub[:rw, :], func=AF.Gelu_apprx_tanh), [uk], [uk])
                V(lambda e, rw=rw, ub=ub, mb_=mb_, qb=qb: e.tensor_tensor(qb[:rw, :], ub[:rw, :], mb_[:rw, :], ALU.mult), [uk, mk_], [qk])
                def tr_(tb=n % 2, rw=rw, qb=qb, qk=qk, g=g, r0=r0):
                    pt = ps[tb][:, 0:256].bitcast(BF16)
                    for j in range(4):
                        T(lambda e, pt=pt, j=j: e.transpose(pt[:, j * 128:j * 128 + rw], qb[:rw, j * 128:(j + 1) * 128], self.identb[:rw, :rw]),
                          [qk] + self.CK, [psk[tb]])
                    srcv = pt.rearrange("p (a b) -> p a b", a=4)[:, :, :rw]
                    dstv = umT[:, g * 4:(g + 1) * 4, r0:r0 + rw]
                    A(lambda e, d=dstv, s=srcv: e.copy(d, s), [psk[tb]], [pf + "umT"])
                pend.append(tr_)
                if len(pend) > 1:
                    pend.pop(0)()
        while pend:
            pend.pop(0)()
        self.S.barrier()
        ar.off = mark0
        wo = [(ar.bf16(48 * 256).rearrange("p (c n) -> p c n", c=48), f"{pf}wo{i}") for i in range(2)]
        yst = [(ar.f32(256), f"{pf}y{i}") for i in range(4)]
        n = 0
        for cg in range(8):
            wt, wk = wo[cg % 2]
            for k3 in range(3):
                self.ldc(wt[:, k3 * 16:(k3 + 1) * 16, :], self.w_out_b[k3 * 2048:(k3 + 1) * 2048, cg * 256:(cg + 1) * 256].rearrange("(c p) n -> p c n", p=128), w=[wk])
            for ti, (r0, rw) in enumerate(TILES):
                b = 2 + n % 6
                yb, yk = yst[n % 4]
                n += 1
                for c in range(48):
                    T(lambda e, b=b, c=c, r0=r0, rw=rw, wt=wt: e.matmul(ps[b][:rw, 0:256], lhsT=umT[:, c, r0:r0 + rw], rhs=wt[:, c, :], start=(c == 0), stop=(c == 47)),
                      [wk, pf + "umT"], [psk[b]])
                if n % 2:
                    V(lambda e, b=b, rw=rw, yb=yb: e.tensor_copy(yb[:rw, :], ps[b][:rw, 0:256]), [psk[b]], [yk])
                else:
                    A(lambda e, b=b, rw=rw, yb=yb: e.copy(yb[:rw, :], ps[b][:rw, 0:256]), [psk[b]], [yk])
                self.stq(sc["ymix"][r0:r0 + rw, cg * 256:(cg + 1) * 256], yb[:rw, :], [yk], ["s_ymix"])
        self.S.barrier()
        ar.reset()
        eps_t = ar.f32(1)
        self.eps_t = eps_t
        self.G(lambda e: e.memset(eps_t, LN_EPS), [], ["eps"])
        gt, bt = ar.f32(D), ar.f32(D)
        self.ld(gt, self.ln["ln_mix_g"][1:2, :].partition_broadcast(128).rearrange("p o n -> p (o n)"), w=[pf + "g"])
        self.ld(bt, self.ln["ln_mix_b"][1:2, :].partition_broadcast(128).rearrange("p o n -> p (o n)"), w=[pf + "b"])
        xr = [(ar.f32(D), f"{pf}xr{i}") for i in range(2)]
        yr = [(ar.f32(D), f"{pf}yr{i}") for i in range(2)]
        scr = (ar.f32(24), ar.f32(2), ar.f32(1))
        for ti, (r0, rw) in enumerate(TILES):
            xb, xk = xr[ti % 2]
            yb, yk = yr[ti % 2]
            self.ld(xb[:rw, :], src[r0:r0 + rw, :], w=[xk])
            self.ld(yb[:rw, :], sc["ymix"][r0:r0 + rw, :], w=[yk])
            V(lambda e, rw=rw, xb=xb, yb=yb: e.scalar_tensor_tensor(out=yb[:rw, :], in0=xb[:rw, :], scalar=ALPHA, in1=yb[:rw, :], op0=ALU.mult, op1=ALU.add), [xk, yk], [yk])
            self.layer_norm_rows(yb, rw, pf + "g", pf + "b", gt, bt, yb, [yk], yk, scr)
            self.stq(dst[r0:r0 + rw, :], yb[:rw, :], [yk], [self.dname(dst)], q="pool")


    def phase_peer_dense(self, src, li, dst, pf):
        ar, ps, psk = self.ar, self.ps, self.psk
        V, A, T = self.V, self.A, self.T
        NTP = 1152
        G_scr = self.sc["G"]
        ar.reset()
        eps_t = ar.f32(1)
        self.eps_t = eps_t
        self.G(lambda e: e.memset(eps_t, LN_EPS), [], ["eps"])
        xT = ar.bf16(16 * NTP).rearrange("p (c n) -> p c n", c=16)
        xTk = pf + "xT"
        V(lambda e: e.memset(xT[:, :, NT:NTP], 0.0), [], [xTk])
        markP = ar.off
        qT = ar.bf16(16 * NT).rearrange("p (c n) -> p c n", c=16)
        skt = ar.bf16(16 * 128).rearrange("p (c n) -> p c n", c=16)
        self.ldc(skt, self.skT[li], w=[pf + "sk"])
        mark = ar.off
        stg = [(ar.f32(D), f"{pf}xs{i}") for i in range(2)]
        wq = ar.bf16(16 * D).rearrange("p (c n) -> p c n", c=16)
        for q4 in range(4):
            self.ldc(wq[:, :, q4 * 512:(q4 + 1) * 512], self.peer_wq[li, :, q4 * 512:(q4 + 1) * 512].rearrange("(c p) n -> p c n", p=128), w=[f"{pf}wq{q4}"])
        self.load_xT(src, [(r0, rw, r0) for (r0, rw) in TILES], xT, xTk, None, stg)
        n = 0
        for cg in range(16):
            for (t0, tn) in ((0, 512), (512, 512), (1024, 64)):
                b = 2 + n % 6
                n += 1
                for c in range(16):
                    T(lambda e, b=b, c=c, cg=cg, t0=t0, tn=tn: e.matmul(ps[b][:, :tn], lhsT=wq[:, c, cg * 128:(cg + 1) * 128], rhs=xT[:, c, t0:t0 + tn],
                                                                       start=(c == 0), stop=(c == 15)), [f"{pf}wq{cg // 4}", xTk], [psk[b]])
                if n % 2:
                    V(lambda e, b=b, cg=cg, t0=t0, tn=tn: e.tensor_copy(qT[:, cg, t0:t0 + tn], ps[b][:, :tn]), [psk[b]], [pf + "qT"])
                else:
                    A(lambda e, b=b, cg=cg, t0=t0, tn=tn: e.copy(qT[:, cg, t0:t0 + tn], ps[b][:, :tn]), [psk[b]], [pf + "qT"])
        self.S.barrier()
        ar.off = mark
        Sc = ar.f32(2048).rearrange("p (c n) -> p c n", c=16)
        Wk = ar.f32(2048).rearrange("p (c n) -> p c n", c=16)
        sv = ar.f32(256).rearrange("p (c n) -> p c n", c=16)
        si = ar.u32(256).rearrange("p (c n) -> p c n", c=16)
        sif = ar.f32(256).rearrange("p (h t n) -> p h t n", h=8, t=2)
        cand = ar.f32(2048).rearrange("p (h n) -> p h n", h=8)
        cw = ar.f32(2048).rearrange("p (h n) -> p h n", h=8)
        cs = ar.f32(128).rearrange("p (h n) -> p h n", h=8)
        cp = ar.u32(128).rearrange("p (h n) -> p h n", h=8)
        cif = ar.f32(128).rearrange("p (h n) -> p h n", h=8)
        cjf = ar.f32(128).rearrange("p (h n) -> p h n", h=8)
        oh = ar.f32(2048).rearrange("p (h k n) -> p h k n", h=8, k=16)
        n0 = ar.f32(128).rearrange("p (h n) -> p h n", h=8)
        n1 = ar.f32(128).rearrange("p (h n) -> p h n", h=8)
        cpf = ar.f32(128).rearrange("p (h n) -> p h n", h=8)
        gwt = ar.f32(128)
        zz = ar.f32(16)
        itT = ar.f32(384).rearrange("p (a n) -> p a n", a=3)
        itB = ar.bf16(384).rearrange("p (a n) -> p a n", a=3)
        io128b = ar.bf16(128)
        V(lambda e: e.tensor_copy(io128b, self.iota128), self.CK, [pf + "io128b"])
        P0 = ar.bf16(64 * 128).rearrange("p (t n) -> p t n", t=64)
        P1 = ar.bf16(64 * 128).rearrange("p (t n) -> p t n", t=64)
        Gbuf = ar.bf16(128 * 128).rearrange("p (i t) -> p i t", i=128)
        sv4 = sv.rearrange("p (h t) n -> p h t n", t=2)
        V(lambda e: e.memset(Gbuf, 0.0), [], [pf + "Gbuf"])
        ectr = [0]

        def scores(ti):
            r0, rw = TILES[ti]
            for g4 in range(4):
                b = 2 + (ti * 4 + g4) % 6
                for j in range(4):
                    cg = g4 * 4 + j
                    T(lambda e, b=b, j=j, cg=cg, r0=r0, rw=rw: e.matmul(ps[b][:rw, j * 128:(j + 1) * 128], lhsT=qT[:, cg, r0:r0 + rw], rhs=skt[:, cg, :], start=True, stop=True),
                      [pf + "qT", pf + "sk"], [psk[b]])
                dstv = Sc[:rw, g4 * 4:(g4 + 1) * 4, :]
                srcv = ps[b][:rw, :].rearrange("p (a n) -> p a n", a=4)
                wkeys = [f"{pf}S{g4 * 4 + j}" for j in range(4)]
                A(lambda e, d=dstv, s=srcv: e.copy(d, s), [psk[b]], wkeys)

        scores(0)
        for ti, (r0, rw) in enumerate(TILES):
            for cg in range(16):
                V(lambda e, cg=cg, rw=rw: e.max(out=sv[:rw, cg, 0:8], in_=Sc[:rw, cg, :]), [f"{pf}S{cg}"], [f"{pf}sv{cg}"])
            for cg in range(16):
                V(lambda e, cg=cg, rw=rw: e.max_index(out=si[:rw, cg, 0:8], in_max=sv[:rw, cg, 0:8], in_values=Sc[:rw, cg, :]), [f"{pf}S{cg}", f"{pf}sv{cg}"], [f"{pf}si{cg}"])
            for cg in range(16):
                V(lambda e, cg=cg, rw=rw: e.match_replace(out=Wk[:rw, cg, :], in_to_replace=sv[:rw, cg, 0:8], in_values=Sc[:rw, cg, :], imm_value=NEG),
                  [f"{pf}S{cg}", f"{pf}sv{cg}"], [f"{pf}W{cg}"])
            for cg in range(16):
                V(lambda e, cg=cg, rw=rw: e.max(out=sv[:rw, cg, 8:16], in_=Wk[:rw, cg, :]), [f"{pf}W{cg}"], [f"{pf}sv{cg}"])
            for cg in range(16):
                V(lambda e, cg=cg, rw=rw: e.max_index(out=si[:rw, cg, 8:16], in_max=sv[:rw, cg, 8:16], in_values=Wk[:rw, cg, :]), [f"{pf}W{cg}", f"{pf}sv{cg}"], [f"{pf}si{cg}"])
            svk = [f"{pf}sv{cg}" for cg in range(16)]
            sik = [f"{pf}si{cg}" for cg in range(16)]
            V(lambda e, rw=rw: e.tensor_copy(sif[:rw].rearrange("p h t n -> p (h t) n"), si[:rw]), sik, [pf + "sif"])
            V(lambda e, rw=rw: e.tensor_tensor(cand[:rw].rearrange("p h (i j) -> p h i j", i=16),
                                               sv4[:rw, :, 0, :].unsqueeze(3).broadcast_to([rw, 8, 16, 16]),
                                               sv4[:rw, :, 1, :].unsqueeze(2).broadcast_to([rw, 8, 16, 16]), ALU.add), svk, [pf + "cand"])
            ck = [f"{pf}c{h}" for h in range(8)]
            for h in range(8):
                V(lambda e, h=h, rw=rw: e.max(out=cs[:rw, h, 0:8], in_=cand[:rw, h, :]), [pf + "cand"], [ck[h] + "s"])
            for h in range(8):
                V(lambda e, h=h, rw=rw: e.max_index(out=cp[:rw, h, 0:8], in_max=cs[:rw, h, 0:8], in_values=cand[:rw, h, :]), [pf + "cand", ck[h] + "s"], [ck[h] + "p"])
            for h in range(8):
                V(lambda e, h=h, rw=rw: e.match_replace(out=cw[:rw, h, :], in_to_replace=cs[:rw, h, 0:8], in_values=cand[:rw, h, :], imm_value=NEG),
                  [pf + "cand", ck[h] + "s"], [ck[h] + "w"])
            for h in range(8):
                V(lambda e, h=h, rw=rw: e.max(out=cs[:rw, h, 8:16], in_=cw[:rw, h, :]), [ck[h] + "w"], [ck[h] + "s"])
            for h in range(8):
                V(lambda e, h=h, rw=rw: e.max_index(out=cp[:rw, h, 8:16], in_max=cs[:rw, h, 8:16], in_values=cw[:rw, h, :]), [ck[h] + "w", ck[h] + "s"], [ck[h] + "p"])
            csk = [c + "s" for c in ck]
            cpk = [c + "p" for c in ck]
            V(lambda e, rw=rw: e.tensor_copy(cpf[:rw], cp[:rw]), cpk, [pf + "cpf"])
            th4 = self.thr16[:rw, :].unsqueeze(1).unsqueeze(1).broadcast_to([rw, 8, 16, 16])
            V(lambda e, rw=rw, th4=th4: e.tensor_tensor(oh[:rw], cpf[:rw].unsqueeze(3).broadcast_to([rw, 8, 16, 16]), th4, ALU.is_ge), [pf + "cpf"] + self.CK, [pf + "oh"])
            V(lambda e, rw=rw: e.tensor_reduce(cif[:rw], oh[:rw], AX.X, ALU.add), [pf + "oh"], [pf + "cif"])
            V(lambda e, rw=rw: e.scalar_tensor_tensor(out=cjf[:rw], in0=cif[:rw], scalar=-16.0, in1=cpf[:rw], op0=ALU.mult, op1=ALU.add), [pf + "cif", pf + "cpf"], [pf + "cjf"])
            io4 = self.iota16[:rw, :].unsqueeze(1).unsqueeze(1).broadcast_to([rw, 8, 16, 16])
            for (cf, half, nn, kk) in ((cif, 0, n0, "n0"), (cjf, 1, n1, "n1")):
                V(lambda e, cf=cf, rw=rw, io4=io4: e.tensor_tensor(oh[:rw], cf[:rw].unsqueeze(3).broadcast_to([rw, 8, 16, 16]), io4, ALU.is_equal),
                  [pf + "cif", pf + "cjf"] + self.CK, [pf + "oh"])
                V(lambda e, half=half, rw=rw: e.tensor_tensor(oh[:rw], oh[:rw], sif[:rw, :, half, :].unsqueeze(2).broadcast_to([rw, 8, 16, 16]), ALU.mult),
                  [pf + "oh", pf + "sif"], [pf + "oh"])
                V(lambda e, nn=nn, rw=rw: e.tensor_reduce(nn[:rw], oh[:rw], AX.X, ALU.add), [pf + "oh"], [pf + kk])
            g3 = gwt[:rw, :].rearrange("p (h n) -> p h n", h=8)
            gk = pf + "gw"
            V(lambda e, rw=rw, g3=g3: e.tensor_tensor(g3, cs[:rw], cs[:rw, :, 0:1].broadcast_to([rw, 8, 16]), ALU.subtract), csk, [gk])
            A(lambda e, g3=g3: e.activation(out=g3, in_=g3, func=AF.Exp), [gk], [gk])
            V(lambda e, rw=rw, g3=g3: e.tensor_reduce(zz[:rw, 0:8], g3, AX.X, ALU.add), [gk], [pf + "zz"])
            V(lambda e, rw=rw: e.reciprocal(zz[:rw, 0:8], zz[:rw, 0:8]), [pf + "zz"], [pf + "zz"])
            V(lambda e, rw=rw, g3=g3: e.tensor_tensor(g3, g3, zz[:rw, 0:8].unsqueeze(2).broadcast_to([rw, 8, 16]), ALU.mult), [gk, pf + "zz"], [gk])
            if ti + 1 < len(TILES):
                scores(ti + 1)
            for a_, (srcap, kk) in enumerate(((n0, pf + "n0"), (n1, pf + "n1"), (None, gk))):
                sap = gwt[:rw, :] if srcap is None else srcap[:rw].rearrange("p h n -> p (h n)")
                T(lambda e, a_=a_, sap=sap, rw=rw: e.transpose(ps[0][:, a_ * 128:a_ * 128 + rw], sap, self.ident[:rw, :rw]), [kk] + self.CK, [f"{pf}psT{a_}"])
            V(lambda e, rw=rw: e.tensor_copy(itT[:, :, :rw], ps[0][:, 0:384].rearrange("p (a n) -> p a n", a=3)[:, :, :rw]),
              [f"{pf}psT{a_}" for a_ in range(3)], [pf + "itT"])
            for qt in range(rw // 32):
                pb_ = (ti * 4 + qt) % 2
                P0q, P1q = P0[:, pb_ * 32:(pb_ + 1) * 32, :], P1[:, pb_ * 32:(pb_ + 1) * 32, :]
                k0, k1 = f"{pf}P0_{pb_}", f"{pf}P1_{pb_}"
                for tt in range(32):
                    tg = qt * 32 + tt
                    V(lambda e, P0q=P0q, tt=tt, tg=tg: e.tensor_scalar(P0q[:, tt, :], io128b, itT[:, 0, tg:tg + 1], itT[:, 2, tg:tg + 1], ALU.is_equal, ALU.mult),
                      [pf + "itT", pf + "io128b"], [k0])
                    V(lambda e, P1q=P1q, tt=tt, tg=tg: e.tensor_scalar(P1q[:, tt, :], io128b, itT[:, 1, tg:tg + 1], None, ALU.is_equal),
                      [pf + "itT", pf + "io128b"], [k1])
                for t4 in range(8):
                    b = 2 + ectr[0] % 6
                    ectr[0] += 1
                    for k in range(4):
                        tt = t4 * 4 + k
                        T(lambda e, b=b, k=k, tt=tt, P0q=P0q, P1q=P1q: e.matmul(ps[b][:, k * 128:(k + 1) * 128], lhsT=P0q[:, tt, :], rhs=P1q[:, tt, :], start=True, stop=True),
                          [k0, k1], [psk[b]])
                    tg = qt * 32 + t4 * 4
                    A(lambda e, b=b, tg=tg: e.copy(Gbuf[:, :, tg:tg + 4], ps[b][:, :].rearrange("p (t i) -> p i t", t=4)), [psk[b]], [pf + "Gbuf"])
            self.stq(G_scr[ti].rearrange("p c t -> p (c t)"), Gbuf.rearrange("p i t -> p (i t)"), [pf + "Gbuf"], ["s_G"])
        self.S.barrier()
        ar.off = markP
        TB = ((0, 512), (512, 512), (1024, 128))
        acc = ar.f32(9 * D).rearrange("p (a n) -> p a n", a=9)
        UcTs = [(ar.bf16(16 * 128).rearrange("p (c n) -> p c n", c=16), f"{pf}U{i}") for i in range(3)]
        Vg = [(ar.bf16(4 * D).rearrange("p (j n) -> p j n", j=4), f"{pf}V{i}") for i in range(2)]
        Gg = [(ar.bf16(9 * 4 * 128).rearrange("p (a j t) -> p a j t", a=9, j=4), f"{pf}G{i}") for i in range(2)]
        Wg = [(ar.bf16(4 * NTP).rearrange("p (j n) -> p j n", j=4), f"{pf}W{i}") for i in range(2)]
        tmps = [(ar.bf16(512), f"{pf}tmp{i}") for i in range(3)]
        uT = self.peer_u
        v4 = self.peer_v.rearrange("(l i c) d -> l i c d", l=2, c=128)
        vctr = [0]
        actr = [0]

        def vside(grp):
            vg, vgk = Vg[grp % 2]
            wg, wgk = Wg[grp % 2]
            for ti in range(9):
                rw = TILES[ti][1]
                for db in range(4):
                    b = 5 + vctr[0] % 3
                    vctr[0] += 1
                    for j in range(4):
                        T(lambda e, b=b, j=j, ti=ti, db=db, wg=wg, vg=vg: e.matmul(ps[b][:, :], lhsT=wg[:, j, ti * 128:(ti + 1) * 128], rhs=vg[:, j, db * 512:(db + 1) * 512],
                                                                                   start=(j == 0), stop=(j == 3)), [wgk, vgk], [psk[b]])
                    ak = f"{pf}acc{ti}_{db}"
                    if grp == 0:
                        V(lambda e, b=b, ti=ti, db=db, rw=rw: e.tensor_copy(acc[:rw, ti, db * 512:(db + 1) * 512], ps[b][:rw, :]), [psk[b]], [ak])
                    else:
                        V(lambda e, b=b, ti=ti, db=db, rw=rw: e.tensor_tensor(acc[:rw, ti, db * 512:(db + 1) * 512], acc[:rw, ti, db * 512:(db + 1) * 512], ps[b][:rw, :], ALU.add),
                          [psk[b], ak], [ak])

        for grp in range(32):
            gg, ggk = Gg[grp % 2]
            vg, vgk = Vg[grp % 2]
            wg, wgk = Wg[grp % 2]
            self.ld(gg, G_scr[:, :, grp * 4:(grp + 1) * 4, :].rearrange("a p c t -> p a c t"), w=[ggk])
            self.ldc(vg, v4[li, :, grp * 4:(grp + 1) * 4, :], w=[vgk])
            for j in range(4):
                c = grp * 4 + j
                ut, utk = UcTs[c % 3]
                row0 = (li * 128 + c) * 2048
                self.ldc(ut, uT[row0:row0 + 2048, :].rearrange("(dc p) i -> p dc i", p=128), w=[utk])
                for bi, (t0, tn) in enumerate(TB):
                    b = actr[0] % 5
                    tmp, tmk = tmps[actr[0] % 3]
                    actr[0] += 1
                    for dc in range(16):
                        T(lambda e, b=b, dc=dc, t0=t0, tn=tn, ut=ut: e.matmul(ps[b][:, :tn], lhsT=ut[:, dc, :], rhs=xT[:, dc, t0:t0 + tn], start=(dc == 0), stop=(dc == 15)),
                          [utk, xTk], [psk[b]])
                    A(lambda e, b=b, tn=tn, tmp=tmp: e.activation(out=tmp[:, :tn], in_=ps[b][:, :tn], func=AF.Gelu_apprx_tanh), [psk[b]], [tmk])
                    a0, a1 = t0 // 128, (t0 + tn) // 128
                    V(lambda e, wg=wg, gg=gg, j=j, t0=t0, tn=tn, a0=a0, a1=a1, tmp=tmp: e.tensor_tensor(
                        wg[:, j, t0:t0 + tn].rearrange("p (a t) -> p a t", t=128), tmp[:, :tn].rearrange("p (a t) -> p a t", t=128),
                        gg[:, a0:a1, j, :], ALU.mult), [tmk, ggk], [wgk])
                if j == 0 and grp > 0:
                    vside(grp - 1)
        vside(31)
        self.S.barrier()
        ar.off = ar.off - 0
        ar2 = Arena(Vg[0][0].rearrange("p j n -> p (j n)").bitcast(F32))
        gt, bt = ar2.f32(D), None
        ar3 = Arena(Vg[1][0].rearrange("p j n -> p (j n)").bitcast(F32))
        bt = ar3.f32(D)
        ar4 = Arena(Wg[0][0].rearrange("p j n -> p (j n)").bitcast(F32))
        xr = [(ar4.f32(D), f"{pf}xr0")]
        ar5 = Arena(Wg[1][0].rearrange("p j n -> p (j n)").bitcast(F32))
        xr.append((ar5.f32(D), f"{pf}xr1"))
        ar6 = Arena(Gg[0][0].rearrange("p a j t -> p (a j t)").bitcast(F32))
        scr = (ar6.f32(24), ar6.f32(2), ar6.f32(1))
        self.ld(gt, self.ln["ln_ffn_g"][li:li + 1, :].partition_broadcast(128).rearrange("p o n -> p (o n)"), w=[pf + "g"])
        self.ld(bt, self.ln["ln_ffn_b"][li:li + 1, :].partition_broadcast(128).rearrange("p o n -> p (o n)"), w=[pf + "b"])
        for ti, (r0, rw) in enumerate(TILES):
            xb, xk = xr[ti % 2]
            aks = [f"{pf}acc{ti}_{db}" for db in range(4)]
            self.ld(xb[:rw, :], src[r0:r0 + rw, :], w=[xk])
            V(lambda e, rw=rw, xb=xb, ti=ti: e.scalar_tensor_tensor(out=xb[:rw, :], in0=xb[:rw, :], scalar=ALPHA, in1=acc[:rw, ti, :], op0=ALU.mult, op1=ALU.add), [xk] + aks, [xk])
            self.layer_norm_rows(xb, rw, pf + "g", pf + "b", gt, bt, xb, [xk], xk, scr)
            self.stq(dst[r0:r0 + rw, :], xb[:rw, :], [xk], [self.dname(dst)], q="pool")


def make_in_maps(inp):
    f = np.float32
    xp, xs = inp["x_prompt"], inp["x_sample"]
    maps = []
    ws = inp["w_s_b"][0]
    wsT = np.ascontiguousarray(ws.transpose(2, 0, 1))
    wsT_s = np.zeros((64, 8, 64), f)
    for b in range(16):
        wsT_s[b * 4:(b + 1) * 4, :, b * 4:(b + 1) * 4] = ws[:, 0:4, 0:4].transpose(2, 0, 1)
    b_s = inp["b_s_b"][0]
    b_s_t = np.ascontiguousarray(b_s.T)
    b_s_s = np.ascontiguousarray(np.tile(b_s[:, 0:4].T, (16, 1)))
    skT = np.ascontiguousarray(inp["peer_sub_keys"].reshape(2, 16, 128, 128).transpose(0, 3, 1, 2))
    shared = dict(
        consts=CONST_PACK,
        w_in_a=inp["w_in_a"][0], b_gate=inp["b_gate_a"], hn_gain=inp["hn_gain_a"].reshape(1, D),
        w_out_a=inp["w_out_a"][0], ln_mix_g=inp["ln_mix_g"], ln_mix_b=inp["ln_mix_b"],
        ln_ffn_g=inp["ln_ffn_g"], ln_ffn_b=inp["ln_ffn_b"],
        w_in_b=inp["w_in_b"][0], b_in_b=inp["b_in_b"], lnv_g=inp["lnv_g_b"], lnv_b=inp["lnv_b_b"],
        wsT=wsT, wsT_s=wsT_s, b_s_t=b_s_t, b_s_s=b_s_s, w_out_b=inp["w_out_b"][0],
        peer_wq=inp["peer_w_q"], skT=skT,
        peer_u=np.ascontiguousarray(inp["peer_u"].reshape(2, 128, 128, D).transpose(0, 2, 3, 1)).reshape(2 * 128 * D, 128),
        peer_v=inp["peer_v"].reshape(2 * 16384, D),
    )
    for c in range(NCORES):
        b, half = c // 2, c % 2
        m = dict(shared)
        m["x_own"] = np.concatenate([xp[b, half * 1024:(half + 1) * 1024], xs[16 * c:16 * c + 16].reshape(64, D)], axis=0)
        m["x_prev"] = np.ascontiguousarray(xp[b, (1 - half) * 1024:(2 - half) * 1024])
        m["flag"] = np.full((128, 1), float(half), f)
        m["C0s"] = np.ascontiguousarray(inp["state_mlstm_C"][0, 16 * c:16 * c + 16].reshape(16 * 8 * 128, 256))
        m["n0sT"] = np.ascontiguousarray(inp["state_mlstm_n"][0, 16 * c:16 * c + 16].reshape(128, 128).T)
        m["m0sT"] = np.ascontiguousarray(inp["state_mlstm_m"][0, 16 * c:16 * c + 16].T)
        maps.append(m)
    return maps


def assemble(results):
    f = np.float32
    y_p = np.zeros((4, 2048, D), f)
    y_s = np.zeros((128, 4, D), f)
    C_p = np.zeros((1, 4, 8, 128, 256), f)
    n_p = np.zeros((1, 4, 8, 128), f)
    m_p = np.zeros((1, 4, 8), f)
    C_s = np.zeros((1, 128, 8, 128, 256), f)
    n_s = np.zeros((1, 128, 8, 128), f)
    m_s = np.zeros((1, 128, 8), f)
    v_s = np.zeros((1, 128, 4, 6144), f)
    for c in range(NCORES):
        r = results[c]
        b, half = c // 2, c % 2
        y_p[b, half * 1024:(half + 1) * 1024] = r["y_own"][:1024]
        y_s[16 * c:16 * c + 16] = r["y_own"][1024:].reshape(16, 4, D)
        if half == 1:
            C_p[0, b] = r["Cp"].reshape(8, 128, 256)
            n_p[0, b] = r["npT"].T
            m_p[0, b] = r["mpT"][:, 0]
        C_s[0, 16 * c:16 * c + 16] = r["Cs"].reshape(16, 8, 128, 256)
        n_s[0, 16 * c:16 * c + 16] = r["nsT"].T.reshape(16, 8, 128)
        m_s[0, 16 * c:16 * c + 16] = r["msT"].T
        v_s[0, 16 * c:16 * c + 16] = r["vs"].reshape(16, 4, 6144)
    return (y_p, y_s, C_p, n_p, m_p, C_s, n_s, m_s, v_s)


_NC_CACHE = {}


def kernel(**inputs):
    inp = {k: np.asarray(v) for k, v in inputs.items()}
    if "nc" not in _NC_CACHE:
        _NC_CACHE["nc"] = Builder(debug=False).build()
    nc = _NC_CACHE["nc"]
    maps = make_in_maps(inp)
    res = run_bass_kernel_spmd(nc, maps, core_ids=list(range(NCORES)))
    return assemble(res.results)
```

```python
import contextlib
import numpy as np
import concourse.bass as bass
import concourse.mybir as mybir
from concourse.bass_utils import run_bass_kernel_spmd

F32 = mybir.dt.float32
BF16 = mybir.dt.bfloat16
I32 = mybir.dt.int32
U32 = mybir.dt.uint32
AF = mybir.ActivationFunctionType
ALU = mybir.AluOpType
AX = mybir.AxisListType

NCORES = 8
D = 2048
NT = 1088
NPV = 1024
NALL = NT + NPV
ALPHA = float(4 ** 0.25)
LN_EPS = 1e-5
NEG = -1.0e30
TILES = [(i * 128, 128) for i in range(8)] + [(1024, 64)]
PTILES = [(NT + i * 128, 128) for i in range(8)]


class Op:
    __slots__ = ("eng", "fn", "reads", "writes", "is_dma", "deps", "signal", "sig", "semkey", "idx")

    def __init__(self, eng, fn, reads, writes, is_dma):
        self.eng = eng
        self.fn = fn
        self.reads = tuple(reads)
        self.writes = tuple(writes)
        self.is_dma = is_dma
        self.deps = []
        self.signal = False
        self.sig = None
        self.semkey = None


class Sched:
    ENGS = ("pe", "act", "dve", "pool", "sp")
    SEM_WRAP = 20000

    def __init__(self, nc):
        self.nc = nc
        self.ops = []
        self.allkeys = set()

    def op(self, eng, fn, reads=(), writes=()):
        o = Op(eng, fn, reads, writes, False)
        self.ops.append(o)
        self.allkeys.update(o.reads)
        self.allkeys.update(o.writes)
        return o

    def dma(self, eng, fn, reads=(), writes=(), semkey=None):
        o = Op(eng, fn, reads, writes, True)
        o.semkey = semkey if semkey is not None else o.writes[0]
        self.ops.append(o)
        self.allkeys.update(o.reads)
        self.allkeys.update(o.writes)
        return o

    def barrier(self):
        self.ops.append("BARRIER")
        self.op("sp", lambda e: e.nop(), reads=(), writes=tuple(self.allkeys) + ("__bar",))
        for e in ("pe", "act", "dve", "pool"):
            self.op(e, None, reads=("__bar",))

    def finalize(self):
        nc = self.nc
        last_w = {}
        readers = {}
        phase = 0
        ops2 = []
        self.dma_slot = {}
        slots_in_phase = {}
        for o in self.ops:
            if isinstance(o, str):
                phase += 1
                slots_in_phase = {}
                continue
            if o.is_dma:
                cls = "sw" if o.eng == "pool" else "hw"
                sk = (cls, o.semkey)
                if sk not in slots_in_phase:
                    slots_in_phase[sk] = (cls, sum(1 for k in slots_in_phase if k[0] == cls))
                self.dma_slot[id(o)] = slots_in_phase[sk]
            ops2.append(o)
        self.ops = ops2
        for i, o in enumerate(self.ops):
            o.idx = i
            deps = {}
            for k in o.reads:
                for p in last_w.get(k, ()):
                    deps[p.idx] = p
            for k in o.writes:
                grp = last_w.get(k, ())
                rd = readers.get(k)
                join = o.is_dma and grp and all(p.is_dma for p in grp) and not rd
                if not join:
                    for p in grp:
                        deps[p.idx] = p
                    for r in rd or ():
                        deps[r.idx] = r
            deps.pop(i, None)
            for p in deps.values():
                if p.is_dma:
                    o.deps.append(p)
                elif p.eng == "pe" and o.eng == "pe" and not o.is_dma:
                    continue
                else:
                    if p.fn is None:
                        raise RuntimeError("dependency on a wait-only op")
                    p.signal = True
                    o.deps.append(p)
            for k in o.reads:
                if o.fn is not None:
                    readers.setdefault(k, []).append(o)
            for k in o.writes:
                grp = last_w.get(k, ())
                rd = readers.get(k)
                if o.is_dma and grp and all(p.is_dma for p in grp) and not rd:
                    last_w[k] = [p for p in grp if p.semkey != o.semkey] + [o]
                else:
                    last_w[k] = [o]
                readers[k] = []
        self._stack = contextlib.ExitStack()
        sem_ctr = [0]

        def newsem(name):
            sem_ctr[0] += 1
            return self._stack.enter_context(nc.semaphore(f"{name}{sem_ctr[0]}"))

        eng_sem, eng_cnt, dma_sem, dma_cnt = {}, {}, {}, {}
        for o in self.ops:
            if o.is_dma:
                k = self.dma_slot[id(o)]
                if k not in dma_sem:
                    dma_sem[k] = newsem("d")
                    dma_cnt[k] = 0
                dma_cnt[k] += 16
                o.sig = (dma_sem[k], dma_cnt[k])
            elif o.signal:
                e = o.eng
                if e not in eng_sem or eng_cnt[e] >= self.SEM_WRAP:
                    eng_sem[e] = newsem("e")
                    eng_cnt[e] = 0
                eng_cnt[e] += 1
                o.sig = (eng_sem[e], eng_cnt[e])
        self.n_sems = sem_ctr[0]
        per = {e: [o for o in self.ops if o.eng == e] for e in self.ENGS}
        handles = {"pe": "tensor", "act": "scalar", "dve": "vector", "pool": "gpsimd", "sp": "sync"}

        def emit_stream(eng, h):
            waited = {}
            for o in per[eng]:
                need = {}
                for p in o.deps:
                    s, v = p.sig
                    key = id(s)
                    if key not in need or need[key][1] < v:
                        need[key] = (s, v)
                for key, (s, v) in need.items():
                    if waited.get(key, 0) >= v:
                        continue
                    h.wait_ge(s, v)
                    waited[key] = v
                if o.fn is not None:
                    ins = o.fn(h)
                    if o.sig is not None:
                        ins.then_inc(o.sig[0], 16 if o.is_dma else 1)

        with nc.Block() as block:
            for e in self.ENGS:
                if not per[e]:
                    continue

                def mk(e):
                    def f(h):
                        emit_stream(e, h)
                    return f
                getattr(block, handles[e])(mk(e))
        self._stack.close()


class Arena:
    def __init__(self, ap):
        self.ap = ap
        self.cap = ap.shape[1]
        self.off = 0

    def reset(self):
        self.off = 0

    def _take(self, nwords):
        assert self.off + nwords <= self.cap, f"arena overflow {self.off}+{nwords}>{self.cap}"
        a = self.ap[:, self.off:self.off + nwords]
        self.off += nwords
        return a

    def f32(self, n):
        return self._take(n)

    def i32(self, n):
        return self._take(n).bitcast(I32)

    def u32(self, n):
        return self._take(n).bitcast(U32)

    def bf16(self, n):
        return self._take((n + 1) // 2).bitcast(BF16)[:, :n]


def _consts():
    c = {}
    c["ident"] = np.eye(128, dtype=np.float32)
    s = np.arange(64)
    tri = (s[:, None] <= s[None, :]).astype(np.float32)
    same = (s[:, None] // 4 == s[None, :] // 4).astype(np.float32)
    t64 = np.zeros((128, 64), np.float32)
    t64[:64] = tri
    c["tri"] = t64
    tb = np.zeros((128, 64), np.float32)
    tb[:64] = tri * same
    c["trib"] = tb
    ss = np.zeros((128, 16), np.float32)
    ss[:64] = (s[:, None] // 4 == np.arange(16)[None, :]).astype(np.float32)
    c["seqsel"] = ss
    cm = np.zeros((128, 16, 64), np.float32)
    cm[:] = (np.arange(16)[:, None] == (s[None, :] // 4)).astype(np.float32)[None]
    c["colmask"] = cm.reshape(128, 1024)
    c["ones"] = np.ones((128, 128), np.float32)
    c["iota16"] = np.tile(np.arange(16, dtype=np.float32)[None, :], (128, 1))
    c["thr16"] = np.tile(16.0 * (np.arange(16, dtype=np.float32)[None, :] + 1.0), (128, 1))
    s128 = np.arange(128)
    c["tri128"] = (s128[:, None] <= s128[None, :]).astype(np.float32)
    c["iota128"] = np.tile(np.arange(128, dtype=np.float32)[None, :], (128, 1))
    offs, o = {}, 0
    for k, v in c.items():
        offs[k] = (o, v.shape[1])
        o += v.shape[1]
    pack = np.concatenate([c[k] for k in c], axis=1)
    return pack, offs


CONST_PACK, CONST_OFFS = _consts()


class Builder:
    def __init__(self, debug=False, stop_after=None, dense=True):
        self.debug = debug
        self.dense = dense
        self.stop_after = stop_after
        self.nc = bass.Bass("TRN2", target_bir_lowering=False)
        self.S = Sched(self.nc)
        self.st = contextlib.ExitStack()
        self.uid = 0
        self.store_q = "sp"

    def key(self, name):
        self.uid += 1
        return f"{name}#{self.uid}"

    def dname(self, ap):
        return "dram_" + str(ap.name)

    def din(self, name, shape, dt=F32):
        return self.nc.dram_tensor(name, list(shape), dt, kind="ExternalInput").ap()

    def dout(self, name, shape, dt=F32):
        return self.nc.dram_tensor(name, list(shape), dt, kind="ExternalOutput").ap()

    def dscr(self, name, shape, dt=F32):
        kind = "ExternalOutput" if self.debug else "Internal"
        return self.nc.dram_tensor(name, list(shape), dt, kind=kind).ap()

    def V(self, fn, r=(), w=()):
        return self.S.op("dve", fn, r, w)

    def A(self, fn, r=(), w=()):
        return self.S.op("act", fn, r, w)

    def T(self, fn, r=(), w=()):
        return self.S.op("pe", fn, r, w)

    def G(self, fn, r=(), w=()):
        return self.S.op("pool", fn, r, w)

    def ld(self, out, in_, r=(), w=(), q="sp"):
        return self.S.dma(q, lambda e: e.dma_start(out=out, in_=in_), r, w)

    def ldc(self, out, in_, r=(), w=()):
        return self.S.dma("pool", lambda e: e.dma_start(out=out, in_=in_), r, w)

    def stq(self, out, in_, r, w, q=None):
        q = q or self.store_q
        return self.S.dma(q, lambda e: e.dma_start(out=out, in_=in_), r, w, semkey=("st", r[0]))

    def build(self):
        nc, st = self.nc, self.st
        dbg = self.debug
        x_own = self.din("x_own", [NT, D])
        x_prev = self.din("x_prev", [NPV, D])
        flag = self.din("flag", [128, 1])
        consts_d = self.din("consts", list(CONST_PACK.shape))
        C0s = self.din("C0s", [16 * 8 * 128, 256])
        n0sT = self.din("n0sT", [128, 128])
        m0sT = self.din("m0sT", [8, 16])
        w_in_a = self.din("w_in_a", [D, 6160])
        b_gate = self.din("b_gate", [1, 16])
        hn_gain = self.din("hn_gain", [1, D])
        w_out_a = self.din("w_out_a", [D, D])
        ln_g = {}
        for nm in ("ln_mix_g", "ln_mix_b", "ln_ffn_g", "ln_ffn_b"):
            ln_g[nm] = self.din(nm, [2, D])
        self.ln = ln_g
        self.w_in_b = self.din("w_in_b", [D, 12288])
        self.b_in_b = self.din("b_in_b", [1, 12288])
        self.lnv_g = self.din("lnv_g", [1, 6144])
        self.lnv_b = self.din("lnv_b", [1, 6144])
        self.wsT = self.din("wsT", [128, 8, 128])
        self.wsT_s = self.din("wsT_s", [64, 8, 64])
        self.b_s_t = self.din("b_s_t", [128, 8])
        self.b_s_s = self.din("b_s_s", [64, 8])
        self.w_out_b = self.din("w_out_b", [6144, D])
        self.peer_wq = self.din("peer_wq", [2, D, D])
        self.skT = self.din("skT", [2, 128, 16, 128])
        self.peer_u = self.din("peer_u", [2 * 128 * D, 128])
        self.peer_v = self.din("peer_v", [2 * 16384, D])

        y_own = self.dout("y_own", [NT, D])
        Cp_o = self.dout("Cp", [8 * 128, 256])
        np_o = self.dout("npT", [128, 8])
        mp_o = self.dout("mpT", [8, 1])
        Cs_o = self.dout("Cs", [16 * 8 * 128, 256])
        ns_o = self.dout("nsT", [128, 128])
        ms_o = self.dout("msT", [8, 16])
        vs_o = self.dout("vs", [64, 6144])
        self.outs = dict(y_own=y_own, Cp=Cp_o, npT=np_o, mpT=mp_o, Cs=Cs_o, nsT=ns_o, msT=ms_o, vs=vs_o)

        sc = {}
        sc["qT"] = self.dscr("s_qT", [8, 128, NT], BF16)
        sc["kT"] = self.dscr("s_kT", [8, 128, NT], BF16)
        sc["k"] = self.dscr("s_k", [NALL, 1024], BF16)
        sc["v"] = self.dscr("s_v", [NALL, 2048], BF16)
        sc["og"] = self.dscr("s_og", [NT, 2048], F32)
        sc["gi"] = self.dscr("s_gi", [NALL, 8], F32)
        sc["lf"] = self.dscr("s_lf", [NALL, 8], F32)
        sc["hn"] = self.dscr("s_hn", [NT, D], BF16)
        sc["x1"] = self.dscr("s_x1", [NT, D], F32)
        sc["x2"] = self.dscr("s_x2", [NT, D], F32)
        sc["x3"] = self.dscr("s_x3", [NT, D], F32)
        sc["vraw"] = self.dscr("s_vraw", [NT, 6144], F32)
        sc["mixed"] = self.dscr("s_mixed", [NT, 6144], F32)
        sc["ymix"] = self.dscr("s_ymix", [NT, D], F32)
        sc["G"] = self.nc.dram_tensor("s_G", [9, 128, 128, 128], BF16, kind="Internal").ap()
        self.sc = sc

        ARENA_WORDS = 49 * 1024
        arena_t = st.enter_context(nc.sbuf_tensor("arena", [128, ARENA_WORDS], F32))
        self.ar = Arena(arena_t[:, :])
        cst = st.enter_context(nc.sbuf_tensor("cst", [128, CONST_PACK.shape[1]], F32))
        identb = st.enter_context(nc.sbuf_tensor("identb", [128, 128], BF16))
        colmb = st.enter_context(nc.sbuf_tensor("colmb", [128, 1024], BF16))
        self.ps = [st.enter_context(nc.psum_tensor(f"ps{i}", [128, 512], F32)) for i in range(8)]
        self.psk = [f"ps{i}" for i in range(8)]

        def cv(name):
            o, n = CONST_OFFS[name]
            return cst[:, o:o + n]
        self.ident = cv("ident")
        self.identb = identb[:, :]
        self.tri = cv("tri")
        self.trib = cv("trib")
        self.seqsel = cv("seqsel")
        self.colmb = colmb[:, :].rearrange("p (b c) -> p b c", b=16)
        self.ones = cv("ones")
        self.iota16 = cv("iota16")
        self.thr16 = cv("thr16")
        self.tri128 = cv("tri128")
        self.gelu_native = True
        self.iota128 = cv("iota128")
        self.ld(cst[:, :], consts_d, w=["cst"])
        self.V(lambda e: e.tensor_copy(identb[:, :], cv("ident")), ["cst"], ["identb"])
        self.V(lambda e: e.tensor_copy(colmb[:, :], cv("colmask")), ["cst"], ["colmb"])
        self.CK = ["cst", "identb", "colmb"]

        self.phase_A(x_own, x_prev, w_in_a, b_gate)
        self.S.barrier()
        if self.stop_after != "A":
            self.phase_B(flag, C0s, n0sT, m0sT, hn_gain)
            self.S.barrier()
        if self.stop_after not in ("A", "B"):
            self.phase_C(x_own, w_out_a)
            self.S.barrier()
        srcs = {k: sc[k] for k in ("x1", "x2", "x3")}
        if dbg:
            for k in ("x1", "x2", "x3"):
                srcs[k] = self.din("dbg_" + k, [NT, D])
        if self.stop_after not in ("A", "B", "C"):
            (self.phase_peer_dense if self.dense else self.phase_peer)(srcs["x1"], 0, sc["x2"], "D_")
            self.S.barrier()
        if self.stop_after not in ("A", "B", "C", "D"):
            self.phase_gmlp(srcs["x2"], sc["x3"])
            self.S.barrier()
        if self.stop_after not in ("A", "B", "C", "D", "E"):
            (self.phase_peer_dense if self.dense else self.phase_peer)(srcs["x3"], 1, y_own, "F_")
            self.S.barrier()
        self.S.op("sp", None, reads=tuple(self.S.allkeys))
        self.S.finalize()
        st.close()
        return nc

    def load_xT(self, src, tiles, xT, xTk, col0_of, stg, src_is_bf16=False, kc=16):
        ps, psk = self.ps, self.psk
        for ti, (row0, rows, col0) in enumerate(tiles):
            buf, bk = stg[ti % len(stg)]
            self.ld(buf[:rows, :], src[row0:row0 + rows, :], w=[bk])
            for g4 in range(kc // 4):
                b = (ti * (kc // 4) + g4) % 2
                if src_is_bf16:
                    pt = ps[b][:, 0:256].bitcast(BF16)
                    idn = self.identb
                else:
                    pt = ps[b][:, :]
                    idn = self.ident
                for j in range(4):
                    c = g4 * 4 + j
                    self.T(lambda e, pt=pt, j=j, c=c, buf=buf, rows=rows, idn=idn: e.transpose(
                        pt[:, j * 128:j * 128 + rows], buf[:rows, c * 128:(c + 1) * 128], idn[:rows, :rows]),
                        [bk] + self.CK, [psk[b]])
                src_v = pt.rearrange("p (a b) -> p a b", a=4)[:, :, :rows]
                dst_v = xT[:, g4 * 4:(g4 + 1) * 4, col0:col0 + rows]
                if g4 % 2 == 0:
                    self.V(lambda e, d=dst_v, s=src_v: e.tensor_copy(d, s), [psk[b]], [xTk])
                else:
                    self.A(lambda e, d=dst_v, s=src_v: e.copy(d, s), [psk[b]], [xTk])

    def layer_norm_rows(self, xin, rows, gk, bk_, gtile, btile, out, keys_in, key_out, scr):
        stats, mv, rstd = scr
        for j in range(4):
            self.V(lambda e, j=j: e.bn_stats(stats[:rows, j * 6:(j + 1) * 6], xin[:rows, j * 512:(j + 1) * 512]),
                   keys_in, [key_out + "_st"])
        self.V(lambda e: e.bn_aggr(mv[:rows, :], stats[:rows, :]), [key_out + "_st"], [key_out + "_mv"])
        nmr = rstd[:, 1:2]
        self.A(lambda e: e.activation(out=rstd[:rows, 0:1], in_=mv[:rows, 1:2], func=AF.Ln, bias=self.eps_t[:rows, :], scale=1.0),
               [key_out + "_mv", "eps"], [key_out + "_rs"])
        self.A(lambda e: e.activation(out=rstd[:rows, 0:1], in_=rstd[:rows, 0:1], func=AF.Exp, scale=-0.5), [key_out + "_rs"], [key_out + "_rs"])
        self.V(lambda e: e.scalar_tensor_tensor(out=nmr[:rows, :], in0=mv[:rows, 0:1], scalar=-1.0, in1=rstd[:rows, 0:1], op0=ALU.mult, op1=ALU.mult),
               [key_out + "_mv", key_out + "_rs"], [key_out + "_nm"])
        self.A(lambda e: e.activation(out=out[:rows, :], in_=xin[:rows, :], func=AF.Identity, bias=nmr[:rows, :], scale=rstd[:rows, 0:1]),
               keys_in + [key_out + "_nm", key_out + "_rs"], [key_out])
        self.V(lambda e: e.tensor_tensor(out[:rows, :], out[:rows, :], gtile[:rows, :], ALU.mult), [key_out, gk], [key_out])
        self.V(lambda e: e.tensor_tensor(out[:rows, :], out[:rows, :], btile[:rows, :], ALU.add), [key_out, bk_], [key_out])

    def phase_A(self, x_own, x_prev, w_in_a, b_gate):
        ar, sc, ps, psk = self.ar, self.sc, self.ps, self.psk
        ar.reset()
        xT = ar.bf16(16 * NALL).rearrange("p (c n) -> p c n", c=16)
        xTk = "A_xT"
        stg = [(ar.f32(D), f"A_xs{i}") for i in range(2)]
        wts = [(ar.bf16(16 * 512).rearrange("p (c n) -> p c n", c=16), f"A_w{i}") for i in range(2)]
        ost = [(ar.f32(512), f"A_o{i}") for i in range(4)]
        bg = ar.f32(16)
        sm = [ar.f32(8) for _ in range(4)]
        self.ld(bg, b_gate.partition_broadcast(128).rearrange("p o n -> p (o n)"), w=["A_bg"])
        self.load_xT(x_own, [(r0, rw, r0) for (r0, rw) in TILES], xT, xTk, None, stg)
        self.load_xT(x_prev, [(i * 128, 128, NT + i * 128) for i in range(8)], xT, xTk, None, stg)
        alltiles = TILES + PTILES
        oi = [0]

        def nxt_ost():
            oi[0] += 1
            return ost[oi[0] % 4]

        pb = [0]

        def nxt_bank():
            pb[0] += 1
            return 2 + pb[0] % 6

        for g in range(13):
            ncol = 512 if g < 12 else 16
            wt, wk = wts[g % 2]
            self.ldc(wt[:, :, :ncol], w_in_a[:, g * 512:g * 512 + ncol].rearrange("(c p) n -> p c n", p=128), w=[wk])
            if g < 4:
                dst = sc["qT"] if g < 2 else sc["kT"]
                scale = 1.0 if g < 2 else float(128 ** -0.5)
                for j in range(4):
                    h = (g % 2) * 4 + j
                    for (t0, tn) in ((0, 512), (512, 512), (1024, 64)):
                        b = nxt_bank()
                        for c in range(16):
                            self.T(lambda e, b=b, c=c, j=j, t0=t0, tn=tn, wt=wt: e.matmul(
                                ps[b][:, :tn], lhsT=wt[:, c, j * 128:(j + 1) * 128], rhs=xT[:, c, t0:t0 + tn],
                                start=(c == 0), stop=(c == 15)), [wk, xTk], [psk[b]])
                        ob, ok = nxt_ost()
                        obb = ob.bitcast(BF16)
                        self.A(lambda e, b=b, tn=tn, obb=obb, scale=scale: e.activation(
                            out=obb[:, :tn], in_=ps[b][:, :tn], func=AF.Copy, scale=scale), [psk[b]], [ok])
                        self.stq(dst[h, :, t0:t0 + tn], obb[:, :tn], [ok], [self.dname(dst)])
            if g >= 2:
                tiles = alltiles if (g < 8 or g == 12) else TILES
                for (r0, rw) in tiles:
                    b = nxt_bank()
                    for c in range(16):
                        self.T(lambda e, b=b, c=c, r0=r0, rw=rw, wt=wt, ncol=ncol: e.matmul(
                            ps[b][:rw, :ncol], lhsT=xT[:, c, r0:r0 + rw], rhs=wt[:, c, :ncol],
                            start=(c == 0), stop=(c == 15)), [wk, xTk], [psk[b]])
                    ob, ok = nxt_ost()
                    if g < 4:
                        obb = ob.bitcast(BF16)
                        self.V(lambda e, b=b, rw=rw, obb=obb: e.tensor_scalar(
                            obb[:rw, :512], ps[b][:rw, :], float(128 ** -0.5), None, ALU.mult), [psk[b]], [ok])
                        self.stq(sc["k"][r0:r0 + rw, (g - 2) * 512:(g - 1) * 512], obb[:rw, :512], [ok], ["s_k"])
                    elif g < 8:
                        obb = ob.bitcast(BF16)
                        self.V(lambda e, b=b, rw=rw, obb=obb: e.tensor_copy(obb[:rw, :512], ps[b][:rw, :]), [psk[b]], [ok])
                        self.stq(sc["v"][r0:r0 + rw, (g - 4) * 512:(g - 3) * 512], obb[:rw, :512], [ok], ["s_v"])
                    elif g < 12:
                        self.A(lambda e, b=b, rw=rw, ob=ob: e.activation(out=ob[:rw, :], in_=ps[b][:rw, :], func=AF.Sigmoid),
                               [psk[b]], [ok])
                        self.stq(sc["og"][r0:r0 + rw, (g - 8) * 512:(g - 7) * 512], ob[:rw, :], [ok], ["s_og"])
                    else:
                        gt = ob[:, 0:16]
                        self.V(lambda e, b=b, rw=rw, gt=gt: e.tensor_tensor(gt[:rw, :], ps[b][:rw, :16], bg[:rw, :], ALU.add),
                               [psk[b], "A_bg"], [ok])
                        self.stq(sc["gi"][r0:r0 + rw, :], gt[:rw, 0:8], [ok], ["s_gi"])
                        ob2, ok2 = nxt_ost()
                        xx = gt[:, 8:16]
                        ab, ex, ln_, mn = ob2[:, 0:8], ob2[:, 8:16], ob2[:, 16:24], ob2[:, 24:32]
                        self.A(lambda e, rw=rw, ab=ab, xx=xx: e.activation(out=ab[:rw, :], in_=xx[:rw, :], func=AF.Abs), [ok], [ok2])
                        self.A(lambda e, rw=rw, ab=ab, ex=ex: e.activation(out=ex[:rw, :], in_=ab[:rw, :], func=AF.Exp, scale=-1.0), [ok2], [ok2])
                        self.V(lambda e, rw=rw, ex=ex: e.tensor_scalar_add(ex[:rw, :], ex[:rw, :], 1.0), [ok2], [ok2])
                        self.A(lambda e, rw=rw, ex=ex, ln_=ln_: e.activation(out=ln_[:rw, :], in_=ex[:rw, :], func=AF.Ln), [ok2], [ok2])
                        self.V(lambda e, rw=rw, mn=mn, xx=xx: e.tensor_scalar_min(mn[:rw, :], xx[:rw, :], 0.0), [ok, ok2], [ok2])
                        self.V(lambda e, rw=rw, mn=mn, ln_=ln_: e.tensor_tensor(mn[:rw, :], mn[:rw, :], ln_[:rw, :], ALU.subtract), [ok2], [ok2])
                        self.stq(sc["lf"][r0:r0 + rw, :], mn[:rw, :], [ok2], ["s_lf"])

    def phase_B(self, flag, C0s, n0sT, m0sT, hn_gain):
        ar, sc, ps, psk = self.ar, self.sc, self.ps, self.psk
        self.store_q = "pool"
        ar.reset()
        V, A, T = self.V, self.A, self.T
        CK = self.CK
        ident, tri, trib, seqsel, ones = self.ident, self.tri, self.trib, self.seqsel, self.ones
        CN = ar.f32(8 * 257).rearrange("p (h n) -> p h n", h=8)
        mT = ar.f32(1)
        flg = ar.f32(1)
        gain = ar.f32(D)
        eps_t = ar.f32(1)
        self.eps_t = eps_t
        self.G(lambda e: e.memset(eps_t, LN_EPS), [], ["eps"])
        self.ld(flg, flag, w=["B_flag"])
        self.ld(gain, hn_gain.partition_broadcast(128).rearrange("p o n -> p (o n)"), w=["B_gain"])
        CNK = [f"CN{h}" for h in range(8)]
        V(lambda e: e.memset(CN, 0.0), [], CNK)
        V(lambda e: e.memset(mT, 0.0), [], ["mT"])
        NB = 2

        def mk(n, f):
            return [f() for _ in range(n)]
        gi_b = mk(NB, lambda: ar.f32(8))
        lf_b = mk(NB, lambda: ar.f32(8))
        k_b = mk(NB, lambda: ar.bf16(1024))
        v_b = mk(NB, lambda: ar.bf16(2048))
        qT_b = mk(NB, lambda: ar.bf16(8 * 64).rearrange("p (h n) -> p h n", h=8))
        kT_b = mk(NB, lambda: ar.bf16(8 * 64).rearrange("p (h n) -> p h n", h=8))
        og_b = mk(NB, lambda: ar.f32(2048))
        sca = ar.f32(64)
        a_t, e_t, cl_t, bM_t = sca[:, 0:8], sca[:, 8:16], sca[:, 16:24], sca[:, 24:32]
        hm = ar.f32(160)
        vpp = ar.bf16(8 * 257).rearrange("p (h n) -> p h n", h=8)
        stm = ar.bf16(8 * 64).rearrange("p (h n) -> p h n", h=8)
        cns = mk(2, lambda: ar.bf16(257))
        Mf = ar.f32(256)
        Mtok = ar.f32(8)
        hg = ar.f32(2048)
        hnb = mk(2, lambda: ar.bf16(2048))
        stt = ar.f32(8 * 6)
        mvv = ar.f32(16)
        rr = ar.f32(16)
        den = ar.f32(8)
        Rm = ar.f32(256)
        Mx = ar.f32(128)
        NCS = 6
        cst_s = mk(NCS, lambda: ar.f32(257))
        kz_all = ar.bf16(16 * 1024).rearrange("p (b n) -> p b n", b=16)
        qz_all = ar.bf16(8 * 16 * 64).rearrange("p (h b n) -> p h b n", h=8, b=16)
        nsT = ar.f32(128)
        n0t = ar.f32(128)
        m0t = ar.f32(16)
        self.ld(n0t, n0sT, w=["B_n0"])
        self.ld(m0t[:8, :], m0sT, w=["B_m0"])

        def chunk(ci, tok0, mode):
            sl = ci % NB
            ks = f"B{sl}"
            samp = mode == "sample"
            NS, LS = (16, 4) if samp else (1, 64)
            trim = trib if samp else tri
            full = mode != "prefix"
            gi, lf, kk, vv, qT, kT, og = gi_b[sl], lf_b[sl], k_b[sl], v_b[sl], qT_b[sl], kT_b[sl], og_b[sl]
            self.ld(gi[:64, :], sc["gi"][tok0:tok0 + 64, :], w=[ks + "gi"])
            self.ld(lf[:64, :], sc["lf"][tok0:tok0 + 64, :], w=[ks + "lf"])
            self.ld(kk[:64, :], sc["k"][tok0:tok0 + 64, :], w=[ks + "k"])
            self.ld(vv[:64, :], sc["v"][tok0:tok0 + 64, :], w=[ks + "v"])
            if full:
                self.ld(qT, sc["qT"][:, :, tok0:tok0 + 64].rearrange("h p n -> p h n"), w=[ks + "qT"])
                self.ld(kT, sc["kT"][:, :, tok0:tok0 + 64].rearrange("h p n -> p h n"), w=[ks + "kT"])
                self.ld(og[:64, :], sc["og"][tok0:tok0 + 64, :], w=[ks + "og"])
            T(lambda e: e.matmul(ps[0][:64, 0:8], lhsT=trim[:64, :], rhs=lf[:64, :], start=True, stop=True), [ks + "lf"] + CK, ["ps0"])
            V(lambda e: e.tensor_tensor(a_t[:64, :], gi[:64, :], ps[0][:64, 0:8], ALU.subtract), [ks + "gi", "ps0"], ["a_t"])
            T(lambda e: e.transpose(ps[0][:8, 64:128], a_t[:64, :], ident[:64, :64]), ["a_t"] + CK, ["ps0"])
            amax = hm[:8, 0:NS]
            V(lambda e: e.tensor_reduce(amax, ps[0][:8, 64:128].rearrange("p (b t) -> p b t", b=NS), AX.X, ALU.max), ["ps0"], ["amax"])
            m_old = m0t[:8, 0:16] if samp else mT[:8, 0:1]
            mk_old = "B_m0" if samp else "mT"
            MT = hm[:8, 16:16 + NS]
            fT = hm[:8, 32:32 + NS]
            V(lambda e: e.tensor_tensor(MT, amax, m_old, ALU.max), ["amax", mk_old], ["MT"])
            V(lambda e: e.tensor_tensor(fT, m_old, MT, ALU.subtract), ["MT", mk_old], ["fT"])
            A(lambda e: e.activation(out=fT, in_=fT, func=AF.Exp), ["fT"], ["fT"])
            selm = seqsel[:64, 0:16] if samp else ones[:64, 0:1]
            T(lambda e: e.matmul(ps[0][:8, 128:128 + NS], lhsT=lf[:64, :], rhs=selm, start=True, stop=True), [ks + "lf"] + CK, ["ps0"])
            mnew = hm[:8, 48:48 + NS]
            V(lambda e: e.tensor_tensor(mnew, ps[0][:8, 128:128 + NS], MT, ALU.add), ["ps0", "MT"], ["mnew"])
            V(lambda e: e.tensor_copy(Mx[:8, 0:64].rearrange("p (b t) -> p b t", b=NS), MT.unsqueeze(2).broadcast_to([8, NS, LS])), ["MT"], ["Mx"])
            T(lambda e: e.transpose(ps[0][:64, 160:168], Mx[:8, 0:64], ident[:8, :8]), ["Mx"] + CK, ["ps0"])
            V(lambda e: e.tensor_copy(Mtok[:64, :], ps[0][:64, 160:168]), ["ps0"], ["Mtok"])
            V(lambda e: e.tensor_tensor(e_t[:64, :], a_t[:64, :], Mtok[:64, :], ALU.subtract), ["a_t", "Mtok"], ["e_t"])
            A(lambda e: e.activation(out=e_t[:64, :], in_=e_t[:64, :], func=AF.Exp), ["e_t"], ["e_t"])
            if full:
                V(lambda e: e.tensor_tensor(bM_t[:64, :], gi[:64, :], a_t[:64, :], ALU.subtract), [ks + "gi", "a_t"], ["bM"])
                V(lambda e: e.tensor_tensor(bM_t[:64, :], bM_t[:64, :], Mtok[:64, :], ALU.add), ["bM", "Mtok"], ["bM"])
                A(lambda e: e.activation(out=cl_t[:64, :], in_=bM_t[:64, :], func=AF.Exp, scale=-1.0), ["bM"], ["cl_t"])
            Rv = Rm[:8, 0:NS * 8].rearrange("p (b h) -> p b h", b=NS)
            V(lambda e: e.tensor_tensor(Rv, fT.unsqueeze(2).broadcast_to([8, NS, 8]),
                                        ident[:8, 0:8].unsqueeze(1).broadcast_to([8, NS, 8]), ALU.mult), ["fT"] + CK, ["Rm"])
            T(lambda e: e.matmul(ps[0][:, 256:256 + NS * 8], lhsT=ones[:8, :], rhs=Rm[:8, 0:NS * 8], start=True, stop=True), ["Rm"] + CK, ["ps0"])
            V(lambda e: e.tensor_copy(Mf[:, 0:NS * 8], ps[0][:, 256:256 + NS * 8]), ["ps0"], ["Mf"])
            V(lambda e: e.tensor_tensor(vpp[:64, :, 0:256], vv[:64, :].rearrange("p (h n) -> p h n", h=8),
                                        e_t[:64, :].unsqueeze(2).broadcast_to([64, 8, 256]), ALU.mult), [ks + "v", "e_t"], ["vpp"])
            V(lambda e: e.tensor_copy(vpp[:64, :, 256:257], e_t[:64, :].unsqueeze(2)), ["e_t", "vpp"], ["vpp"])
            if full:
                for h in range(8):
                    T(lambda e, h=h: e.matmul(ps[1][:64, h * 64:(h + 1) * 64], lhsT=kT[:, h, :], rhs=qT[:, h, :], start=True, stop=True),
                      [ks + "kT", ks + "qT"], ["ps1"])
                V(lambda e: e.tensor_tensor(stm[:64, :, :], ps[1][:64, :].rearrange("p (h n) -> p h n", h=8),
                                            trim[:64, :].unsqueeze(1).broadcast_to([64, 8, 64]), ALU.mult), ["ps1"] + CK, ["stm"])
            spend = []
            if samp:
                for b in range(16):
                    V(lambda e, b=b: e.tensor_scalar(kz_all[:64, b, :], kk[:64, :], seqsel[:64, b:b + 1], None, ALU.mult), [ks + "k"] + CK, ["kz_all"])
                for h in range(8):
                    V(lambda e, h=h: e.tensor_tensor(qz_all[:, h], qT[:, h, :].unsqueeze(1).broadcast_to([128, 16, 64]), self.colmb, ALU.mult), [ks + "qT"] + CK, ["qz_all"])
            for h in range(8):
                nb = 2 + h % 2
                nk = f"psn{h % 2}"
                if full:
                    T(lambda e, h=h, nb=nb: e.matmul(ps[nb][:64, 0:257], lhsT=stm[:64, h, :], rhs=vpp[:64, h, :], start=True, stop=False),
                      ["stm", "vpp"], [nk])
                if not samp:
                    cb, ckk = cns[h % 2], f"cns{h % 2}"
                    if full:
                        A(lambda e, h=h, cb=cb: e.activation(out=cb, in_=CN[:, h, :], func=AF.Copy, scale=Mf[:, h:h + 1]), [f"CN{h}", "Mf"], [ckk])
                        T(lambda e, h=h, nb=nb, cb=cb: e.matmul(ps[nb][:64, 0:257], lhsT=qT[:, h, :], rhs=cb, start=False, stop=True),
                          [ckk, ks + "qT"], [nk])
                    kb = 4 + h % 2
                    T(lambda e, h=h, kb=kb: e.matmul(ps[kb][:, 0:257], lhsT=kk[:64, h * 128:(h + 1) * 128], rhs=vpp[:64, h, :], start=True, stop=True),
                      [ks + "k", "vpp"], [f"psk{h % 2}"])
                    V(lambda e, h=h, kb=kb: e.scalar_tensor_tensor(out=CN[:, h, :], in0=CN[:, h, :], scalar=Mf[:, h:h + 1], in1=ps[kb][:, 0:257],
                                                                    op0=ALU.mult, op1=ALU.add), [f"CN{h}", "Mf", f"psk{h % 2}"], [f"CN{h}"])
                else:
                    for b in range(16):
                        i = h * 16 + b
                        cs_, csk = cst_s[i % NCS], f"cs{i % NCS}"
                        row = (b * 8 + h) * 128
                        self.ld(cs_[:, 0:256], C0s[row:row + 128, :], w=[csk])
                        V(lambda e, cs_=cs_, b=b, h=h: e.tensor_copy(cs_[:, 256:257], n0t[:, b * 8 + h:b * 8 + h + 1]), ["B_n0", csk], [csk])
                        cb, ckk = cns[i % 2], f"cns{i % 2}"
                        A(lambda e, cb=cb, cs_=cs_, b=b, h=h: e.activation(out=cb, in_=cs_, func=AF.Copy, scale=Mf[:, b * 8 + h:b * 8 + h + 1]),
                          [csk, "Mf"], [ckk])
                        T(lambda e, nb=nb, cb=cb, b=b, h=h: e.matmul(ps[nb][:64, 0:257], lhsT=qz_all[:, h, b, :], rhs=cb, start=False, stop=(b == 15)),
                          ["qz_all", ckk], [nk])
                        kb = 4 + i % 2
                        T(lambda e, kb=kb, h=h, b=b: e.matmul(ps[kb][:, 0:257], lhsT=kz_all[:64, b, h * 128:(h + 1) * 128], rhs=vpp[:64, h, :], start=True, stop=True),
                          ["kz_all", "vpp"], [f"psk{i % 2}"])
                        def back_(cs_=cs_, csk=csk, kb=kb, b=b, h=h, i=i, row=row):
                            V(lambda e: e.scalar_tensor_tensor(out=cs_, in0=cs_, scalar=Mf[:, b * 8 + h:b * 8 + h + 1], in1=ps[kb][:, 0:257],
                                                               op0=ALU.mult, op1=ALU.add), [csk, "Mf", f"psk{i % 2}"], [csk])
                            self.stq(self.outs["Cs"][row:row + 128, :], cs_[:, 0:256], [csk], ["Cs"])
                            V(lambda e: e.tensor_copy(nsT[:, b * 8 + h:b * 8 + h + 1], cs_[:, 256:257]), [csk, "nsT"], ["nsT"])
                        if spend:
                            spend.pop(0)()
                        spend.append(back_)
                if full:
                    A(lambda e, h=h, nb=nb: e.activation(out=den[:64, h:h + 1], in_=ps[nb][:64, 256:257], func=AF.Abs), [nk], ["den"])
                    V(lambda e, h=h: e.tensor_tensor(den[:64, h:h + 1], den[:64, h:h + 1], cl_t[:64, h:h + 1], ALU.max), ["den", "cl_t"], ["den"])
                    V(lambda e, h=h: e.reciprocal(den[:64, h:h + 1], den[:64, h:h + 1]), ["den"], ["den"])
                    V(lambda e, h=h, nb=nb: e.scalar_tensor_tensor(out=hg[:64, h * 256:(h + 1) * 256], in0=ps[nb][:64, 0:256], scalar=den[:64, h:h + 1],
                                                                    in1=og[:64, h * 256:(h + 1) * 256], op0=ALU.mult, op1=ALU.mult),
                      [nk, "den", ks + "og"], ["hg"])
                    V(lambda e, h=h: e.bn_stats(stt[:64, h * 6:(h + 1) * 6], hg[:64, h * 256:(h + 1) * 256]), ["hg"], ["stt"])
                    V(lambda e, h=h: e.bn_aggr(mvv[:64, h * 2:(h + 1) * 2], stt[:64, h * 6:(h + 1) * 6]), ["stt"], ["mvv"])
            if full:
                mv3 = mvv[:64, :].rearrange("p (h t) -> p h t", t=2)
                A(lambda e: e.activation(out=rr[:64, 0:8], in_=mv3[:, :, 1], func=AF.Ln, bias=eps_t[:64, :], scale=1.0), ["mvv", "eps"], ["rr"])
                A(lambda e: e.activation(out=rr[:64, 0:8], in_=rr[:64, 0:8], func=AF.Exp, scale=-0.5), ["rr"], ["rr"])
                for h in range(8):
                    V(lambda e, h=h: e.tensor_scalar(hg[:64, h * 256:(h + 1) * 256], hg[:64, h * 256:(h + 1) * 256], mvv[:64, 2 * h:2 * h + 1], rr[:64, h:h + 1],
                                                     ALU.subtract, ALU.mult), ["hg", "mvv", "rr"], ["hg"])
                ob, obk = hnb[ci % 2], f"hnb{ci % 2}"
                V(lambda e, ob=ob: e.tensor_tensor(ob[:64, :], hg[:64, :], gain[:64, :], ALU.mult), ["hg", "B_gain"], [obk])
                self.stq(sc["hn"][tok0:tok0 + 64, :], ob[:64, :], [obk], ["s_hn"])
            while spend:
                spend.pop(0)()
            if not samp:
                V(lambda e: e.tensor_copy(mT[:8, :], mnew), ["mnew"], ["mT"])
            else:
                V(lambda e: e.tensor_copy(hm[:8, 64:80], mnew), ["mnew"], ["ms_out"])
                self.stq(self.outs["msT"], hm[:8, 64:80], ["ms_out"], ["msT"])
                self.stq(self.outs["nsT"], nsT, ["nsT"], ["nsTo"])

        for ci in range(16):
            chunk(ci, NT + ci * 64, "prefix")
        V(lambda e: e.tensor_scalar(CN, CN, flg[:, 0:1], None, ALU.mult), CNK + ["B_flag"], CNK)
        V(lambda e: e.tensor_scalar(mT[:8, :], mT[:8, :], flg[:8, 0:1], None, ALU.mult), ["mT", "B_flag"], ["mT"])
        for ci in range(16):
            chunk(ci, ci * 64, "own")
        for h in range(8):
            self.stq(self.outs["Cp"][h * 128:(h + 1) * 128, :], CN[:, h, 0:256], [f"CN{h}"], ["Cp"])
        V(lambda e: e.tensor_copy(Rm[:, 128:136], CN[:, :, 256]), CNK + ["Rm"], ["npo"])
        self.stq(self.outs["npT"], Rm[:, 128:136], ["npo"], ["npT"])
        self.stq(self.outs["mpT"], mT[:8, :], ["mT"], ["mpT"])
        chunk(16, 1024, "sample")
        self.store_q = "sp"

    def out_proj_ln(self, src_bf16, resid_src, w_dram, kc, li, which, dst, prefix):
        ar, ps, psk = self.ar, self.ps, self.psk
        V, A, T = self.V, self.A, self.T
        ar.reset()
        eps_t = ar.f32(1)
        self.eps_t = eps_t
        self.G(lambda e: e.memset(eps_t, LN_EPS), [], ["eps"])
        xT = ar.bf16(kc * NT).rearrange("p (c n) -> p c n", c=kc)
        xTk = prefix + "xT"
        stg = [(ar.bf16(kc * 128), f"{prefix}xs{i}") for i in range(2)]
        self.load_xT(src_bf16, [(r0, rw, r0) for (r0, rw) in TILES], xT, xTk, None, stg, src_is_bf16=True, kc=kc)
        wt = ar.bf16(kc * D).rearrange("p (c n) -> p c n", c=kc)
        for q4 in range(4):
            self.ldc(wt[:, :, q4 * 512:(q4 + 1) * 512], w_dram[:, q4 * 512:(q4 + 1) * 512].rearrange("(c p) n -> p c n", p=128), w=[prefix + "w"])
        gt, bt = ar.f32(D), ar.f32(D)
        self.ld(gt, self.ln[f"ln_{which}_g"][li:li + 1, :].partition_broadcast(128).rearrange("p o n -> p (o n)"), w=[prefix + "g"])
        self.ld(bt, self.ln[f"ln_{which}_b"][li:li + 1, :].partition_broadcast(128).rearrange("p o n -> p (o n)"), w=[prefix + "b"])
        xr = [(ar.f32(D), f"{prefix}xr{i}") for i in range(2)]
        pre = [(ar.f32(D), f"{prefix}pre{i}") for i in range(2)]
        scr = (ar.f32(24), ar.f32(2), ar.f32(2))
        for ti, (r0, rw) in enumerate(TILES):
            xb, xk = xr[ti % 2]
            pb, pk = pre[ti % 2]
            self.ld(xb[:rw, :], resid_src[r0:r0 + rw, :], w=[xk])
            for q4 in range(4):
                b = 2 + (ti * 4 + q4) % 6
                for c in range(kc):
                    T(lambda e, b=b, c=c, q4=q4, r0=r0, rw=rw: e.matmul(ps[b][:rw, :], lhsT=xT[:, c, r0:r0 + rw], rhs=wt[:, c, q4 * 512:(q4 + 1) * 512],
                                                                         start=(c == 0), stop=(c == kc - 1)), [xTk, prefix + "w"], [psk[b]])
                V(lambda e, b=b, q4=q4, rw=rw, xb=xb, pb=pb: e.scalar_tensor_tensor(out=pb[:rw, q4 * 512:(q4 + 1) * 512], in0=xb[:rw, q4 * 512:(q4 + 1) * 512],
                                                                                     scalar=ALPHA, in1=ps[b][:rw, :], op0=ALU.mult, op1=ALU.add),
                  [xk, psk[b]], [pk])
            self.layer_norm_rows(pb, rw, prefix + "g", prefix + "b", gt, bt, pb, [pk], pk, scr)
            self.stq(dst[r0:r0 + rw, :], pb[:rw, :], [pk], [self.dname(dst)], q="pool")

    def phase_C(self, x_own, w_out_a):
        self.out_proj_ln(self.sc["hn"], x_own, w_out_a, 16, 0, "mix", self.sc["x1"], "C_")


    def gelu_tanh(self, out, in_, tmp, rows, kin, kout, ktmp):
        V, A = self.V, self.A
        if self.gelu_native:
            A(lambda e: e.activation(out=out, in_=in_, func=AF.Gelu_apprx_tanh), kin, [kout])
            return
        A(lambda e: e.activation(out=tmp, in_=in_, func=AF.Square), kin, [ktmp])
        V(lambda e: e.tensor_scalar(tmp, tmp, 0.044715, 1.0, ALU.mult, ALU.add), [ktmp], [ktmp])
        V(lambda e: e.tensor_tensor(tmp, tmp, in_, ALU.mult), [ktmp] + kin, [ktmp])
        A(lambda e: e.activation(out=tmp, in_=tmp, func=AF.Tanh, scale=0.7978845608028654), [ktmp], [ktmp])
        V(lambda e: e.tensor_scalar(tmp, tmp, 1.0, 0.5, ALU.add, ALU.mult), [ktmp], [ktmp])
        V(lambda e: e.tensor_tensor(out, tmp, in_, ALU.mult), [ktmp] + kin, [kout])

    def phase_peer(self, src, li, dst, pf):
        ar, ps, psk = self.ar, self.ps, self.psk
        V, A, T = self.V, self.A, self.T
        ar.reset()
        eps_t = ar.f32(1)
        self.eps_t = eps_t
        self.G(lambda e: e.memset(eps_t, LN_EPS), [], ["eps"])
        qT = ar.bf16(16 * NT).rearrange("p (c n) -> p c n", c=16)
        skt = ar.bf16(16 * 128).rearrange("p (c n) -> p c n", c=16)
        gt, bt = ar.f32(D), ar.f32(D)
        self.ld(gt, self.ln["ln_ffn_g"][li:li + 1, :].partition_broadcast(128).rearrange("p o n -> p (o n)"), w=[pf + "g"])
        self.ld(bt, self.ln["ln_ffn_b"][li:li + 1, :].partition_broadcast(128).rearrange("p o n -> p (o n)"), w=[pf + "b"])
        self.ldc(skt, self.skT[li], w=[pf + "sk"])
        mark = ar.off
        xT = ar.bf16(16 * NT).rearrange("p (c n) -> p c n", c=16)
        stg = [(ar.f32(D), f"{pf}xs{i}") for i in range(2)]
        wq = ar.bf16(16 * D).rearrange("p (c n) -> p c n", c=16)
        for q4 in range(4):
            self.ldc(wq[:, :, q4 * 512:(q4 + 1) * 512], self.peer_wq[li, :, q4 * 512:(q4 + 1) * 512].rearrange("(c p) n -> p c n", p=128), w=[pf + "wq"])
        self.load_xT(src, [(r0, rw, r0) for (r0, rw) in TILES], xT, pf + "xT", None, stg)
        n = 0
        for cg in range(16):
            for (t0, tn) in ((0, 512), (512, 512), (1024, 64)):
                b = 2 + n % 6
                n += 1
                for c in range(16):
                    T(lambda e, b=b, c=c, cg=cg, t0=t0, tn=tn: e.matmul(ps[b][:, :tn], lhsT=wq[:, c, cg * 128:(cg + 1) * 128], rhs=xT[:, c, t0:t0 + tn],
                                                                       start=(c == 0), stop=(c == 15)), [pf + "wq", pf + "xT"], [psk[b]])
                if n % 2:
                    V(lambda e, b=b, cg=cg, t0=t0, tn=tn: e.tensor_copy(qT[:, cg, t0:t0 + tn], ps[b][:, :tn]), [psk[b]], [pf + "qT"])
                else:
                    A(lambda e, b=b, cg=cg, t0=t0, tn=tn: e.copy(qT[:, cg, t0:t0 + tn], ps[b][:, :tn]), [psk[b]], [pf + "qT"])
        self.S.barrier()
        ar.off = mark
        def two(f):
            return [f(), f()]
        xt = two(lambda: ar.f32(D))
        idx = two(lambda: ar.i32(128))
        gw = two(lambda: ar.f32(128))
        Sc = ar.f32(2048).rearrange("p (c n) -> p c n", c=16)
        Wk = ar.f32(2048).rearrange("p (c n) -> p c n", c=16)
        sv = ar.f32(256).rearrange("p (c n) -> p c n", c=16)
        si = ar.u32(256).rearrange("p (c n) -> p c n", c=16)
        sif = ar.f32(256).rearrange("p (h t n) -> p h t n", h=8, t=2)
        cand = ar.f32(2048).rearrange("p (h n) -> p h n", h=8)
        cw = ar.f32(2048).rearrange("p (h n) -> p h n", h=8)
        cs = ar.f32(128).rearrange("p (h n) -> p h n", h=8)
        cp = ar.u32(128).rearrange("p (h n) -> p h n", h=8)
        ci = ar.u32(128).rearrange("p (h n) -> p h n", h=8)
        cj = ar.u32(128).rearrange("p (h n) -> p h n", h=8)
        cif = ar.f32(128).rearrange("p (h n) -> p h n", h=8)
        cjf = ar.f32(128).rearrange("p (h n) -> p h n", h=8)
        oh = ar.f32(2048).rearrange("p (h k n) -> p h k n", h=8, k=16)
        n0 = ar.f32(128).rearrange("p (h n) -> p h n", h=8)
        n1 = ar.f32(128).rearrange("p (h n) -> p h n", h=8)
        zz = ar.f32(16)
        av = ar.f32(128)
        wv = ar.f32(128)
        gtmp = ar.f32(128)
        NG = 4
        gb = [ar.f32(D) for _ in range(NG)]
        junk = ar.bf16(D)
        acc = two(lambda: ar.f32(D))
        scr = (ar.f32(24), ar.f32(2), ar.f32(2))
        sv4 = sv.rearrange("p (h t) n -> p h t n", t=2)
        for i in range(2):
            V(lambda e, i=i: e.memset(idx[i], 0), [], [f"{pf}idx{i}"])

        def topk(ti):
            r0, rw = TILES[ti]
            s2 = ti % 2
            xk, ik, gk = f"{pf}xt{s2}", f"{pf}idx{s2}", f"{pf}gw{s2}"
            self.ld(xt[s2][:rw, :], src[r0:r0 + rw, :], w=[xk])
            for g4 in range(4):
                b = 2 + (ti * 4 + g4) % 6
                for j in range(4):
                    cg = g4 * 4 + j
                    T(lambda e, b=b, j=j, cg=cg: e.matmul(ps[b][:rw, j * 128:(j + 1) * 128], lhsT=qT[:, cg, r0:r0 + rw], rhs=skt[:, cg, :], start=True, stop=True),
                      [pf + "qT", pf + "sk"], [psk[b]])
                dstv = Sc[:rw, g4 * 4:(g4 + 1) * 4, :]
                srcv = ps[b][:rw, :].rearrange("p (a n) -> p a n", a=4)
                wkeys = [f"{pf}S{g4 * 4 + j}" for j in range(4)]
                if g4 % 2:
                    A(lambda e, d=dstv, s=srcv: e.copy(d, s), [psk[b]], wkeys)
                else:
                    V(lambda e, d=dstv, s=srcv: e.tensor_copy(d, s), [psk[b]], wkeys)
            for cg in range(16):
                V(lambda e, cg=cg: e.max(out=sv[:rw, cg, 0:8], in_=Sc[:rw, cg, :]), [f"{pf}S{cg}"], [f"{pf}sv{cg}"])
            for cg in range(16):
                V(lambda e, cg=cg: e.max_index(out=si[:rw, cg, 0:8], in_max=sv[:rw, cg, 0:8], in_values=Sc[:rw, cg, :]), [f"{pf}S{cg}", f"{pf}sv{cg}"], [f"{pf}si{cg}"])
            for cg in range(16):
                V(lambda e, cg=cg: e.match_replace(out=Wk[:rw, cg, :], in_to_replace=sv[:rw, cg, 0:8], in_values=Sc[:rw, cg, :], imm_value=NEG),
                  [f"{pf}S{cg}", f"{pf}sv{cg}"], [f"{pf}W{cg}"])
            for cg in range(16):
                V(lambda e, cg=cg: e.max(out=sv[:rw, cg, 8:16], in_=Wk[:rw, cg, :]), [f"{pf}W{cg}"], [f"{pf}sv{cg}"])
            for cg in range(16):
                V(lambda e, cg=cg: e.max_index(out=si[:rw, cg, 8:16], in_max=sv[:rw, cg, 8:16], in_values=Wk[:rw, cg, :]), [f"{pf}W{cg}", f"{pf}sv{cg}"], [f"{pf}si{cg}"])
            svk = [f"{pf}sv{cg}" for cg in range(16)]
            sik = [f"{pf}si{cg}" for cg in range(16)]
            V(lambda e: e.tensor_copy(sif[:rw].rearrange("p h t n -> p (h t) n"), si[:rw]), sik, [pf + "sif"])
            V(lambda e: e.tensor_tensor(cand[:rw].rearrange("p h (i j) -> p h i j", i=16),
                                        sv4[:rw, :, 0, :].unsqueeze(3).broadcast_to([rw, 8, 16, 16]),
                                        sv4[:rw, :, 1, :].unsqueeze(2).broadcast_to([rw, 8, 16, 16]), ALU.add), svk, [pf + "cand"])
            ck = [f"{pf}c{h}" for h in range(8)]
            for h in range(8):
                V(lambda e, h=h: e.max(out=cs[:rw, h, 0:8], in_=cand[:rw, h, :]), [pf + "cand"], [ck[h] + "s"])
            for h in range(8):
                V(lambda e, h=h: e.max_index(out=cp[:rw, h, 0:8], in_max=cs[:rw, h, 0:8], in_values=cand[:rw, h, :]), [pf + "cand", ck[h] + "s"], [ck[h] + "p"])
            for h in range(8):
                V(lambda e, h=h: e.match_replace(out=cw[:rw, h, :], in_to_replace=cs[:rw, h, 0:8], in_values=cand[:rw, h, :], imm_value=NEG),
                  [pf + "cand", ck[h] + "s"], [ck[h] + "w"])
            for h in range(8):
                V(lambda e, h=h: e.max(out=cs[:rw, h, 8:16], in_=cw[:rw, h, :]), [ck[h] + "w"], [ck[h] + "s"])
            for h in range(8):
                V(lambda e, h=h: e.max_index(out=cp[:rw, h, 8:16], in_max=cs[:rw, h, 8:16], in_values=cw[:rw, h, :]), [ck[h] + "w", ck[h] + "s"], [ck[h] + "p"])
            csk = [c + "s" for c in ck]
            cpk = [c + "p" for c in ck]
            cpf = n1
            V(lambda e: e.tensor_copy(cpf[:rw], cp[:rw]), cpk, [pf + "cpf"])
            th4 = self.thr16[:rw, :].unsqueeze(1).unsqueeze(1).broadcast_to([rw, 8, 16, 16])
            V(lambda e: e.tensor_tensor(oh[:rw], cpf[:rw].unsqueeze(3).broadcast_to([rw, 8, 16, 16]), th4, ALU.is_ge), [pf + "cpf"] + self.CK, [pf + "oh"])
            V(lambda e: e.tensor_reduce(cif[:rw], oh[:rw], AX.X, ALU.add), [pf + "oh"], [pf + "cif"])
            V(lambda e: e.scalar_tensor_tensor(out=cjf[:rw], in0=cif[:rw], scalar=-16.0, in1=cpf[:rw], op0=ALU.mult, op1=ALU.add), [pf + "cif", pf + "cpf"], [pf + "cjf"])
            io4 = self.iota16[:rw, :].unsqueeze(1).unsqueeze(1).broadcast_to([rw, 8, 16, 16])
            for (cf, half, nn, kk) in ((cif, 0, n0, "n0"), (cjf, 1, n1, "n1")):
                V(lambda e, cf=cf: e.tensor_tensor(oh[:rw], cf[:rw].unsqueeze(3).broadcast_to([rw, 8, 16, 16]), io4, ALU.is_equal),
                  [pf + "cif", pf + "cjf"] + self.CK, [pf + "oh"])
                V(lambda e, half=half: e.tensor_tensor(oh[:rw], oh[:rw], sif[:rw, :, half, :].unsqueeze(2).broadcast_to([rw, 8, 16, 16]), ALU.mult),
                  [pf + "oh", pf + "sif"], [pf + "oh"])
                V(lambda e, nn=nn: e.tensor_reduce(nn[:rw], oh[:rw], AX.X, ALU.add), [pf + "oh"], [pf + kk])
            V(lambda e: e.scalar_tensor_tensor(out=n0[:rw], in0=n0[:rw], scalar=128.0, in1=n1[:rw], op0=ALU.mult, op1=ALU.add), [pf + "n0", pf + "n1"], [pf + "n0"])
            V(lambda e: e.tensor_scalar_add(n0[:rw], n0[:rw], float(li * 16384)), [pf + "n0"], [pf + "n0"])
            V(lambda e: e.tensor_copy(idx[s2][:rw, :].rearrange("p (h n) -> p h n", h=8), n0[:rw]), [pf + "n0"], [ik])
            g3 = gw[s2][:rw, :].rearrange("p (h n) -> p h n", h=8)
            V(lambda e: e.tensor_tensor(g3, cs[:rw], cs[:rw, :, 0:1].broadcast_to([rw, 8, 16]), ALU.subtract), csk, [gk])
            A(lambda e: e.activation(out=g3, in_=g3, func=AF.Exp), [gk], [gk])
            V(lambda e: e.tensor_reduce(zz[:rw, 0:8], g3, AX.X, ALU.add), [gk], [pf + "zz"])
            V(lambda e: e.reciprocal(zz[:rw, 0:8], zz[:rw, 0:8]), [pf + "zz"], [pf + "zz"])
            V(lambda e: e.tensor_tensor(g3, g3, zz[:rw, 0:8].unsqueeze(2).broadcast_to([rw, 8, 16]), ALU.mult), [gk, pf + "zz"], [gk])

        gctr = [0]

        def gather(tab, ti, hk):
            r0, rw = TILES[ti]
            s2 = ti % 2
            sl = gctr[0] % NG
            gctr[0] += 1
            buf, bk = gb[sl], f"{pf}gb{sl}"
            self.S.dma("pool", lambda e, buf=buf: e.indirect_dma_start(
                out=buf[:rw, :], out_offset=None, in_=tab,
                in_offset=bass.IndirectOffsetOnAxis(ap=idx[s2][:rw, hk:hk + 1], axis=0)), [f"{pf}idx{s2}"], [bk])
            return buf, bk

        def udots(ti):
            r0, rw = TILES[ti]
            s2 = ti % 2
            for hk in range(128):
                buf, bk = gather(self.peer_u, ti, hk)
                V(lambda e, buf=buf, hk=hk: e.scalar_tensor_tensor(out=junk[:rw, :], in0=buf[:rw, :], scalar=1.0, in1=xt[s2][:rw, :],
                                                                   op0=ALU.mult, op1=ALU.mult, accum_out=av[:rw, hk:hk + 1]),
                  [bk, f"{pf}xt{s2}"], [f"{pf}junk{hk % 4}", f"{pf}av{hk % 8}"])
            avk = [f"{pf}av{i}" for i in range(8)]
            self.gelu_tanh(wv[:rw, :], av[:rw, :], gtmp[:rw, :], rw, avk, pf + "wv", pf + "gtmp")
            V(lambda e: e.tensor_tensor(wv[:rw, :], wv[:rw, :], gw[s2][:rw, :], ALU.mult), [pf + "wv", f"{pf}gw{s2}"], [pf + "wv"])

        def vacc(ti):
            r0, rw = TILES[ti]
            s2 = ti % 2
            for hk in range(128):
                buf, bk = gather(self.peer_v, ti, hk)
                a_, ak = acc[hk % 2], f"{pf}acc{hk % 2}"
                if hk < 2:
                    V(lambda e, buf=buf, hk=hk, a_=a_: e.tensor_scalar(a_[:rw, :], buf[:rw, :], wv[:rw, hk:hk + 1], None, ALU.mult), [bk, pf + "wv"], [ak])
                else:
                    V(lambda e, buf=buf, hk=hk, a_=a_: e.scalar_tensor_tensor(out=a_[:rw, :], in0=buf[:rw, :], scalar=wv[:rw, hk:hk + 1], in1=a_[:rw, :],
                                                                               op0=ALU.mult, op1=ALU.add), [bk, pf + "wv", ak], [ak])
            a0, a1 = acc
            V(lambda e: e.tensor_tensor(a0[:rw, :], a0[:rw, :], a1[:rw, :], ALU.add), [pf + "acc0", pf + "acc1"], [pf + "acc0"])
            V(lambda e: e.scalar_tensor_tensor(out=a0[:rw, :], in0=xt[s2][:rw, :], scalar=ALPHA, in1=a0[:rw, :], op0=ALU.mult, op1=ALU.add),
              [pf + "acc0", f"{pf}xt{s2}"], [pf + "acc0"])
            self.layer_norm_rows(a0, rw, pf + "g", pf + "b", gt, bt, a1, [pf + "acc0"], pf + "acc1", scr)
            self.stq(dst[r0:r0 + rw, :], a1[:rw, :], [pf + "acc1"], [self.dname(dst)])

        topk(0)
        for ti in range(len(TILES)):
            udots(ti)
            if ti + 1 < len(TILES):
                topk(ti + 1)
            vacc(ti)

    def phase_gmlp(self, src, dst):
        ar, sc, ps, psk = self.ar, self.sc, self.ps, self.psk
        V, A, T = self.V, self.A, self.T
        pf = "E_"
        ar.reset()
        eps_t = ar.f32(1)
        self.eps_t = eps_t
        self.G(lambda e: e.memset(eps_t, LN_EPS), [], ["eps"])
        um_off = ar.off
        umT = ar.bf16(48 * NT).rearrange("p (c n) -> p c n", c=48)
        ar2 = Arena(ar.ap[:, um_off:ar.off])
        mark0 = ar.off
        xT = ar.bf16(16 * NT).rearrange("p (c n) -> p c n", c=16)
        stats = ar.f32(9 * 72).rearrange("p (t s) -> p t s", t=9)
        mark1 = ar.off
        stg = [(ar2.f32(D), f"{pf}xs{i}") for i in range(2)]
        self.load_xT(src, [(r0, rw, r0) for (r0, rw) in TILES], xT, pf + "xT", None, stg)
        wts = [(ar2.bf16(16 * 512).rearrange("p (c n) -> p c n", c=16), f"{pf}w{i}") for i in range(2)]
        bts = [(ar2.f32(512), f"{pf}bi{i}") for i in range(2)]
        ost = [(ar2.f32(512), f"{pf}o{i}") for i in range(4)]
        n = 0
        for g in range(12):
            c0 = 6144 + g * 512
            wt, wk = wts[g % 2]
            bt_, bk_ = bts[g % 2]
            self.ldc(wt, self.w_in_b[:, c0:c0 + 512].rearrange("(c p) n -> p c n", p=128), w=[wk])
            self.ld(bt_, self.b_in_b[0:1, c0:c0 + 512].partition_broadcast(128).rearrange("p o n -> p (o n)"), w=[bk_])
            for ti, (r0, rw) in enumerate(TILES):
                b = 2 + n % 6
                ob, ok = ost[n % 4]
                n += 1
                for c in range(16):
                    T(lambda e, b=b, c=c, r0=r0, rw=rw, wt=wt: e.matmul(ps[b][:rw, :], lhsT=xT[:, c, r0:r0 + rw], rhs=wt[:, c, :], start=(c == 0), stop=(c == 15)),
                      [wk, pf + "xT"], [psk[b]])
                V(lambda e, b=b, rw=rw, ob=ob, bt_=bt_: e.tensor_tensor(ob[:rw, :], ps[b][:rw, :], bt_[:rw, :], ALU.add), [psk[b], bk_], [ok])
                A(lambda e, rw=rw, ob=ob: e.activation(out=ob[:rw, :], in_=ob[:rw, :], func=AF.Gelu_apprx_tanh), [ok], [ok])
                V(lambda e, rw=rw, ob=ob, ti=ti, g=g: e.bn_stats(stats[:rw, ti, g * 6:(g + 1) * 6], ob[:rw, :]), [ok], [pf + "stats"])
                self.stq(sc["vraw"][r0:r0 + rw, g * 512:(g + 1) * 512], ob[:rw, :], [ok], ["s_vraw"])
        self.S.barrier()
        ar.off = mark1
        ar2.reset()
        vts = [(ar2.f32(6144), pf + "vt0"), (ar.f32(6144), pf + "vt1")]
        lg, lb = ar2.f32(6144), ar2.f32(6144)
        vnb = ar2.bf16(6144)
        mxs = [(ar.f32(768), f"{pf}mx{i}") for i in range(6)]
        wsf = ar.f32(1024).rearrange("p (g t) -> p g t", g=8)
        wsm = ar.bf16(1024).rearrange("p (g t) -> p g t", g=8)
        wsf_s = ar.f32(512).rearrange("p (g t) -> p g t", g=8)
        wsm_s = ar.bf16(512).rearrange("p (g t) -> p g t", g=8)
        bst, bss = ar.f32(8), ar.f32(8)
        self.ld(lg, self.lnv_g.partition_broadcast(128).rearrange("p o n -> p (o n)"), w=[pf + "lg"])
        self.ld(lb, self.lnv_b.partition_broadcast(128).rearrange("p o n -> p (o n)"), w=[pf + "lb"])
        self.ld(wsf, self.wsT, w=[pf + "wsf"])
        self.ld(wsf_s[:64], self.wsT_s, w=[pf + "wsfs"])
        self.ld(bst, self.b_s_t, w=[pf + "bst"])
        self.ld(bss[:64], self.b_s_s, w=[pf + "bss"])
        V(lambda e: e.tensor_tensor(wsm, wsf, self.tri128.unsqueeze(1).broadcast_to([128, 8, 128]), ALU.mult), [pf + "wsf"] + self.CK, [pf + "wsm"])
        V(lambda e: e.tensor_tensor(wsm_s[:64], wsf_s[:64], self.tri[:64, :].unsqueeze(1).broadcast_to([64, 8, 64]), ALU.mult), [pf + "wsfs"] + self.CK, [pf + "wsms"])
        n = 0
        vnb2 = [(vnb, pf + "vnb0"), (ar2.bf16(6144), pf + "vnb1")]
        sm2 = [(ar.f32(2), ar.f32(1), ar.f32(1), f"{pf}sm{i}") for i in range(2)]

        def e3_stage1(ti):
            r0, rw = TILES[ti]
            vt, vtk = vts[ti % 2]
            vb, vbk = vnb2[ti % 2]
            mv, rs, nmr, smk = sm2[ti % 2]
            self.ld(vt[:rw, :], sc["vraw"][r0:r0 + rw, :], w=[vtk] + [f"{vtk}_{c}" for c in range(4)])
            V(lambda e: e.bn_aggr(mv[:rw, :], stats[:rw, ti, :]), [pf + "stats"], [smk])
            A(lambda e: e.activation(out=rs[:rw, :], in_=mv[:rw, 1:2], func=AF.Sqrt, bias=eps_t[:rw, :], scale=1.0), [smk, "eps"], [smk])
            V(lambda e: e.reciprocal(rs[:rw, :], rs[:rw, :]), [smk], [smk])
            V(lambda e: e.scalar_tensor_tensor(out=nmr[:rw, :], in0=mv[:rw, 0:1], scalar=-1.0, in1=rs[:rw, :], op0=ALU.mult, op1=ALU.mult), [smk], [smk])
            for cb4 in range(4):
                cs_ = slice(cb4 * 1536, (cb4 + 1) * 1536)
                kq = f"{vtk}_{cb4}"
                A(lambda e, cs_=cs_: e.activation(out=vt[:rw, cs_], in_=vt[:rw, cs_], func=AF.Identity, bias=nmr[:rw, :], scale=rs[:rw, 0:1]), [vtk, smk], [kq])
            for cb4 in range(4):
                cs_ = slice(cb4 * 1536, (cb4 + 1) * 1536)
                kq = f"{vtk}_{cb4}"
                V(lambda e, cs_=cs_: e.tensor_tensor(vt[:rw, cs_], vt[:rw, cs_], lg[:rw, cs_], ALU.mult), [kq, pf + "lg"], [kq])
                V(lambda e, cs_=cs_: e.tensor_tensor(vt[:rw, cs_], vt[:rw, cs_], lb[:rw, cs_], ALU.add), [kq, pf + "lb"], [kq])
                A(lambda e, cs_=cs_: e.copy(vb[:rw, cs_], vt[:rw, cs_]), [kq], [f"{vbk}_{cb4}"])
            if rw == 64:
                self.stq(self.outs["vs"], vt[:64, :], [f"{vtk}_{c}" for c in range(4)], ["vs"], q="pool")

        def e3_stage2(ti):
            nonlocal n
            r0, rw = TILES[ti]
            samp = rw == 64
            vb, vbk = vnb2[ti % 2]
            wm = wsm_s if samp else wsm
            wmk = pf + ("wsms" if samp else "wsm")
            bs_ = bss if samp else bst
            bsk = pf + ("bss" if samp else "bst")
            for g8 in range(8):
                mx, mk_ = mxs[(ti * 8 + g8) % 6]
                for (c0, cn) in ((0, 512), (512, 256)):
                    b = 2 + n % 6
                    n += 1
                    T(lambda e, b=b, g8=g8, c0=c0, cn=cn: e.matmul(ps[b][:rw, :cn], lhsT=wm[:rw, g8, :rw], rhs=vb[:rw, g8 * 768 + c0:g8 * 768 + c0 + cn],
                                                                   start=True, stop=True), [wmk, f"{vbk}_{g8 // 2}"], [psk[b]])
                    if c0 == 0:
                        V(lambda e, b=b, g8=g8, c0=c0, cn=cn, mx=mx: e.tensor_scalar(mx[:rw, c0:c0 + cn], ps[b][:rw, :cn], bs_[:rw, g8:g8 + 1], None, ALU.add),
                          [psk[b], bsk], [mk_])
                    else:
                        A(lambda e, b=b, g8=g8, c0=c0, cn=cn, mx=mx: e.activation(out=mx[:rw, c0:c0 + cn], in_=ps[b][:rw, :cn], func=AF.Identity, bias=bs_[:rw, g8:g8 + 1], scale=1.0),
                          [psk[b], bsk], [mk_])
                self.stq(sc["mixed"][r0:r0 + rw, g8 * 768:(g8 + 1) * 768], mx[:rw, :], [mk_], ["s_mixed"], q="pool")

        e3_stage1(0)
        for ti in range(len(TILES)):
            if ti + 1 < len(TILES):
                e3_stage1(ti + 1)
            e3_stage2(ti)
        self.S.barrier()
        ar.off = mark1
        wt1s = [(ar.bf16(16 * 512).rearrange("p (c n) -> p c n", c=16), f"{pf}w1_{i}") for i in range(2)]
        bts = [(ar.f32(512), f"{pf}bj{i}") for i in range(2)]
        ust = [(ar.f32(512), f"{pf}u{i}") for i in range(3)]
        mst = [(ar.f32(512), f"{pf}m{i}") for i in range(3)]
        umb = [(ar.bf16(512), f"{pf}um{i}") for i in range(3)]
        n = 0
        pend = []
        for g in range(12):
            c0 = g * 512
            bt_, bk_ = bts[g % 2]
            wt1, w1k = wt1s[g % 2]
            self.ldc(wt1, self.w_in_b[:, c0:c0 + 512].rearrange("(c p) n -> p c n", p=128), w=[w1k])
            self.ld(bt_, self.b_in_b[0:1, c0:c0 + 512].partition_broadcast(128).rearrange("p o n -> p (o n)"), w=[bk_])
            for ti, (r0, rw) in enumerate(TILES):
                b = 2 + n % 6
                ub, uk = ust[n % 3]
                mb_, mk_ = mst[n % 3]
                qb, qk = umb[n % 3]
                n += 1
                self.ld(mb_[:rw, :], sc["mixed"][r0:r0 + rw, c0:c0 + 512], w=[mk_])
                for c in range(16):
                    T(lambda e, b=b, c=c, r0=r0, rw=rw, wt1=wt1: e.matmul(ps[b][:rw, :], lhsT=xT[:, c, r0:r0 + rw], rhs=wt1[:, c, :], start=(c == 0), stop=(c == 15)),
                      [w1k, pf + "xT"], [psk[b]])
                V(lambda e, b=b, rw=rw, ub=ub, bt_=bt_: e.tensor_tensor(ub[:rw, :], ps[b][:rw, :], bt_[:rw, :], ALU.add), [psk[b], bk_], [uk])
                A(lambda e, rw=rw, ub=ub: e.activation(out=ub[:rw, :], in_=ub[:rw, :], func=AF.Gelu_apprx_tanh), [uk], [uk])
                V(lambda e, rw=rw, ub=ub, mb_=mb_, qb=qb: e.tensor_tensor(qb[:rw, :], ub[:rw, :], mb_[:rw, :], ALU.mult), [uk, mk_], [qk])
                def tr_(tb=n % 2, rw=rw, qb=qb, qk=qk, g=g, r0=r0):
                    pt = ps[tb][:, 0:256].bitcast(BF16)
                    for j in range(4):
                        T(lambda e, pt=pt, j=j: e.transpose(pt[:, j * 128:j * 128 + rw], qb[:rw, j * 128:(j + 1) * 128], self.identb[:rw, :rw]),
                          [qk] + self.CK, [psk[tb]])
                    srcv = pt.rearrange("p (a b) -> p a b", a=4)[:, :, :rw]
                    dstv = umT[:, g * 4:(g + 1) * 4, r0:r0 + rw]
                    A(lambda e, d=dstv, s=srcv: e.copy(d, s), [psk[tb]], [pf + "umT"])
                pend.append(tr_)
                if len(pend) > 1:
                    pend.pop(0)()
        while pend:
            pend.pop(0)()
        self.S.barrier()
        ar.off = mark0
        wo = [(ar.bf16(48 * 256).rearrange("p (c n) -> p c n", c=48), f"{pf}wo{i}") for i in range(2)]
        yst = [(ar.f32(256), f"{pf}y{i}") for i in range(4)]
        n = 0
        for cg in range(8):
            wt, wk = wo[cg % 2]
            for k3 in range(3):
                self.ldc(wt[:, k3 * 16:(k3 + 1) * 16, :], self.w_out_b[k3 * 2048:(k3 + 1) * 2048, cg * 256:(cg + 1) * 256].rearrange("(c p) n -> p c n", p=128), w=[wk])
            for ti, (r0, rw) in enumerate(TILES):
                b = 2 + n % 6
                yb, yk = yst[n % 4]
                n += 1
                for c in range(48):
                    T(lambda e, b=b, c=c, r0=r0, rw=rw, wt=wt: e.matmul(ps[b][:rw, 0:256], lhsT=umT[:, c, r0:r0 + rw], rhs=wt[:, c, :], start=(c == 0), stop=(c == 47)),
                      [wk, pf + "umT"], [psk[b]])
                if n % 2:
                    V(lambda e, b=b, rw=rw, yb=yb: e.tensor_copy(yb[:rw, :], ps[b][:rw, 0:256]), [psk[b]], [yk])
                else:
                    A(lambda e, b=b, rw=rw, yb=yb: e.copy(yb[:rw, :], ps[b][:rw, 0:256]), [psk[b]], [yk])
                self.stq(sc["ymix"][r0:r0 + rw, cg * 256:(cg + 1) * 256], yb[:rw, :], [yk], ["s_ymix"])
        self.S.barrier()
        ar.reset()
        eps_t = ar.f32(1)
        self.eps_t = eps_t
        self.G(lambda e: e.memset(eps_t, LN_EPS), [], ["eps"])
        gt, bt = ar.f32(D), ar.f32(D)
        self.ld(gt, self.ln["ln_mix_g"][1:2, :].partition_broadcast(128).rearrange("p o n -> p (o n)"), w=[pf + "g"])
        self.ld(bt, self.ln["ln_mix_b"][1:2, :].partition_broadcast(128).rearrange("p o n -> p (o n)"), w=[pf + "b"])
        xr = [(ar.f32(D), f"{pf}xr{i}") for i in range(2)]
        yr = [(ar.f32(D), f"{pf}yr{i}") for i in range(2)]
        scr = (ar.f32(24), ar.f32(2), ar.f32(2))
        for ti, (r0, rw) in enumerate(TILES):
            xb, xk = xr[ti % 2]
            yb, yk = yr[ti % 2]
            self.ld(xb[:rw, :], src[r0:r0 + rw, :], w=[xk])
            self.ld(yb[:rw, :], sc["ymix"][r0:r0 + rw, :], w=[yk])
            V(lambda e, rw=rw, xb=xb, yb=yb: e.scalar_tensor_tensor(out=yb[:rw, :], in0=xb[:rw, :], scalar=ALPHA, in1=yb[:rw, :], op0=ALU.mult, op1=ALU.add), [xk, yk], [yk])
            self.layer_norm_rows(yb, rw, pf + "g", pf + "b", gt, bt, yb, [yk], yk, scr)
            self.stq(dst[r0:r0 + rw, :], yb[:rw, :], [yk], [self.dname(dst)], q="pool")


    def phase_peer_dense(self, src, li, dst, pf):
        ar, ps, psk = self.ar, self.ps, self.psk
        V, A, T = self.V, self.A, self.T
        NTP = 1152
        G_scr = self.sc["G"]
        ar.reset()
        eps_t = ar.f32(1)
        self.eps_t = eps_t
        self.G(lambda e: e.memset(eps_t, LN_EPS), [], ["eps"])
        xT = ar.bf16(16 * NTP).rearrange("p (c n) -> p c n", c=16)
        xTk = pf + "xT"
        V(lambda e: e.memset(xT[:, :, NT:NTP], 0.0), [], [xTk])
        markP = ar.off
        qT = ar.bf16(16 * NT).rearrange("p (c n) -> p c n", c=16)
        skt = ar.bf16(16 * 128).rearrange("p (c n) -> p c n", c=16)
        self.ldc(skt, self.skT[li], w=[pf + "sk"])
        mark = ar.off
        stg = [(ar.f32(D), f"{pf}xs{i}") for i in range(2)]
        wq = ar.bf16(16 * D).rearrange("p (c n) -> p c n", c=16)
        for q4 in range(4):
            self.ldc(wq[:, :, q4 * 512:(q4 + 1) * 512], self.peer_wq[li, :, q4 * 512:(q4 + 1) * 512].rearrange("(c p) n -> p c n", p=128), w=[f"{pf}wq{q4}"])
        self.load_xT(src, [(r0, rw, r0) for (r0, rw) in TILES], xT, xTk, None, stg)
        n = 0
        for cg in range(16):
            for (t0, tn) in ((0, 512), (512, 512), (1024, 64)):
                b = 2 + n % 6
                n += 1
                for c in range(16):
                    T(lambda e, b=b, c=c, cg=cg, t0=t0, tn=tn: e.matmul(ps[b][:, :tn], lhsT=wq[:, c, cg * 128:(cg + 1) * 128], rhs=xT[:, c, t0:t0 + tn],
                                                                       start=(c == 0), stop=(c == 15)), [f"{pf}wq{cg // 4}", xTk], [psk[b]])
                if n % 2:
                    V(lambda e, b=b, cg=cg, t0=t0, tn=tn: e.tensor_copy(qT[:, cg, t0:t0 + tn], ps[b][:, :tn]), [psk[b]], [pf + "qT"])
                else:
                    A(lambda e, b=b, cg=cg, t0=t0, tn=tn: e.copy(qT[:, cg, t0:t0 + tn], ps[b][:, :tn]), [psk[b]], [pf + "qT"])
        self.S.barrier()
        ar.off = mark
        Sc = ar.f32(2048).rearrange("p (c n) -> p c n", c=16)
        Wk = ar.f32(2048).rearrange("p (c n) -> p c n", c=16)
        sv = ar.f32(256).rearrange("p (c n) -> p c n", c=16)
        si = ar.u32(256).rearrange("p (c n) -> p c n", c=16)
        sif = ar.f32(256).rearrange("p (h t n) -> p h t n", h=8, t=2)
        cand = ar.f32(2048).rearrange("p (h n) -> p h n", h=8)
        cw = ar.f32(2048).rearrange("p (h n) -> p h n", h=8)
        cs = ar.f32(128).rearrange("p (h n) -> p h n", h=8)
        cp = ar.u32(128).rearrange("p (h n) -> p h n", h=8)
        cif = ar.f32(128).rearrange("p (h n) -> p h n", h=8)
        cjf = ar.f32(128).rearrange("p (h n) -> p h n", h=8)
        oh = ar.f32(2048).rearrange("p (h k n) -> p h k n", h=8, k=16)
        n0 = ar.f32(128).rearrange("p (h n) -> p h n", h=8)
        n1 = ar.f32(128).rearrange("p (h n) -> p h n", h=8)
        cpf = ar.f32(128).rearrange("p (h n) -> p h n", h=8)
        gwt = ar.f32(128)
        zz = ar.f32(16)
        itT = ar.f32(384).rearrange("p (a n) -> p a n", a=3)
        itB = ar.bf16(384).rearrange("p (a n) -> p a n", a=3)
        io128b = ar.bf16(128)
        V(lambda e: e.tensor_copy(io128b, self.iota128), self.CK, [pf + "io128b"])
        P0 = ar.bf16(64 * 128).rearrange("p (t n) -> p t n", t=64)
        P1 = ar.bf16(64 * 128).rearrange("p (t n) -> p t n", t=64)
        Gbuf = ar.bf16(128 * 128).rearrange("p (i t) -> p i t", i=128)
        sv4 = sv.rearrange("p (h t) n -> p h t n", t=2)
        V(lambda e: e.memset(Gbuf, 0.0), [], [pf + "Gbuf"])
        ectr = [0]

        def scores(ti):
            r0, rw = TILES[ti]
            for g4 in range(4):
                b = 2 + (ti * 4 + g4) % 6
                for j in range(4):
                    cg = g4 * 4 + j
                    T(lambda e, b=b, j=j, cg=cg, r0=r0, rw=rw: e.matmul(ps[b][:rw, j * 128:(j + 1) * 128], lhsT=qT[:, cg, r0:r0 + rw], rhs=skt[:, cg, :], start=True, stop=True),
                      [pf + "qT", pf + "sk"], [psk[b]])
                dstv = Sc[:rw, g4 * 4:(g4 + 1) * 4, :]
                srcv = ps[b][:rw, :].rearrange("p (a n) -> p a n", a=4)
                wkeys = [f"{pf}S{g4 * 4 + j}" for j in range(4)]
                A(lambda e, d=dstv, s=srcv: e.copy(d, s), [psk[b]], wkeys)

        scores(0)
        for ti, (r0, rw) in enumerate(TILES):
            for cg in range(16):
                V(lambda e, cg=cg, rw=rw: e.max(out=sv[:rw, cg, 0:8], in_=Sc[:rw, cg, :]), [f"{pf}S{cg}"], [f"{pf}sv{cg}"])
            for cg in range(16):
                V(lambda e, cg=cg, rw=rw: e.max_index(out=si[:rw, cg, 0:8], in_max=sv[:rw, cg, 0:8], in_values=Sc[:rw, cg, :]), [f"{pf}S{cg}", f"{pf}sv{cg}"], [f"{pf}si{cg}"])
            for cg in range(16):
                V(lambda e, cg=cg, rw=rw: e.match_replace(out=Wk[:rw, cg, :], in_to_replace=sv[:rw, cg, 0:8], in_values=Sc[:rw, cg, :], imm_value=NEG),
                  [f"{pf}S{cg}", f"{pf}sv{cg}"], [f"{pf}W{cg}"])
            for cg in range(16):
                V(lambda e, cg=cg, rw=rw: e.max(out=sv[:rw, cg, 8:16], in_=Wk[:rw, cg, :]), [f"{pf}W{cg}"], [f"{pf}sv{cg}"])
            for cg in range(16):
                V(lambda e, cg=cg, rw=rw: e.max_index(out=si[:rw, cg, 8:16], in_max=sv[:rw, cg, 8:16], in_values=Wk[:rw, cg, :]), [f"{pf}W{cg}", f"{pf}sv{cg}"], [f"{pf}si{cg}"])
            svk = [f"{pf}sv{cg}" for cg in range(16)]
            sik = [f"{pf}si{cg}" for cg in range(16)]
            V(lambda e, rw=rw: e.tensor_copy(sif[:rw].rearrange("p h t n -> p (h t) n"), si[:rw]), sik, [pf + "sif"])
            V(lambda e, rw=rw: e.tensor_tensor(cand[:rw].rearrange("p h (i j) -> p h i j", i=16),
                                               sv4[:rw, :, 0, :].unsqueeze(3).broadcast_to([rw, 8, 16, 16]),
                                               sv4[:rw, :, 1, :].unsqueeze(2).broadcast_to([rw, 8, 16, 16]), ALU.add), svk, [pf + "cand"])
            ck = [f"{pf}c{h}" for h in range(8)]
            for h in range(8):
                V(lambda e, h=h, rw=rw: e.max(out=cs[:rw, h, 0:8], in_=cand[:rw, h, :]), [pf + "cand"], [ck[h] + "s"])
            for h in range(8):
                V(lambda e, h=h, rw=rw: e.max_index(out=cp[:rw, h, 0:8], in_max=cs[:rw, h, 0:8], in_values=cand[:rw, h, :]), [pf + "cand", ck[h] + "s"], [ck[h] + "p"])
            for h in range(8):
                V(lambda e, h=h, rw=rw: e.match_replace(out=cw[:rw, h, :], in_to_replace=cs[:rw, h, 0:8], in_values=cand[:rw, h, :], imm_value=NEG),
                  [pf + "cand", ck[h] + "s"], [ck[h] + "w"])
            for h in range(8):
                V(lambda e, h=h, rw=rw: e.max(out=cs[:rw, h, 8:16], in_=cw[:rw, h, :]), [ck[h] + "w"], [ck[h] + "s"])
            for h in range(8):
                V(lambda e, h=h, rw=rw: e.max_index(out=cp[:rw, h, 8:16], in_max=cs[:rw, h, 8:16], in_values=cw[:rw, h, :]), [ck[h] + "w", ck[h] + "s"], [ck[h] + "p"])
            csk = [c + "s" for c in ck]
            cpk = [c + "p" for c in ck]
            V(lambda e, rw=rw: e.tensor_copy(cpf[:rw], cp[:rw]), cpk, [pf + "cpf"])
            th4 = self.thr16[:rw, :].unsqueeze(1).unsqueeze(1).broadcast_to([rw, 8, 16, 16])
            V(lambda e, rw=rw, th4=th4: e.tensor_tensor(oh[:rw], cpf[:rw].unsqueeze(3).broadcast_to([rw, 8, 16, 16]), th4, ALU.is_ge), [pf + "cpf"] + self.CK, [pf + "oh"])
            V(lambda e, rw=rw: e.tensor_reduce(cif[:rw], oh[:rw], AX.X, ALU.add), [pf + "oh"], [pf + "cif"])
            V(lambda e, rw=rw: e.scalar_tensor_tensor(out=cjf[:rw], in0=cif[:rw], scalar=-16.0, in1=cpf[:rw], op0=ALU.mult, op1=ALU.add), [pf + "cif", pf + "cpf"], [pf + "cjf"])
            io4 = self.iota16[:rw, :].unsqueeze(1).unsqueeze(1).broadcast_to([rw, 8, 16, 16])
            for (cf, half, nn, kk) in ((cif, 0, n0, "n0"), (cjf, 1, n1, "n1")):
                V(lambda e, cf=cf, rw=rw, io4=io4: e.tensor_tensor(oh[:rw], cf[:rw].unsqueeze(3).broadcast_to([rw, 8, 16, 16]), io4, ALU.is_equal),
                  [pf + "cif", pf + "cjf"] + self.CK, [pf + "oh"])
                V(lambda e, half=half, rw=rw: e.tensor_tensor(oh[:rw], oh[:rw], sif[:rw, :, half, :].unsqueeze(2).broadcast_to([rw, 8, 16, 16]), ALU.mult),
                  [pf + "oh", pf + "sif"], [pf + "oh"])
                V(lambda e, nn=nn, rw=rw: e.tensor_reduce(nn[:rw], oh[:rw], AX.X, ALU.add), [pf + "oh"], [pf + kk])
            g3 = gwt[:rw, :].rearrange("p (h n) -> p h n", h=8)
            gk = pf + "gw"
            V(lambda e, rw=rw, g3=g3: e.tensor_tensor(g3, cs[:rw], cs[:rw, :, 0:1].broadcast_to([rw, 8, 16]), ALU.subtract), csk, [gk])
            A(lambda e, g3=g3: e.activation(out=g3, in_=g3, func=AF.Exp), [gk], [gk])
            V(lambda e, rw=rw, g3=g3: e.tensor_reduce(zz[:rw, 0:8], g3, AX.X, ALU.add), [gk], [pf + "zz"])
            V(lambda e, rw=rw: e.reciprocal(zz[:rw, 0:8], zz[:rw, 0:8]), [pf + "zz"], [pf + "zz"])
            V(lambda e, rw=rw, g3=g3: e.tensor_tensor(g3, g3, zz[:rw, 0:8].unsqueeze(2).broadcast_to([rw, 8, 16]), ALU.mult), [gk, pf + "zz"], [gk])
            if ti + 1 < len(TILES):
                scores(ti + 1)
            for a_, (srcap, kk) in enumerate(((n0, pf + "n0"), (n1, pf + "n1"), (None, gk))):
                sap = gwt[:rw, :] if srcap is None else srcap[:rw].rearrange("p h n -> p (h n)")
                T(lambda e, a_=a_, sap=sap, rw=rw: e.transpose(ps[0][:, a_ * 128:a_ * 128 + rw], sap, self.ident[:rw, :rw]), [kk] + self.CK, [f"{pf}psT{a_}"])
            V(lambda e, rw=rw: e.tensor_copy(itT[:, :, :rw], ps[0][:, 0:384].rearrange("p (a n) -> p a n", a=3)[:, :, :rw]),
              [f"{pf}psT{a_}" for a_ in range(3)], [pf + "itT"])
            for qt in range(rw // 32):
                pb_ = (ti * 4 + qt) % 2
                P0q, P1q = P0[:, pb_ * 32:(pb_ + 1) * 32, :], P1[:, pb_ * 32:(pb_ + 1) * 32, :]
                k0, k1 = f"{pf}P0_{pb_}", f"{pf}P1_{pb_}"
                for tt in range(32):
                    tg = qt * 32 + tt
                    V(lambda e, P0q=P0q, tt=tt, tg=tg: e.tensor_scalar(P0q[:, tt, :], io128b, itT[:, 0, tg:tg + 1], itT[:, 2, tg:tg + 1], ALU.is_equal, ALU.mult),
                      [pf + "itT", pf + "io128b"], [k0])
                    V(lambda e, P1q=P1q, tt=tt, tg=tg: e.tensor_scalar(P1q[:, tt, :], io128b, itT[:, 1, tg:tg + 1], None, ALU.is_equal),
                      [pf + "itT", pf + "io128b"], [k1])
                for t4 in range(8):
                    b = 2 + ectr[0] % 6
                    ectr[0] += 1
                    for k in range(4):
                        tt = t4 * 4 + k
                        T(lambda e, b=b, k=k, tt=tt, P0q=P0q, P1q=P1q: e.matmul(ps[b][:, k * 128:(k + 1) * 128], lhsT=P0q[:, tt, :], rhs=P1q[:, tt, :], start=True, stop=True),
                          [k0, k1], [psk[b]])
                    tg = qt * 32 + t4 * 4
                    A(lambda e, b=b, tg=tg: e.copy(Gbuf[:, :, tg:tg + 4], ps[b][:, :].rearrange("p (t i) -> p i t", t=4)), [psk[b]], [pf + "Gbuf"])
            self.stq(G_scr[ti].rearrange("p c t -> p (c t)"), Gbuf.rearrange("p i t -> p (i t)"), [pf + "Gbuf"], ["s_G"])
        self.S.barrier()
        ar.off = markP
        TB = ((0, 512), (512, 512), (1024, 128))
        acc = ar.f32(9 * D).rearrange("p (a n) -> p a n", a=9)
        UcTs = [(ar.bf16(16 * 128).rearrange("p (c n) -> p c n", c=16), f"{pf}U{i}") for i in range(3)]
        Vg = [(ar.bf16(4 * D).rearrange("p (j n) -> p j n", j=4), f"{pf}V{i}") for i in range(2)]
        Gg = [(ar.bf16(9 * 4 * 128).rearrange("p (a j t) -> p a j t", a=9, j=4), f"{pf}G{i}") for i in range(2)]
        Wg = [(ar.bf16(4 * NTP).rearrange("p (j n) -> p j n", j=4), f"{pf}W{i}") for i in range(2)]
        tmps = [(ar.bf16(512), f"{pf}tmp{i}") for i in range(3)]
        uT = self.peer_u
        v4 = self.peer_v.rearrange("(l i c) d -> l i c d", l=2, c=128)
        vctr = [0]
        actr = [0]

        def vside(grp):
            vg, vgk = Vg[grp % 2]
            wg, wgk = Wg[grp % 2]
            for ti in range(9):
                rw = TILES[ti][1]
                for db in range(4):
                    b = 5 + vctr[0] % 3
                    vctr[0] += 1
                    for j in range(4):
                        T(lambda e, b=b, j=j, ti=ti, db=db, wg=wg, vg=vg: e.matmul(ps[b][:, :], lhsT=wg[:, j, ti * 128:(ti + 1) * 128], rhs=vg[:, j, db * 512:(db + 1) * 512],
                                                                                   start=(j == 0), stop=(j == 3)), [wgk, vgk], [psk[b]])
                    ak = f"{pf}acc{ti}_{db}"
                    if grp == 0:
                        V(lambda e, b=b, ti=ti, db=db, rw=rw: e.tensor_copy(acc[:rw, ti, db * 512:(db + 1) * 512], ps[b][:rw, :]), [psk[b]], [ak])
                    else:
                        V(lambda e, b=b, ti=ti, db=db, rw=rw: e.tensor_tensor(acc[:rw, ti, db * 512:(db + 1) * 512], acc[:rw, ti, db * 512:(db + 1) * 512], ps[b][:rw, :], ALU.add),
                          [psk[b], ak], [ak])

        for grp in range(32):
            gg, ggk = Gg[grp % 2]
            vg, vgk = Vg[grp % 2]
            wg, wgk = Wg[grp % 2]
            self.ld(gg, G_scr[:, :, grp * 4:(grp + 1) * 4, :].rearrange("a p c t -> p a c t"), w=[ggk])
            self.ldc(vg, v4[li, :, grp * 4:(grp + 1) * 4, :], w=[vgk])
            for j in range(4):
                c = grp * 4 + j
                ut, utk = UcTs[c % 3]
                row0 = (li * 128 + c) * 2048
                self.ldc(ut, uT[row0:row0 + 2048, :].rearrange("(dc p) i -> p dc i", p=128), w=[utk])
                for bi, (t0, tn) in enumerate(TB):
                    b = actr[0] % 5
                    tmp, tmk = tmps[actr[0] % 3]
                    actr[0] += 1
                    for dc in range(16):
                        T(lambda e, b=b, dc=dc, t0=t0, tn=tn, ut=ut: e.matmul(ps[b][:, :tn], lhsT=ut[:, dc, :], rhs=xT[:, dc, t0:t0 + tn], start=(dc == 0), stop=(dc == 15)),
                          [utk, xTk], [psk[b]])
                    A(lambda e, b=b, tn=tn, tmp=tmp: e.activation(out=tmp[:, :tn], in_=ps[b][:, :tn], func=AF.Gelu_apprx_tanh), [psk[b]], [tmk])
                    a0, a1 = t0 // 128, (t0 + tn) // 128
                    V(lambda e, wg=wg, gg=gg, j=j, t0=t0, tn=tn, a0=a0, a1=a1, tmp=tmp: e.tensor_tensor(
                        wg[:, j, t0:t0 + tn].rearrange("p (a t) -> p a t", t=128), tmp[:, :tn].rearrange("p (a t) -> p a t", t=128),
                        gg[:, a0:a1, j, :], ALU.mult), [tmk, ggk], [wgk])
                if j == 0 and grp > 0:
                    vside(grp - 1)
        vside(31)
        self.S.barrier()
        ar.off = ar.off - 0
        ar2 = Arena(Vg[0][0].rearrange("p j n -> p (j n)").bitcast(F32))
        gt, bt = ar2.f32(D), None
        ar3 = Arena(Vg[1][0].rearrange("p j n -> p (j n)").bitcast(F32))
        bt = ar3.f32(D)
        ar4 = Arena(Wg[0][0].rearrange("p j n -> p (j n)").bitcast(F32))
        xr = [(ar4.f32(D), f"{pf}xr0")]
        ar5 = Arena(Wg[1][0].rearrange("p j n -> p (j n)").bitcast(F32))
        xr.append((ar5.f32(D), f"{pf}xr1"))
        ar6 = Arena(Gg[0][0].rearrange("p a j t -> p (a j t)").bitcast(F32))
        scr = (ar6.f32(24), ar6.f32(2), ar6.f32(2))
        self.ld(gt, self.ln["ln_ffn_g"][li:li + 1, :].partition_broadcast(128).rearrange("p o n -> p (o n)"), w=[pf + "g"])
        self.ld(bt, self.ln["ln_ffn_b"][li:li + 1, :].partition_broadcast(128).rearrange("p o n -> p (o n)"), w=[pf + "b"])
        for ti, (r0, rw) in enumerate(TILES):
            xb, xk = xr[ti % 2]
            aks = [f"{pf}acc{ti}_{db}" for db in range(4)]
            self.ld(xb[:rw, :], src[r0:r0 + rw, :], w=[xk])
            V(lambda e, rw=rw, xb=xb, ti=ti: e.scalar_tensor_tensor(out=xb[:rw, :], in0=xb[:rw, :], scalar=ALPHA, in1=acc[:rw, ti, :], op0=ALU.mult, op1=ALU.add), [xk] + aks, [xk])
            self.layer_norm_rows(xb, rw, pf + "g", pf + "b", gt, bt, xb, [xk], xk, scr)
            self.stq(dst[r0:r0 + rw, :], xb[:rw, :], [xk], [self.dname(dst)], q="pool")


def make_in_maps(inp):
    f = np.float32
    xp, xs = inp["x_prompt"], inp["x_sample"]
    maps = []
    ws = inp["w_s_b"][0]
    wsT = np.ascontiguousarray(ws.transpose(2, 0, 1))
    wsT_s = np.zeros((64, 8, 64), f)
    for b in range(16):
        wsT_s[b * 4:(b + 1) * 4, :, b * 4:(b + 1) * 4] = ws[:, 0:4, 0:4].transpose(2, 0, 1)
    b_s = inp["b_s_b"][0]
    b_s_t = np.ascontiguousarray(b_s.T)
    b_s_s = np.ascontiguousarray(np.tile(b_s[:, 0:4].T, (16, 1)))
    skT = np.ascontiguousarray(inp["peer_sub_keys"].reshape(2, 16, 128, 128).transpose(0, 3, 1, 2))
    shared = dict(
        consts=CONST_PACK,
        w_in_a=inp["w_in_a"][0], b_gate=inp["b_gate_a"], hn_gain=inp["hn_gain_a"].reshape(1, D),
        w_out_a=inp["w_out_a"][0], ln_mix_g=inp["ln_mix_g"], ln_mix_b=inp["ln_mix_b"],
        ln_ffn_g=inp["ln_ffn_g"], ln_ffn_b=inp["ln_ffn_b"],
        w_in_b=inp["w_in_b"][0], b_in_b=inp["b_in_b"], lnv_g=inp["lnv_g_b"], lnv_b=inp["lnv_b_b"],
        wsT=wsT, wsT_s=wsT_s, b_s_t=b_s_t, b_s_s=b_s_s, w_out_b=inp["w_out_b"][0],
        peer_wq=inp["peer_w_q"], skT=skT,
        peer_u=np.ascontiguousarray(inp["peer_u"].reshape(2, 128, 128, D).transpose(0, 2, 3, 1)).reshape(2 * 128 * D, 128),
        peer_v=inp["peer_v"].reshape(2 * 16384, D),
    )
    for c in range(NCORES):
        b, half = c // 2, c % 2
        m = dict(shared)
        m["x_own"] = np.concatenate([xp[b, half * 1024:(half + 1) * 1024], xs[16 * c:16 * c + 16].reshape(64, D)], axis=0)
        m["x_prev"] = np.ascontiguousarray(xp[b, (1 - half) * 1024:(2 - half) * 1024])
        m["flag"] = np.full((128, 1), float(half), f)
        m["C0s"] = np.ascontiguousarray(inp["state_mlstm_C"][0, 16 * c:16 * c + 16].reshape(16 * 8 * 128, 256))
        m["n0sT"] = np.ascontiguousarray(inp["state_mlstm_n"][0, 16 * c:16 * c + 16].reshape(128, 128).T)
        m["m0sT"] = np.ascontiguousarray(inp["state_mlstm_m"][0, 16 * c:16 * c + 16].T)
        maps.append(m)
    return maps


def assemble(results):
    f = np.float32
    y_p = np.zeros((4, 2048, D), f)
    y_s = np.zeros((128, 4, D), f)
    C_p = np.zeros((1, 4, 8, 128, 256), f)
    n_p = np.zeros((1, 4, 8, 128), f)
    m_p = np.zeros((1, 4, 8), f)
    C_s = np.zeros((1, 128, 8, 128, 256), f)
    n_s = np.zeros((1, 128, 8, 128), f)
    m_s = np.zeros((1, 128, 8), f)
    v_s = np.zeros((1, 128, 4, 6144), f)
    for c in range(NCORES):
        r = results[c]
        b, half = c // 2, c % 2
        y_p[b, half * 1024:(half + 1) * 1024] = r["y_own"][:1024]
        y_s[16 * c:16 * c + 16] = r["y_own"][1024:].reshape(16, 4, D)
        if half == 1:
            C_p[0, b] = r["Cp"].reshape(8, 128, 256)
            n_p[0, b] = r["npT"].T
            m_p[0, b] = r["mpT"][:, 0]
        C_s[0, 16 * c:16 * c + 16] = r["Cs"].reshape(16, 8, 128, 256)
        n_s[0, 16 * c:16 * c + 16] = r["nsT"].T.reshape(16, 8, 128)
        m_s[0, 16 * c:16 * c + 16] = r["msT"].T
        v_s[0, 16 * c:16 * c + 16] = r["vs"].reshape(16, 4, 6144)
    return (y_p, y_s, C_p, n_p, m_p, C_s, n_s, m_s, v_s)


_NC_CACHE = {}


def kernel(**inputs):
    inp = {k: np.asarray(v) for k, v in inputs.items()}
    if "nc" not in _NC_CACHE:
        _NC_CACHE["nc"] = Builder(debug=False).build()
    nc = _NC_CACHE["nc"]
    maps = make_in_maps(inp)
    res = run_bass_kernel_spmd(nc, maps, core_ids=list(range(NCORES)))
    return assemble(res.results)
```

```python
import contextlib
import numpy as np
import concourse.bass as bass
import concourse.mybir as mybir
from concourse.bass_utils import run_bass_kernel_spmd

F32 = mybir.dt.float32
BF16 = mybir.dt.bfloat16
I32 = mybir.dt.int32
U32 = mybir.dt.uint32
AF = mybir.ActivationFunctionType
ALU = mybir.AluOpType
AX = mybir.AxisListType

NCORES = 8
D = 2048
NT = 1088
NPV = 1024
NALL = NT + NPV
ALPHA = float(4 ** 0.25)
LN_EPS = 1e-5
NEG = -1.0e30
TILES = [(i * 128, 128) for i in range(8)] + [(1024, 64)]
PTILES = [(NT + i * 128, 128) for i in range(8)]


class Op:
    __slots__ = ("eng", "fn", "reads", "writes", "is_dma", "deps", "signal", "sig", "semkey", "idx")

    def __init__(self, eng, fn, reads, writes, is_dma):
        self.eng = eng
        self.fn = fn
        self.reads = tuple(reads)
        self.writes = tuple(writes)
        self.is_dma = is_dma
        self.deps = []
        self.signal = False
        self.sig = None
        self.semkey = None


class Sched:
    ENGS = ("pe", "act", "dve", "pool", "sp")
    SEM_WRAP = 20000

    def __init__(self, nc):
        self.nc = nc
        self.ops = []
        self.allkeys = set()

    def op(self, eng, fn, reads=(), writes=()):
        o = Op(eng, fn, reads, writes, False)
        self.ops.append(o)
        self.allkeys.update(o.reads)
        self.allkeys.update(o.writes)
        return o

    def dma(self, eng, fn, reads=(), writes=(), semkey=None):
        o = Op(eng, fn, reads, writes, True)
        o.semkey = semkey if semkey is not None else o.writes[0]
        self.ops.append(o)
        self.allkeys.update(o.reads)
        self.allkeys.update(o.writes)
        return o

    def barrier(self):
        self.ops.append("BARRIER")
        self.op("sp", lambda e: e.nop(), reads=(), writes=tuple(self.allkeys) + ("__bar",))
        for e in ("pe", "act", "dve", "pool"):
            self.op(e, None, reads=("__bar",))

    def finalize(self):
        nc = self.nc
        last_w = {}
        readers = {}
        phase = 0
        ops2 = []
        self.dma_slot = {}
        slots_in_phase = {}
        for o in self.ops:
            if isinstance(o, str):
                phase += 1
                slots_in_phase = {}
                continue
            if o.is_dma:
                cls = "sw" if o.eng == "pool" else "hw"
                sk = (cls, o.semkey)
                if sk not in slots_in_phase:
                    slots_in_phase[sk] = (cls, sum(1 for k in slots_in_phase if k[0] == cls))
                self.dma_slot[id(o)] = slots_in_phase[sk]
            ops2.append(o)
        self.ops = ops2
        for i, o in enumerate(self.ops):
            o.idx = i
            deps = {}
            for k in o.reads:
                for p in last_w.get(k, ()):
                    deps[p.idx] = p
            for k in o.writes:
                grp = last_w.get(k, ())
                rd = readers.get(k)
                join = o.is_dma and grp and all(p.is_dma for p in grp) and not rd
                if not join:
                    for p in grp:
                        deps[p.idx] = p
                    for r in rd or ():
                        deps[r.idx] = r
            deps.pop(i, None)
            for p in deps.values():
                if p.is_dma:
                    o.deps.append(p)
                elif p.eng == "pe" and o.eng == "pe" and not o.is_dma:
                    continue
                else:
                    if p.fn is None:
                        raise RuntimeError("dependency on a wait-only op")
                    p.signal = True
                    o.deps.append(p)
            for k in o.reads:
                if o.fn is not None:
                    readers.setdefault(k, []).append(o)
            for k in o.writes:
                grp = last_w.get(k, ())
                rd = readers.get(k)
                if o.is_dma and grp and all(p.is_dma for p in grp) and not rd:
                    last_w[k] = [p for p in grp if p.semkey != o.semkey] + [o]
                else:
                    last_w[k] = [o]
                readers[k] = []
        self._stack = contextlib.ExitStack()
        sem_ctr = [0]

        def newsem(name):
            sem_ctr[0] += 1
            return self._stack.enter_context(nc.semaphore(f"{name}{sem_ctr[0]}"))

        eng_sem, eng_cnt, dma_sem, dma_cnt = {}, {}, {}, {}
        for o in self.ops:
            if o.is_dma:
                k = self.dma_slot[id(o)]
                if k not in dma_sem:
                    dma_sem[k] = newsem("d")
                    dma_cnt[k] = 0
                dma_cnt[k] += 16
                o.sig = (dma_sem[k], dma_cnt[k])
            elif o.signal:
                e = o.eng
                if e not in eng_sem or eng_cnt[e] >= self.SEM_WRAP:
                    eng_sem[e] = newsem("e")
                    eng_cnt[e] = 0
                eng_cnt[e] += 1
                o.sig = (eng_sem[e], eng_cnt[e])
        self.n_sems = sem_ctr[0]
        per = {e: [o for o in self.ops if o.eng == e] for e in self.ENGS}
        handles = {"pe": "tensor", "act": "scalar", "dve": "vector", "pool": "gpsimd", "sp": "sync"}

        def emit_stream(eng, h):
            waited = {}
            for o in per[eng]:
                need = {}
                for p in o.deps:
                    s, v = p.sig
                    key = id(s)
                    if key not in need or need[key][1] < v:
                        need[key] = (s, v)
                for key, (s, v) in need.items():
                    if waited.get(key, 0) >= v:
                        continue
                    h.wait_ge(s, v)
                    waited[key] = v
                if o.fn is not None:
                    ins = o.fn(h)
                    if o.sig is not None:
                        ins.then_inc(o.sig[0], 16 if o.is_dma else 1)

        with nc.Block() as block:
            for e in self.ENGS:
                if not per[e]:
                    continue

                def mk(e):
                    def f(h):
                        emit_stream(e, h)
                    return f
                getattr(block, handles[e])(mk(e))
        self._stack.close()


class Arena:
    def __init__(self, ap):
        self.ap = ap
        self.cap = ap.shape[1]
        self.off = 0

    def reset(self):
        self.off = 0

    def _take(self, nwords):
        assert self.off + nwords <= self.cap, f"arena overflow {self.off}+{nwords}>{self.cap}"
        a = self.ap[:, self.off:self.off + nwords]
        self.off += nwords
        return a

    def f32(self, n):
        return self._take(n)

    def i32(self, n):
        return self._take(n).bitcast(I32)

    def u32(self, n):
        return self._take(n).bitcast(U32)

    def bf16(self, n):
        return self._take((n + 1) // 2).bitcast(BF16)[:, :n]


def _consts():
    c = {}
    c["ident"] = np.eye(128, dtype=np.float32)
    s = np.arange(64)
    tri = (s[:, None] <= s[None, :]).astype(np.float32)
    same = (s[:, None] // 4 == s[None, :] // 4).astype(np.float32)
    t64 = np.zeros((128, 64), np.float32)
    t64[:64] = tri
    c["tri"] = t64
    tb = np.zeros((128, 64), np.float32)
    tb[:64] = tri * same
    c["trib"] = tb
    ss = np.zeros((128, 16), np.float32)
    ss[:64] = (s[:, None] // 4 == np.arange(16)[None, :]).astype(np.float32)
    c["seqsel"] = ss
    cm = np.zeros((128, 16, 64), np.float32)
    cm[:] = (np.arange(16)[:, None] == (s[None, :] // 4)).astype(np.float32)[None]
    c["colmask"] = cm.reshape(128, 1024)
    c["ones"] = np.ones((128, 128), np.float32)
    c["iota16"] = np.tile(np.arange(16, dtype=np.float32)[None, :], (128, 1))
    c["thr16"] = np.tile(16.0 * (np.arange(16, dtype=np.float32)[None, :] + 1.0), (128, 1))
    s128 = np.arange(128)
    c["tri128"] = (s128[:, None] <= s128[None, :]).astype(np.float32)
    c["iota128"] = np.tile(np.arange(128, dtype=np.float32)[None, :], (128, 1))
    offs, o = {}, 0
    for k, v in c.items():
        offs[k] = (o, v.shape[1])
        o += v.shape[1]
    pack = np.concatenate([c[k] for k in c], axis=1)
    return pack, offs


CONST_PACK, CONST_OFFS = _consts()


class Builder:
    def __init__(self, debug=False, stop_after=None, dense=True):
        self.debug = debug
        self.dense = dense
        self.stop_after = stop_after
        self.nc = bass.Bass("TRN2", target_bir_lowering=False)
        self.S = Sched(self.nc)
        self.st = contextlib.ExitStack()
        self.uid = 0
        self.store_q = "sp"

    def key(self, name):
        self.uid += 1
        return f"{name}#{self.uid}"

    def dname(self, ap):
        return "dram_" + str(ap.name)

    def din(self, name, shape, dt=F32):
        return self.nc.dram_tensor(name, list(shape), dt, kind="ExternalInput").ap()

    def dout(self, name, shape, dt=F32):
        return self.nc.dram_tensor(name, list(shape), dt, kind="ExternalOutput").ap()

    def dscr(self, name, shape, dt=F32):
        kind = "ExternalOutput" if self.debug else "Internal"
        return self.nc.dram_tensor(name, list(shape), dt, kind=kind).ap()

    def V(self, fn, r=(), w=()):
        return self.S.op("dve", fn, r, w)

    def A(self, fn, r=(), w=()):
        return self.S.op("act", fn, r, w)

    def T(self, fn, r=(), w=()):
        return self.S.op("pe", fn, r, w)

    def G(self, fn, r=(), w=()):
        return self.S.op("pool", fn, r, w)

    def ld(self, out, in_, r=(), w=(), q="sp"):
        return self.S.dma(q, lambda e: e.dma_start(out=out, in_=in_), r, w)

    def ldc(self, out, in_, r=(), w=()):
        return self.S.dma("pool", lambda e: e.dma_start(out=out, in_=in_), r, w)

    def stq(self, out, in_, r, w, q=None):
        q = q or self.store_q
        return self.S.dma(q, lambda e: e.dma_start(out=out, in_=in_), r, w, semkey=("st", r[0]))

    def build(self):
        nc, st = self.nc, self.st
        dbg = self.debug
        x_own = self.din("x_own", [NT, D])
        x_prev = self.din("x_prev", [NPV, D])
        flag = self.din("flag", [128, 1])
        consts_d = self.din("consts", list(CONST_PACK.shape))
        C0s = self.din("C0s", [16 * 8 * 128, 256])
        n0sT = self.din("n0sT", [128, 128])
        m0sT = self.din("m0sT", [8, 16])
        w_in_a = self.din("w_in_a", [D, 6160])
        b_gate = self.din("b_gate", [1, 16])
        hn_gain = self.din("hn_gain", [1, D])
        w_out_a = self.din("w_out_a", [D, D])
        ln_g = {}
        for nm in ("ln_mix_g", "ln_mix_b", "ln_ffn_g", "ln_ffn_b"):
            ln_g[nm] = self.din(nm, [2, D])
        self.ln = ln_g
        self.w_in_b = self.din("w_in_b", [D, 12288])
        self.b_in_b = self.din("b_in_b", [1, 12288])
        self.lnv_g = self.din("lnv_g", [1, 6144])
        self.lnv_b = self.din("lnv_b", [1, 6144])
        self.wsT = self.din("wsT", [128, 8, 128])
        self.wsT_s = self.din("wsT_s", [64, 8, 64])
        self.b_s_t = self.din("b_s_t", [128, 8])
        self.b_s_s = self.din("b_s_s", [64, 8])
        self.w_out_b = self.din("w_out_b", [6144, D])
        self.peer_wq = self.din("peer_wq", [2, D, D])
        self.skT = self.din("skT", [2, 128, 16, 128])
        self.peer_u = self.din("peer_u", [2 * 128 * D, 128])
        self.peer_v = self.din("peer_v", [2 * 16384, D])

        y_own = self.dout("y_own", [NT, D])
        Cp_o = self.dout("Cp", [8 * 128, 256])
        np_o = self.dout("npT", [128, 8])
        mp_o = self.dout("mpT", [8, 1])
        Cs_o = self.dout("Cs", [16 * 8 * 128, 256])
        ns_o = self.dout("nsT", [128, 128])
        ms_o = self.dout("msT", [8, 16])
        vs_o = self.dout("vs", [64, 6144])
        self.outs = dict(y_own=y_own, Cp=Cp_o, npT=np_o, mpT=mp_o, Cs=Cs_o, nsT=ns_o, msT=ms_o, vs=vs_o)

        sc = {}
        sc["qT"] = self.dscr("s_qT", [8, 128, NT], BF16)
        sc["kT"] = self.dscr("s_kT", [8, 128, NT], BF16)
        sc["k"] = self.dscr("s_k", [NALL, 1024], BF16)
        sc["v"] = self.dscr("s_v", [NALL, 2048], BF16)
        sc["og"] = self.dscr("s_og", [NT, 2048], F32)
        sc["gi"] = self.dscr("s_gi", [NALL, 8], F32)
        sc["lf"] = self.dscr("s_lf", [NALL, 8], F32)
        sc["hn"] = self.dscr("s_hn", [NT, D], BF16)
        sc["x1"] = self.dscr("s_x1", [NT, D], F32)
        sc["x2"] = self.dscr("s_x2", [NT, D], F32)
        sc["x3"] = self.dscr("s_x3", [NT, D], F32)
        sc["vraw"] = self.dscr("s_vraw", [NT, 6144], F32)
        sc["mixed"] = self.dscr("s_mixed", [NT, 6144], F32)
        sc["ymix"] = self.dscr("s_ymix", [NT, D], F32)
        sc["G"] = self.nc.dram_tensor("s_G", [9, 128, 128, 128], BF16, kind="Internal").ap()
        self.sc = sc

        ARENA_WORDS = 49 * 1024
        arena_t = st.enter_context(nc.sbuf_tensor("arena", [128, ARENA_WORDS], F32))
        self.ar = Arena(arena_t[:, :])
        cst = st.enter_context(nc.sbuf_tensor("cst", [128, CONST_PACK.shape[1]], F32))
        identb = st.enter_context(nc.sbuf_tensor("identb", [128, 128], BF16))
        colmb = st.enter_context(nc.sbuf_tensor("colmb", [128, 1024], BF16))
        self.ps = [st.enter_context(nc.psum_tensor(f"ps{i}", [128, 512], F32)) for i in range(8)]
        self.psk = [f"ps{i}" for i in range(8)]

        def cv(name):
            o, n = CONST_OFFS[name]
            return cst[:, o:o + n]
        self.ident = cv("ident")
        self.identb = identb[:, :]
        self.tri = cv("tri")
        self.trib = cv("trib")
        self.seqsel = cv("seqsel")
        self.colmb = colmb[:, :].rearrange("p (b c) -> p b c", b=16)
        self.ones = cv("ones")
        self.iota16 = cv("iota16")
        self.thr16 = cv("thr16")
        self.tri128 = cv("tri128")
        self.gelu_native = True
        self.iota128 = cv("iota128")
        self.ld(cst[:, :], consts_d, w=["cst"])
        self.V(lambda e: e.tensor_copy(identb[:, :], cv("ident")), ["cst"], ["identb"])
        self.V(lambda e: e.tensor_copy(colmb[:, :], cv("colmask")), ["cst"], ["colmb"])
        self.CK = ["cst", "identb", "colmb"]

        self.phase_A(x_own, x_prev, w_in_a, b_gate)
        self.S.barrier()
        if self.stop_after != "A":
            self.phase_B(flag, C0s, n0sT, m0sT, hn_gain)
            self.S.barrier()
        if self.stop_after not in ("A", "B"):
            self.phase_C(x_own, w_out_a)
            self.S.barrier()
        srcs = {k: sc[k] for k in ("x1", "x2", "x3")}
        if dbg:
            for k in ("x1", "x2", "x3"):
                srcs[k] = self.din("dbg_" + k, [NT, D])
        if self.stop_after not in ("A", "B", "C"):
            (self.phase_peer_dense if self.dense else self.phase_peer)(srcs["x1"], 0, sc["x2"], "D_")
            self.S.barrier()
        if self.stop_after not in ("A", "B", "C", "D"):
            self.phase_gmlp(srcs["x2"], sc["x3"])
            self.S.barrier()
        if self.stop_after not in ("A", "B", "C", "D", "E"):
            (self.phase_peer_dense if self.dense else self.phase_peer)(srcs["x3"], 1, y_own, "F_")
            self.S.barrier()
        self.S.op("sp", None, reads=tuple(self.S.allkeys))
        self.S.finalize()
        st.close()
        return nc

    def load_xT(self, src, tiles, xT, xTk, col0_of, stg, src_is_bf16=False, kc=16):
        ps, psk = self.ps, self.psk
        for ti, (row0, rows, col0) in enumerate(tiles):
            buf, bk = stg[ti % len(stg)]
            self.ld(buf[:rows, :], src[row0:row0 + rows, :], w=[bk])
            for g4 in range(kc // 4):
                b = (ti * (kc // 4) + g4) % 2
                if src_is_bf16:
                    pt = ps[b][:, 0:256].bitcast(BF16)
                    idn = self.identb
                else:
                    pt = ps[b][:, :]
                    idn = self.ident
                for j in range(4):
                    c = g4 * 4 + j
                    self.T(lambda e, pt=pt, j=j, c=c, buf=buf, rows=rows, idn=idn: e.transpose(
                        pt[:, j * 128:j * 128 + rows], buf[:rows, c * 128:(c + 1) * 128], idn[:rows, :rows]),
                        [bk] + self.CK, [psk[b]])
                src_v = pt.rearrange("p (a b) -> p a b", a=4)[:, :, :rows]
                dst_v = xT[:, g4 * 4:(g4 + 1) * 4, col0:col0 + rows]
                if g4 % 2 == 0:
                    self.V(lambda e, d=dst_v, s=src_v: e.tensor_copy(d, s), [psk[b]], [xTk])
                else:
                    self.A(lambda e, d=dst_v, s=src_v: e.copy(d, s), [psk[b]], [xTk])

    def layer_norm_rows(self, xin, rows, gk, bk_, gtile, btile, out, keys_in, key_out, scr):
        stats, mv, rstd = scr
        for j in range(4):
            self.V(lambda e, j=j: e.bn_stats(stats[:rows, j * 6:(j + 1) * 6], xin[:rows, j * 512:(j + 1) * 512]),
                   keys_in, [key_out + "_st"])
        self.V(lambda e: e.bn_aggr(mv[:rows, :], stats[:rows, :]), [key_out + "_st"], [key_out + "_mv"])
        self.A(lambda e: e.activation(out=rstd[:rows, :], in_=mv[:rows, 1:2], func=AF.Sqrt, bias=self.eps_t[:rows, :], scale=1.0),
               [key_out + "_mv", "eps"], [key_out + "_rs"])
        self.V(lambda e: e.reciprocal(rstd[:rows, :], rstd[:rows, :]), [key_out + "_rs"], [key_out + "_rs"])
        self.V(lambda e: e.tensor_scalar(out[:rows, :], xin[:rows, :], mv[:rows, 0:1], rstd[:rows, 0:1], ALU.subtract, ALU.mult),
               keys_in + [key_out + "_mv", key_out + "_rs"], [key_out])
        self.V(lambda e: e.tensor_tensor(out[:rows, :], out[:rows, :], gtile[:rows, :], ALU.mult), [key_out, gk], [key_out])
        self.V(lambda e: e.tensor_tensor(out[:rows, :], out[:rows, :], btile[:rows, :], ALU.add), [key_out, bk_], [key_out])

    def phase_A(self, x_own, x_prev, w_in_a, b_gate):
        ar, sc, ps, psk = self.ar, self.sc, self.ps, self.psk
        ar.reset()
        xT = ar.bf16(16 * NALL).rearrange("p (c n) -> p c n", c=16)
        xTk = "A_xT"
        stg = [(ar.f32(D), f"A_xs{i}") for i in range(2)]
        wts = [(ar.bf16(16 * 512).rearrange("p (c n) -> p c n", c=16), f"A_w{i}") for i in range(2)]
        ost = [(ar.f32(512), f"A_o{i}") for i in range(4)]
        bg = ar.f32(16)
        sm = [ar.f32(8) for _ in range(4)]
        self.ld(bg, b_gate.partition_broadcast(128).rearrange("p o n -> p (o n)"), w=["A_bg"])
        self.load_xT(x_own, [(r0, rw, r0) for (r0, rw) in TILES], xT, xTk, None, stg)
        self.load_xT(x_prev, [(i * 128, 128, NT + i * 128) for i in range(8)], xT, xTk, None, stg)
        alltiles = TILES + PTILES
        oi = [0]

        def nxt_ost():
            oi[0] += 1
            return ost[oi[0] % 4]

        pb = [0]

        def nxt_bank():
            pb[0] += 1
            return 2 + pb[0] % 6

        for g in range(13):
            ncol = 512 if g < 12 else 16
            wt, wk = wts[g % 2]
            self.ldc(wt[:, :, :ncol], w_in_a[:, g * 512:g * 512 + ncol].rearrange("(c p) n -> p c n", p=128), w=[wk])
            if g < 4:
                dst = sc["qT"] if g < 2 else sc["kT"]
                scale = 1.0 if g < 2 else float(128 ** -0.5)
                for j in range(4):
                    h = (g % 2) * 4 + j
                    for (t0, tn) in ((0, 512), (512, 512), (1024, 64)):
                        b = nxt_bank()
                        for c in range(16):
                            self.T(lambda e, b=b, c=c, j=j, t0=t0, tn=tn, wt=wt: e.matmul(
                                ps[b][:, :tn], lhsT=wt[:, c, j * 128:(j + 1) * 128], rhs=xT[:, c, t0:t0 + tn],
                                start=(c == 0), stop=(c == 15)), [wk, xTk], [psk[b]])
                        ob, ok = nxt_ost()
                        obb = ob.bitcast(BF16)
                        self.A(lambda e, b=b, tn=tn, obb=obb, scale=scale: e.activation(
                            out=obb[:, :tn], in_=ps[b][:, :tn], func=AF.Copy, scale=scale), [psk[b]], [ok])
                        self.stq(dst[h, :, t0:t0 + tn], obb[:, :tn], [ok], [self.dname(dst)])
            if g >= 2:
                tiles = alltiles if (g < 8 or g == 12) else TILES
                for (r0, rw) in tiles:
                    b = nxt_bank()
                    for c in range(16):
                        self.T(lambda e, b=b, c=c, r0=r0, rw=rw, wt=wt, ncol=ncol: e.matmul(
                            ps[b][:rw, :ncol], lhsT=xT[:, c, r0:r0 + rw], rhs=wt[:, c, :ncol],
                            start=(c == 0), stop=(c == 15)), [wk, xTk], [psk[b]])
                    ob, ok = nxt_ost()
                    if g < 4:
                        obb = ob.bitcast(BF16)
                        self.V(lambda e, b=b, rw=rw, obb=obb: e.tensor_scalar(
                            obb[:rw, :512], ps[b][:rw, :], float(128 ** -0.5), None, ALU.mult), [psk[b]], [ok])
                        self.stq(sc["k"][r0:r0 + rw, (g - 2) * 512:(g - 1) * 512], obb[:rw, :512], [ok], ["s_k"])
                    elif g < 8:
                        obb = ob.bitcast(BF16)
                        self.V(lambda e, b=b, rw=rw, obb=obb: e.tensor_copy(obb[:rw, :512], ps[b][:rw, :]), [psk[b]], [ok])
                        self.stq(sc["v"][r0:r0 + rw, (g - 4) * 512:(g - 3) * 512], obb[:rw, :512], [ok], ["s_v"])
                    elif g < 12:
                        self.A(lambda e, b=b, rw=rw, ob=ob: e.activation(out=ob[:rw, :], in_=ps[b][:rw, :], func=AF.Sigmoid),
                               [psk[b]], [ok])
                        self.stq(sc["og"][r0:r0 + rw, (g - 8) * 512:(g - 7) * 512], ob[:rw, :], [ok], ["s_og"])
                    else:
                        gt = ob[:, 0:16]
                        self.V(lambda e, b=b, rw=rw, gt=gt: e.tensor_tensor(gt[:rw, :], ps[b][:rw, :16], bg[:rw, :], ALU.add),
                               [psk[b], "A_bg"], [ok])
                        self.stq(sc["gi"][r0:r0 + rw, :], gt[:rw, 0:8], [ok], ["s_gi"])
                        ob2, ok2 = nxt_ost()
                        xx = gt[:, 8:16]
                        ab, ex, ln_, mn = ob2[:, 0:8], ob2[:, 8:16], ob2[:, 16:24], ob2[:, 24:32]
                        self.A(lambda e, rw=rw, ab=ab, xx=xx: e.activation(out=ab[:rw, :], in_=xx[:rw, :], func=AF.Abs), [ok], [ok2])
                        self.A(lambda e, rw=rw, ab=ab, ex=ex: e.activation(out=ex[:rw, :], in_=ab[:rw, :], func=AF.Exp, scale=-1.0), [ok2], [ok2])
                        self.V(lambda e, rw=rw, ex=ex: e.tensor_scalar_add(ex[:rw, :], ex[:rw, :], 1.0), [ok2], [ok2])
                        self.A(lambda e, rw=rw, ex=ex, ln_=ln_: e.activation(out=ln_[:rw, :], in_=ex[:rw, :], func=AF.Ln), [ok2], [ok2])
                        self.V(lambda e, rw=rw, mn=mn, xx=xx: e.tensor_scalar_min(mn[:rw, :], xx[:rw, :], 0.0), [ok, ok2], [ok2])
                        self.V(lambda e, rw=rw, mn=mn, ln_=ln_: e.tensor_tensor(mn[:rw, :], mn[:rw, :], ln_[:rw, :], ALU.subtract), [ok2], [ok2])
                        self.stq(sc["lf"][r0:r0 + rw, :], mn[:rw, :], [ok2], ["s_lf"])

    def phase_B(self, flag, C0s, n0sT, m0sT, hn_gain):
        ar, sc, ps, psk = self.ar, self.sc, self.ps, self.psk
        self.store_q = "pool"
        ar.reset()
        V, A, T = self.V, self.A, self.T
        CK = self.CK
        ident, tri, trib, seqsel, ones = self.ident, self.tri, self.trib, self.seqsel, self.ones
        CN = ar.f32(8 * 257).rearrange("p (h n) -> p h n", h=8)
        mT = ar.f32(1)
        flg = ar.f32(1)
        gain = ar.f32(D)
        eps_t = ar.f32(1)
        self.eps_t = eps_t
        self.G(lambda e: e.memset(eps_t, LN_EPS), [], ["eps"])
        self.ld(flg, flag, w=["B_flag"])
        self.ld(gain, hn_gain.partition_broadcast(128).rearrange("p o n -> p (o n)"), w=["B_gain"])
        CNK = [f"CN{h}" for h in range(8)]
        V(lambda e: e.memset(CN, 0.0), [], CNK)
        V(lambda e: e.memset(mT, 0.0), [], ["mT"])
        NB = 2

        def mk(n, f):
            return [f() for _ in range(n)]
        gi_b = mk(NB, lambda: ar.f32(8))
        lf_b = mk(NB, lambda: ar.f32(8))
        k_b = mk(NB, lambda: ar.bf16(1024))
        v_b = mk(NB, lambda: ar.bf16(2048))
        qT_b = mk(NB, lambda: ar.bf16(8 * 64).rearrange("p (h n) -> p h n", h=8))
        kT_b = mk(NB, lambda: ar.bf16(8 * 64).rearrange("p (h n) -> p h n", h=8))
        og_b = mk(NB, lambda: ar.f32(2048))
        sca = ar.f32(64)
        a_t, e_t, cl_t, bM_t = sca[:, 0:8], sca[:, 8:16], sca[:, 16:24], sca[:, 24:32]
        hm = ar.f32(160)
        vpp = ar.bf16(8 * 257).rearrange("p (h n) -> p h n", h=8)
        stm = ar.bf16(8 * 64).rearrange("p (h n) -> p h n", h=8)
        cns = mk(2, lambda: ar.bf16(257))
        Mf = ar.f32(256)
        Mtok = ar.f32(8)
        hg = ar.f32(2048)
        hnb = mk(2, lambda: ar.bf16(2048))
        stt = ar.f32(8 * 6)
        mvv = ar.f32(16)
        rr = ar.f32(16)
        den = ar.f32(8)
        Rm = ar.f32(256)
        Mx = ar.f32(128)
        NCS = 6
        cst_s = mk(NCS, lambda: ar.f32(257))
        kz_all = ar.bf16(16 * 1024).rearrange("p (b n) -> p b n", b=16)
        qz_all = ar.bf16(8 * 16 * 64).rearrange("p (h b n) -> p h b n", h=8, b=16)
        nsT = ar.f32(128)
        n0t = ar.f32(128)
        m0t = ar.f32(16)
        self.ld(n0t, n0sT, w=["B_n0"])
        self.ld(m0t[:8, :], m0sT, w=["B_m0"])

        def chunk(ci, tok0, mode):
            sl = ci % NB
            ks = f"B{sl}"
            samp = mode == "sample"
            NS, LS = (16, 4) if samp else (1, 64)
            trim = trib if samp else tri
            full = mode != "prefix"
            gi, lf, kk, vv, qT, kT, og = gi_b[sl], lf_b[sl], k_b[sl], v_b[sl], qT_b[sl], kT_b[sl], og_b[sl]
            self.ld(gi[:64, :], sc["gi"][tok0:tok0 + 64, :], w=[ks + "gi"])
            self.ld(lf[:64, :], sc["lf"][tok0:tok0 + 64, :], w=[ks + "lf"])
            self.ld(kk[:64, :], sc["k"][tok0:tok0 + 64, :], w=[ks + "k"])
            self.ld(vv[:64, :], sc["v"][tok0:tok0 + 64, :], w=[ks + "v"])
            if full:
                self.ld(qT, sc["qT"][:, :, tok0:tok0 + 64].rearrange("h p n -> p h n"), w=[ks + "qT"])
                self.ld(kT, sc["kT"][:, :, tok0:tok0 + 64].rearrange("h p n -> p h n"), w=[ks + "kT"])
                self.ld(og[:64, :], sc["og"][tok0:tok0 + 64, :], w=[ks + "og"])
            T(lambda e: e.matmul(ps[0][:64, 0:8], lhsT=trim[:64, :], rhs=lf[:64, :], start=True, stop=True), [ks + "lf"] + CK, ["ps0"])
            V(lambda e: e.tensor_tensor(a_t[:64, :], gi[:64, :], ps[0][:64, 0:8], ALU.subtract), [ks + "gi", "ps0"], ["a_t"])
            T(lambda e: e.transpose(ps[0][:8, 64:128], a_t[:64, :], ident[:64, :64]), ["a_t"] + CK, ["ps0"])
            amax = hm[:8, 0:NS]
            V(lambda e: e.tensor_reduce(amax, ps[0][:8, 64:128].rearrange("p (b t) -> p b t", b=NS), AX.X, ALU.max), ["ps0"], ["amax"])
            m_old = m0t[:8, 0:16] if samp else mT[:8, 0:1]
            mk_old = "B_m0" if samp else "mT"
            MT = hm[:8, 16:16 + NS]
            fT = hm[:8, 32:32 + NS]
            V(lambda e: e.tensor_tensor(MT, amax, m_old, ALU.max), ["amax", mk_old], ["MT"])
            V(lambda e: e.tensor_tensor(fT, m_old, MT, ALU.subtract), ["MT", mk_old], ["fT"])
            A(lambda e: e.activation(out=fT, in_=fT, func=AF.Exp), ["fT"], ["fT"])
            selm = seqsel[:64, 0:16] if samp else ones[:64, 0:1]
            T(lambda e: e.matmul(ps[0][:8, 128:128 + NS], lhsT=lf[:64, :], rhs=selm, start=True, stop=True), [ks + "lf"] + CK, ["ps0"])
            mnew = hm[:8, 48:48 + NS]
            V(lambda e: e.tensor_tensor(mnew, ps[0][:8, 128:128 + NS], MT, ALU.add), ["ps0", "MT"], ["mnew"])
            V(lambda e: e.tensor_copy(Mx[:8, 0:64].rearrange("p (b t) -> p b t", b=NS), MT.unsqueeze(2).broadcast_to([8, NS, LS])), ["MT"], ["Mx"])
            T(lambda e: e.transpose(ps[0][:64, 160:168], Mx[:8, 0:64], ident[:8, :8]), ["Mx"] + CK, ["ps0"])
            V(lambda e: e.tensor_copy(Mtok[:64, :], ps[0][:64, 160:168]), ["ps0"], ["Mtok"])
            V(lambda e: e.tensor_tensor(e_t[:64, :], a_t[:64, :], Mtok[:64, :], ALU.subtract), ["a_t", "Mtok"], ["e_t"])
            A(lambda e: e.activation(out=e_t[:64, :], in_=e_t[:64, :], func=AF.Exp), ["e_t"], ["e_t"])
            if full:
                V(lambda e: e.tensor_tensor(bM_t[:64, :], gi[:64, :], a_t[:64, :], ALU.subtract), [ks + "gi", "a_t"], ["bM"])
                V(lambda e: e.tensor_tensor(bM_t[:64, :], bM_t[:64, :], Mtok[:64, :], ALU.add), ["bM", "Mtok"], ["bM"])
                A(lambda e: e.activation(out=cl_t[:64, :], in_=bM_t[:64, :], func=AF.Exp, scale=-1.0), ["bM"], ["cl_t"])
            Rv = Rm[:8, 0:NS * 8].rearrange("p (b h) -> p b h", b=NS)
            V(lambda e: e.tensor_tensor(Rv, fT.unsqueeze(2).broadcast_to([8, NS, 8]),
                                        ident[:8, 0:8].unsqueeze(1).broadcast_to([8, NS, 8]), ALU.mult), ["fT"] + CK, ["Rm"])
            T(lambda e: e.matmul(ps[0][:, 256:256 + NS * 8], lhsT=ones[:8, :], rhs=Rm[:8, 0:NS * 8], start=True, stop=True), ["Rm"] + CK, ["ps0"])
            V(lambda e: e.tensor_copy(Mf[:, 0:NS * 8], ps[0][:, 256:256 + NS * 8]), ["ps0"], ["Mf"])
            V(lambda e: e.tensor_tensor(vpp[:64, :, 0:256], vv[:64, :].rearrange("p (h n) -> p h n", h=8),
                                        e_t[:64, :].unsqueeze(2).broadcast_to([64, 8, 256]), ALU.mult), [ks + "v", "e_t"], ["vpp"])
            V(lambda e: e.tensor_copy(vpp[:64, :, 256:257], e_t[:64, :].unsqueeze(2)), ["e_t", "vpp"], ["vpp"])
            if full:
                for h in range(8):
                    T(lambda e, h=h: e.matmul(ps[1][:64, h * 64:(h + 1) * 64], lhsT=kT[:, h, :], rhs=qT[:, h, :], start=True, stop=True),
                      [ks + "kT", ks + "qT"], ["ps1"])
                V(lambda e: e.tensor_tensor(stm[:64, :, :], ps[1][:64, :].rearrange("p (h n) -> p h n", h=8),
                                            trim[:64, :].unsqueeze(1).broadcast_to([64, 8, 64]), ALU.mult), ["ps1"] + CK, ["stm"])
            spend = []
            if samp:
                for b in range(16):
                    V(lambda e, b=b: e.tensor_scalar(kz_all[:64, b, :], kk[:64, :], seqsel[:64, b:b + 1], None, ALU.mult), [ks + "k"] + CK, ["kz_all"])
                for h in range(8):
                    V(lambda e, h=h: e.tensor_tensor(qz_all[:, h], qT[:, h, :].unsqueeze(1).broadcast_to([128, 16, 64]), self.colmb, ALU.mult), [ks + "qT"] + CK, ["qz_all"])
            for h in range(8):
                nb = 2 + h % 2
                nk = f"psn{h % 2}"
                if full:
                    T(lambda e, h=h, nb=nb: e.matmul(ps[nb][:64, 0:257], lhsT=stm[:64, h, :], rhs=vpp[:64, h, :], start=True, stop=False),
                      ["stm", "vpp"], [nk])
                if not samp:
                    cb, ckk = cns[h % 2], f"cns{h % 2}"
                    if full:
                        A(lambda e, h=h, cb=cb: e.activation(out=cb, in_=CN[:, h, :], func=AF.Copy, scale=Mf[:, h:h + 1]), [f"CN{h}", "Mf"], [ckk])
                        T(lambda e, h=h, nb=nb, cb=cb: e.matmul(ps[nb][:64, 0:257], lhsT=qT[:, h, :], rhs=cb, start=False, stop=True),
                          [ckk, ks + "qT"], [nk])
                    kb = 4 + h % 2
                    T(lambda e, h=h, kb=kb: e.matmul(ps[kb][:, 0:257], lhsT=kk[:64, h * 128:(h + 1) * 128], rhs=vpp[:64, h, :], start=True, stop=True),
                      [ks + "k", "vpp"], [f"psk{h % 2}"])
                    V(lambda e, h=h, kb=kb: e.scalar_tensor_tensor(out=CN[:, h, :], in0=CN[:, h, :], scalar=Mf[:, h:h + 1], in1=ps[kb][:, 0:257],
                                                                    op0=ALU.mult, op1=ALU.add), [f"CN{h}", "Mf", f"psk{h % 2}"], [f"CN{h}"])
                else:
                    for b in range(16):
                        i = h * 16 + b
                        cs_, csk = cst_s[i % NCS], f"cs{i % NCS}"
                        row = (b * 8 + h) * 128
                        self.ld(cs_[:, 0:256], C0s[row:row + 128, :], w=[csk])
                        V(lambda e, cs_=cs_, b=b, h=h: e.tensor_copy(cs_[:, 256:257], n0t[:, b * 8 + h:b * 8 + h + 1]), ["B_n0", csk], [csk])
                        cb, ckk = cns[i % 2], f"cns{i % 2}"
                        A(lambda e, cb=cb, cs_=cs_, b=b, h=h: e.activation(out=cb, in_=cs_, func=AF.Copy, scale=Mf[:, b * 8 + h:b * 8 + h + 1]),
                          [csk, "Mf"], [ckk])
                        T(lambda e, nb=nb, cb=cb, b=b, h=h: e.matmul(ps[nb][:64, 0:257], lhsT=qz_all[:, h, b, :], rhs=cb, start=False, stop=(b == 15)),
                          ["qz_all", ckk], [nk])
                        kb = 4 + i % 2
                        T(lambda e, kb=kb, h=h, b=b: e.matmul(ps[kb][:, 0:257], lhsT=kz_all[:64, b, h * 128:(h + 1) * 128], rhs=vpp[:64, h, :], start=True, stop=True),
                          ["kz_all", "vpp"], [f"psk{i % 2}"])
                        def back_(cs_=cs_, csk=csk, kb=kb, b=b, h=h, i=i, row=row):
                            V(lambda e: e.scalar_tensor_tensor(out=cs_, in0=cs_, scalar=Mf[:, b * 8 + h:b * 8 + h + 1], in1=ps[kb][:, 0:257],
                                                               op0=ALU.mult, op1=ALU.add), [csk, "Mf", f"psk{i % 2}"], [csk])
                            self.stq(self.outs["Cs"][row:row + 128, :], cs_[:, 0:256], [csk], ["Cs"])
                            V(lambda e: e.tensor_copy(nsT[:, b * 8 + h:b * 8 + h + 1], cs_[:, 256:257]), [csk, "nsT"], ["nsT"])
                        if spend:
                            spend.pop(0)()
                        spend.append(back_)
                if full:
                    A(lambda e, h=h, nb=nb: e.activation(out=den[:64, h:h + 1], in_=ps[nb][:64, 256:257], func=AF.Abs), [nk], ["den"])
                    V(lambda e, h=h: e.tensor_tensor(den[:64, h:h + 1], den[:64, h:h + 1], cl_t[:64, h:h + 1], ALU.max), ["den", "cl_t"], ["den"])
                    V(lambda e, h=h: e.reciprocal(den[:64, h:h + 1], den[:64, h:h + 1]), ["den"], ["den"])
                    V(lambda e, h=h, nb=nb: e.scalar_tensor_tensor(out=hg[:64, h * 256:(h + 1) * 256], in0=ps[nb][:64, 0:256], scalar=den[:64, h:h + 1],
                                                                    in1=og[:64, h * 256:(h + 1) * 256], op0=ALU.mult, op1=ALU.mult),
                      [nk, "den", ks + "og"], ["hg"])
                    V(lambda e, h=h: e.bn_stats(stt[:64, h * 6:(h + 1) * 6], hg[:64, h * 256:(h + 1) * 256]), ["hg"], ["stt"])
                    V(lambda e, h=h: e.bn_aggr(mvv[:64, h * 2:(h + 1) * 2], stt[:64, h * 6:(h + 1) * 6]), ["stt"], ["mvv"])
            if full:
                mv3 = mvv[:64, :].rearrange("p (h t) -> p h t", t=2)
                A(lambda e: e.activation(out=rr[:64, 0:8], in_=mv3[:, :, 1], func=AF.Ln, bias=eps_t[:64, :], scale=1.0), ["mvv", "eps"], ["rr"])
                A(lambda e: e.activation(out=rr[:64, 0:8], in_=rr[:64, 0:8], func=AF.Exp, scale=-0.5), ["rr"], ["rr"])
                for h in range(8):
                    V(lambda e, h=h: e.tensor_scalar(hg[:64, h * 256:(h + 1) * 256], hg[:64, h * 256:(h + 1) * 256], mvv[:64, 2 * h:2 * h + 1], rr[:64, h:h + 1],
                                                     ALU.subtract, ALU.mult), ["hg", "mvv", "rr"], ["hg"])
                ob, obk = hnb[ci % 2], f"hnb{ci % 2}"
                V(lambda e, ob=ob: e.tensor_tensor(ob[:64, :], hg[:64, :], gain[:64, :], ALU.mult), ["hg", "B_gain"], [obk])
                self.stq(sc["hn"][tok0:tok0 + 64, :], ob[:64, :], [obk], ["s_hn"])
            while spend:
                spend.pop(0)()
            if not samp:
                V(lambda e: e.tensor_copy(mT[:8, :], mnew), ["mnew"], ["mT"])
            else:
                V(lambda e: e.tensor_copy(hm[:8, 64:80], mnew), ["mnew"], ["ms_out"])
                self.stq(self.outs["msT"], hm[:8, 64:80], ["ms_out"], ["msT"])
                self.stq(self.outs["nsT"], nsT, ["nsT"], ["nsTo"])

        for ci in range(16):
            chunk(ci, NT + ci * 64, "prefix")
        V(lambda e: e.tensor_scalar(CN, CN, flg[:, 0:1], None, ALU.mult), CNK + ["B_flag"], CNK)
        V(lambda e: e.tensor_scalar(mT[:8, :], mT[:8, :], flg[:8, 0:1], None, ALU.mult), ["mT", "B_flag"], ["mT"])
        for ci in range(16):
            chunk(ci, ci * 64, "own")
        for h in range(8):
            self.stq(self.outs["Cp"][h * 128:(h + 1) * 128, :], CN[:, h, 0:256], [f"CN{h}"], ["Cp"])
        V(lambda e: e.tensor_copy(Rm[:, 128:136], CN[:, :, 256]), CNK + ["Rm"], ["npo"])
        self.stq(self.outs["npT"], Rm[:, 128:136], ["npo"], ["npT"])
        self.stq(self.outs["mpT"], mT[:8, :], ["mT"], ["mpT"])
        chunk(16, 1024, "sample")
        self.store_q = "sp"

    def out_proj_ln(self, src_bf16, resid_src, w_dram, kc, li, which, dst, prefix):
        ar, ps, psk = self.ar, self.ps, self.psk
        V, A, T = self.V, self.A, self.T
        ar.reset()
        eps_t = ar.f32(1)
        self.eps_t = eps_t
        self.G(lambda e: e.memset(eps_t, LN_EPS), [], ["eps"])
        xT = ar.bf16(kc * NT).rearrange("p (c n) -> p c n", c=kc)
        xTk = prefix + "xT"
        stg = [(ar.bf16(kc * 128), f"{prefix}xs{i}") for i in range(2)]
        self.load_xT(src_bf16, [(r0, rw, r0) for (r0, rw) in TILES], xT, xTk, None, stg, src_is_bf16=True, kc=kc)
        wt = ar.bf16(kc * D).rearrange("p (c n) -> p c n", c=kc)
        for q4 in range(4):
            self.ldc(wt[:, :, q4 * 512:(q4 + 1) * 512], w_dram[:, q4 * 512:(q4 + 1) * 512].rearrange("(c p) n -> p c n", p=128), w=[f"{prefix}w{q4}"])
        gt, bt = ar.f32(D), ar.f32(D)
        self.ld(gt, self.ln[f"ln_{which}_g"][li:li + 1, :].partition_broadcast(128).rearrange("p o n -> p (o n)"), w=[prefix + "g"])
        self.ld(bt, self.ln[f"ln_{which}_b"][li:li + 1, :].partition_broadcast(128).rearrange("p o n -> p (o n)"), w=[prefix + "b"])
        xr = [(ar.f32(D), f"{prefix}xr{i}") for i in range(2)]
        pre = [(ar.f32(D), f"{prefix}pre{i}") for i in range(2)]
        scr = (ar.f32(24), ar.f32(2), ar.f32(1))
        for ti, (r0, rw) in enumerate(TILES):
            xb, xk = xr[ti % 2]
            pb, pk = pre[ti % 2]
            self.ld(xb[:rw, :], resid_src[r0:r0 + rw, :], w=[xk])
            for q4 in range(4):
                b = 2 + (ti * 4 + q4) % 6
                for c in range(kc):
                    T(lambda e, b=b, c=c, q4=q4, r0=r0, rw=rw: e.matmul(ps[b][:rw, :], lhsT=xT[:, c, r0:r0 + rw], rhs=wt[:, c, q4 * 512:(q4 + 1) * 512],
                                                                         start=(c == 0), stop=(c == kc - 1)), [xTk, f"{prefix}w{q4}"], [psk[b]])
                V(lambda e, b=b, q4=q4, rw=rw, xb=xb, pb=pb: e.scalar_tensor_tensor(out=pb[:rw, q4 * 512:(q4 + 1) * 512], in0=xb[:rw, q4 * 512:(q4 + 1) * 512],
                                                                                     scalar=ALPHA, in1=ps[b][:rw, :], op0=ALU.mult, op1=ALU.add),
                  [xk, psk[b]], [pk])
            self.layer_norm_rows(pb, rw, prefix + "g", prefix + "b", gt, bt, pb, [pk], pk, scr)
            self.stq(dst[r0:r0 + rw, :], pb[:rw, :], [pk], [self.dname(dst)], q="pool")

    def phase_C(self, x_own, w_out_a):
        self.out_proj_ln(self.sc["hn"], x_own, w_out_a, 16, 0, "mix", self.sc["x1"], "C_")


    def gelu_tanh(self, out, in_, tmp, rows, kin, kout, ktmp):
        V, A = self.V, self.A
        if self.gelu_native:
            A(lambda e: e.activation(out=out, in_=in_, func=AF.Gelu_apprx_tanh), kin, [kout])
            return
        A(lambda e: e.activation(out=tmp, in_=in_, func=AF.Square), kin, [ktmp])
        V(lambda e: e.tensor_scalar(tmp, tmp, 0.044715, 1.0, ALU.mult, ALU.add), [ktmp], [ktmp])
        V(lambda e: e.tensor_tensor(tmp, tmp, in_, ALU.mult), [ktmp] + kin, [ktmp])
        A(lambda e: e.activation(out=tmp, in_=tmp, func=AF.Tanh, scale=0.7978845608028654), [ktmp], [ktmp])
        V(lambda e: e.tensor_scalar(tmp, tmp, 1.0, 0.5, ALU.add, ALU.mult), [ktmp], [ktmp])
        V(lambda e: e.tensor_tensor(out, tmp, in_, ALU.mult), [ktmp] + kin, [kout])

    def phase_peer(self, src, li, dst, pf):
        ar, ps, psk = self.ar, self.ps, self.psk
        V, A, T = self.V, self.A, self.T
        ar.reset()
        eps_t = ar.f32(1)
        self.eps_t = eps_t
        self.G(lambda e: e.memset(eps_t, LN_EPS), [], ["eps"])
        qT = ar.bf16(16 * NT).rearrange("p (c n) -> p c n", c=16)
        skt = ar.bf16(16 * 128).rearrange("p (c n) -> p c n", c=16)
        gt, bt = ar.f32(D), ar.f32(D)
        self.ld(gt, self.ln["ln_ffn_g"][li:li + 1, :].partition_broadcast(128).rearrange("p o n -> p (o n)"), w=[pf + "g"])
        self.ld(bt, self.ln["ln_ffn_b"][li:li + 1, :].partition_broadcast(128).rearrange("p o n -> p (o n)"), w=[pf + "b"])
        self.ldc(skt, self.skT[li], w=[pf + "sk"])
        mark = ar.off
        xT = ar.bf16(16 * NT).rearrange("p (c n) -> p c n", c=16)
        stg = [(ar.f32(D), f"{pf}xs{i}") for i in range(2)]
        wq = ar.bf16(16 * D).rearrange("p (c n) -> p c n", c=16)
        for q4 in range(4):
            self.ldc(wq[:, :, q4 * 512:(q4 + 1) * 512], self.peer_wq[li, :, q4 * 512:(q4 + 1) * 512].rearrange("(c p) n -> p c n", p=128), w=[pf + "wq"])
        self.load_xT(src, [(r0, rw, r0) for (r0, rw) in TILES], xT, pf + "xT", None, stg)
        n = 0
        for cg in range(16):
            for (t0, tn) in ((0, 512), (512, 512), (1024, 64)):
                b = 2 + n % 6
                n += 1
                for c in range(16):
                    T(lambda e, b=b, c=c, cg=cg, t0=t0, tn=tn: e.matmul(ps[b][:, :tn], lhsT=wq[:, c, cg * 128:(cg + 1) * 128], rhs=xT[:, c, t0:t0 + tn],
                                                                       start=(c == 0), stop=(c == 15)), [pf + "wq", pf + "xT"], [psk[b]])
                if n % 2:
                    V(lambda e, b=b, cg=cg, t0=t0, tn=tn: e.tensor_copy(qT[:, cg, t0:t0 + tn], ps[b][:, :tn]), [psk[b]], [pf + "qT"])
                else:
                    A(lambda e, b=b, cg=cg, t0=t0, tn=tn: e.copy(qT[:, cg, t0:t0 + tn], ps[b][:, :tn]), [psk[b]], [pf + "qT"])
        self.S.barrier()
        ar.off = mark
        def two(f):
            return [f(), f()]
        xt = two(lambda: ar.f32(D))
        idx = two(lambda: ar.i32(128))
        gw = two(lambda: ar.f32(128))
        Sc = ar.f32(2048).rearrange("p (c n) -> p c n", c=16)
        Wk = ar.f32(2048).rearrange("p (c n) -> p c n", c=16)
        sv = ar.f32(256).rearrange("p (c n) -> p c n", c=16)
        si = ar.u32(256).rearrange("p (c n) -> p c n", c=16)
        sif = ar.f32(256).rearrange("p (h t n) -> p h t n", h=8, t=2)
        cand = ar.f32(2048).rearrange("p (h n) -> p h n", h=8)
        cw = ar.f32(2048).rearrange("p (h n) -> p h n", h=8)
        cs = ar.f32(128).rearrange("p (h n) -> p h n", h=8)
        cp = ar.u32(128).rearrange("p (h n) -> p h n", h=8)
        ci = ar.u32(128).rearrange("p (h n) -> p h n", h=8)
        cj = ar.u32(128).rearrange("p (h n) -> p h n", h=8)
        cif = ar.f32(128).rearrange("p (h n) -> p h n", h=8)
        cjf = ar.f32(128).rearrange("p (h n) -> p h n", h=8)
        oh = ar.f32(2048).rearrange("p (h k n) -> p h k n", h=8, k=16)
        n0 = ar.f32(128).rearrange("p (h n) -> p h n", h=8)
        n1 = ar.f32(128).rearrange("p (h n) -> p h n", h=8)
        zz = ar.f32(16)
        av = ar.f32(128)
        wv = ar.f32(128)
        gtmp = ar.f32(128)
        NG = 4
        gb = [ar.f32(D) for _ in range(NG)]
        junk = ar.bf16(D)
        acc = two(lambda: ar.f32(D))
        scr = (ar.f32(24), ar.f32(2), ar.f32(1))
        sv4 = sv.rearrange("p (h t) n -> p h t n", t=2)
        for i in range(2):
            V(lambda e, i=i: e.memset(idx[i], 0), [], [f"{pf}idx{i}"])

        def topk(ti):
            r0, rw = TILES[ti]
            s2 = ti % 2
            xk, ik, gk = f"{pf}xt{s2}", f"{pf}idx{s2}", f"{pf}gw{s2}"
            self.ld(xt[s2][:rw, :], src[r0:r0 + rw, :], w=[xk])
            for g4 in range(4):
                b = 2 + (ti * 4 + g4) % 6
                for j in range(4):
                    cg = g4 * 4 + j
                    T(lambda e, b=b, j=j, cg=cg: e.matmul(ps[b][:rw, j * 128:(j + 1) * 128], lhsT=qT[:, cg, r0:r0 + rw], rhs=skt[:, cg, :], start=True, stop=True),
                      [pf + "qT", pf + "sk"], [psk[b]])
                dstv = Sc[:rw, g4 * 4:(g4 + 1) * 4, :]
                srcv = ps[b][:rw, :].rearrange("p (a n) -> p a n", a=4)
                wkeys = [f"{pf}S{g4 * 4 + j}" for j in range(4)]
                if g4 % 2:
                    A(lambda e, d=dstv, s=srcv: e.copy(d, s), [psk[b]], wkeys)
                else:
                    V(lambda e, d=dstv, s=srcv: e.tensor_copy(d, s), [psk[b]], wkeys)
            for cg in range(16):
                V(lambda e, cg=cg: e.max(out=sv[:rw, cg, 0:8], in_=Sc[:rw, cg, :]), [f"{pf}S{cg}"], [f"{pf}sv{cg}"])
            for cg in range(16):
                V(lambda e, cg=cg: e.max_index(out=si[:rw, cg, 0:8], in_max=sv[:rw, cg, 0:8], in_values=Sc[:rw, cg, :]), [f"{pf}S{cg}", f"{pf}sv{cg}"], [f"{pf}si{cg}"])
            for cg in range(16):
                V(lambda e, cg=cg: e.match_replace(out=Wk[:rw, cg, :], in_to_replace=sv[:rw, cg, 0:8], in_values=Sc[:rw, cg, :], imm_value=NEG),
                  [f"{pf}S{cg}", f"{pf}sv{cg}"], [f"{pf}W{cg}"])
            for cg in range(16):
                V(lambda e, cg=cg: e.max(out=sv[:rw, cg, 8:16], in_=Wk[:rw, cg, :]), [f"{pf}W{cg}"], [f"{pf}sv{cg}"])
            for cg in range(16):
                V(lambda e, cg=cg: e.max_index(out=si[:rw, cg, 8:16], in_max=sv[:rw, cg, 8:16], in_values=Wk[:rw, cg, :]), [f"{pf}W{cg}", f"{pf}sv{cg}"], [f"{pf}si{cg}"])
            svk = [f"{pf}sv{cg}" for cg in range(16)]
            sik = [f"{pf}si{cg}" for cg in range(16)]
            V(lambda e: e.tensor_copy(sif[:rw].rearrange("p h t n -> p (h t) n"), si[:rw]), sik, [pf + "sif"])
            V(lambda e: e.tensor_tensor(cand[:rw].rearrange("p h (i j) -> p h i j", i=16),
                                        sv4[:rw, :, 0, :].unsqueeze(3).broadcast_to([rw, 8, 16, 16]),
                                        sv4[:rw, :, 1, :].unsqueeze(2).broadcast_to([rw, 8, 16, 16]), ALU.add), svk, [pf + "cand"])
            ck = [f"{pf}c{h}" for h in range(8)]
            for h in range(8):
                V(lambda e, h=h: e.max(out=cs[:rw, h, 0:8], in_=cand[:rw, h, :]), [pf + "cand"], [ck[h] + "s"])
            for h in range(8):
                V(lambda e, h=h: e.max_index(out=cp[:rw, h, 0:8], in_max=cs[:rw, h, 0:8], in_values=cand[:rw, h, :]), [pf + "cand", ck[h] + "s"], [ck[h] + "p"])
            for h in range(8):
                V(lambda e, h=h: e.match_replace(out=cw[:rw, h, :], in_to_replace=cs[:rw, h, 0:8], in_values=cand[:rw, h, :], imm_value=NEG),
                  [pf + "cand", ck[h] + "s"], [ck[h] + "w"])
            for h in range(8):
                V(lambda e, h=h: e.max(out=cs[:rw, h, 8:16], in_=cw[:rw, h, :]), [ck[h] + "w"], [ck[h] + "s"])
            for h in range(8):
                V(lambda e, h=h: e.max_index(out=cp[:rw, h, 8:16], in_max=cs[:rw, h, 8:16], in_values=cw[:rw, h, :]), [ck[h] + "w", ck[h] + "s"], [ck[h] + "p"])
            csk = [c + "s" for c in ck]
            cpk = [c + "p" for c in ck]
            cpf = n1
            V(lambda e: e.tensor_copy(cpf[:rw], cp[:rw]), cpk, [pf + "cpf"])
            th4 = self.thr16[:rw, :].unsqueeze(1).unsqueeze(1).broadcast_to([rw, 8, 16, 16])
            V(lambda e: e.tensor_tensor(oh[:rw], cpf[:rw].unsqueeze(3).broadcast_to([rw, 8, 16, 16]), th4, ALU.is_ge), [pf + "cpf"] + self.CK, [pf + "oh"])
            V(lambda e: e.tensor_reduce(cif[:rw], oh[:rw], AX.X, ALU.add), [pf + "oh"], [pf + "cif"])
            V(lambda e: e.scalar_tensor_tensor(out=cjf[:rw], in0=cif[:rw], scalar=-16.0, in1=cpf[:rw], op0=ALU.mult, op1=ALU.add), [pf + "cif", pf + "cpf"], [pf + "cjf"])
            io4 = self.iota16[:rw, :].unsqueeze(1).unsqueeze(1).broadcast_to([rw, 8, 16, 16])
            for (cf, half, nn, kk) in ((cif, 0, n0, "n0"), (cjf, 1, n1, "n1")):
                V(lambda e, cf=cf: e.tensor_tensor(oh[:rw], cf[:rw].unsqueeze(3).broadcast_to([rw, 8, 16, 16]), io4, ALU.is_equal),
                  [pf + "cif", pf + "cjf"] + self.CK, [pf + "oh"])
                V(lambda e, half=half: e.tensor_tensor(oh[:rw], oh[:rw], sif[:rw, :, half, :].unsqueeze(2).broadcast_to([rw, 8, 16, 16]), ALU.mult),
                  [pf + "oh", pf + "sif"], [pf + "oh"])
                V(lambda e, nn=nn: e.tensor_reduce(nn[:rw], oh[:rw], AX.X, ALU.add), [pf + "oh"], [pf + kk])
            V(lambda e: e.scalar_tensor_tensor(out=n0[:rw], in0=n0[:rw], scalar=128.0, in1=n1[:rw], op0=ALU.mult, op1=ALU.add), [pf + "n0", pf + "n1"], [pf + "n0"])
            V(lambda e: e.tensor_scalar_add(n0[:rw], n0[:rw], float(li * 16384)), [pf + "n0"], [pf + "n0"])
            V(lambda e: e.tensor_copy(idx[s2][:rw, :].rearrange("p (h n) -> p h n", h=8), n0[:rw]), [pf + "n0"], [ik])
            g3 = gw[s2][:rw, :].rearrange("p (h n) -> p h n", h=8)
            V(lambda e: e.tensor_tensor(g3, cs[:rw], cs[:rw, :, 0:1].broadcast_to([rw, 8, 16]), ALU.subtract), csk, [gk])
            A(lambda e: e.activation(out=g3, in_=g3, func=AF.Exp), [gk], [gk])
            V(lambda e: e.tensor_reduce(zz[:rw, 0:8], g3, AX.X, ALU.add), [gk], [pf + "zz"])
            V(lambda e: e.reciprocal(zz[:rw, 0:8], zz[:rw, 0:8]), [pf + "zz"], [pf + "zz"])
            V(lambda e: e.tensor_tensor(g3, g3, zz[:rw, 0:8].unsqueeze(2).broadcast_to([rw, 8, 16]), ALU.mult), [gk, pf + "zz"], [gk])

        gctr = [0]

        def gather(tab, ti, hk):
            r0, rw = TILES[ti]
            s2 = ti % 2
            sl = gctr[0] % NG
            gctr[0] += 1
            buf, bk = gb[sl], f"{pf}gb{sl}"
            self.S.dma("pool", lambda e, buf=buf: e.indirect_dma_start(
                out=buf[:rw, :], out_offset=None, in_=tab,
                in_offset=bass.IndirectOffsetOnAxis(ap=idx[s2][:rw, hk:hk + 1], axis=0)), [f"{pf}idx{s2}"], [bk])
            return buf, bk

        def udots(ti):
            r0, rw = TILES[ti]
            s2 = ti % 2
            for hk in range(128):
                buf, bk = gather(self.peer_u, ti, hk)
                V(lambda e, buf=buf, hk=hk: e.scalar_tensor_tensor(out=junk[:rw, :], in0=buf[:rw, :], scalar=1.0, in1=xt[s2][:rw, :],
                                                                   op0=ALU.mult, op1=ALU.mult, accum_out=av[:rw, hk:hk + 1]),
                  [bk, f"{pf}xt{s2}"], [f"{pf}junk{hk % 4}", f"{pf}av{hk % 8}"])
            avk = [f"{pf}av{i}" for i in range(8)]
            self.gelu_tanh(wv[:rw, :], av[:rw, :], gtmp[:rw, :], rw, avk, pf + "wv", pf + "gtmp")
            V(lambda e: e.tensor_tensor(wv[:rw, :], wv[:rw, :], gw[s2][:rw, :], ALU.mult), [pf + "wv", f"{pf}gw{s2}"], [pf + "wv"])

        def vacc(ti):
            r0, rw = TILES[ti]
            s2 = ti % 2
            for hk in range(128):
                buf, bk = gather(self.peer_v, ti, hk)
                a_, ak = acc[hk % 2], f"{pf}acc{hk % 2}"
                if hk < 2:
                    V(lambda e, buf=buf, hk=hk, a_=a_: e.tensor_scalar(a_[:rw, :], buf[:rw, :], wv[:rw, hk:hk + 1], None, ALU.mult), [bk, pf + "wv"], [ak])
                else:
                    V(lambda e, buf=buf, hk=hk, a_=a_: e.scalar_tensor_tensor(out=a_[:rw, :], in0=buf[:rw, :], scalar=wv[:rw, hk:hk + 1], in1=a_[:rw, :],
                                                                               op0=ALU.mult, op1=ALU.add), [bk, pf + "wv", ak], [ak])
            a0, a1 = acc
            V(lambda e: e.tensor_tensor(a0[:rw, :], a0[:rw, :], a1[:rw, :], ALU.add), [pf + "acc0", pf + "acc1"], [pf + "acc0"])
            V(lambda e: e.scalar_tensor_tensor(out=a0[:rw, :], in0=xt[s2][:rw, :], scalar=ALPHA, in1=a0[:rw, :], op0=ALU.mult, op1=ALU.add),
              [pf + "acc0", f"{pf}xt{s2}"], [pf + "acc0"])
            self.layer_norm_rows(a0, rw, pf + "g", pf + "b", gt, bt, a1, [pf + "acc0"], pf + "acc1", scr)
            self.stq(dst[r0:r0 + rw, :], a1[:rw, :], [pf + "acc1"], [self.dname(dst)])

        topk(0)
        for ti in range(len(TILES)):
            udots(ti)
            if ti + 1 < len(TILES):
                topk(ti + 1)
            vacc(ti)

    def phase_gmlp(self, src, dst):
        ar, sc, ps, psk = self.ar, self.sc, self.ps, self.psk
        V, A, T = self.V, self.A, self.T
        pf = "E_"
        ar.reset()
        eps_t = ar.f32(1)
        self.eps_t = eps_t
        self.G(lambda e: e.memset(eps_t, LN_EPS), [], ["eps"])
        um_off = ar.off
        umT = ar.bf16(48 * NT).rearrange("p (c n) -> p c n", c=48)
        ar2 = Arena(ar.ap[:, um_off:ar.off])
        mark0 = ar.off
        xT = ar.bf16(16 * NT).rearrange("p (c n) -> p c n", c=16)
        stats = ar.f32(9 * 72).rearrange("p (t s) -> p t s", t=9)
        mark1 = ar.off
        stg = [(ar2.f32(D), f"{pf}xs{i}") for i in range(2)]
        self.load_xT(src, [(r0, rw, r0) for (r0, rw) in TILES], xT, pf + "xT", None, stg)
        wts = [(ar2.bf16(16 * 512).rearrange("p (c n) -> p c n", c=16), f"{pf}w{i}") for i in range(2)]
        bts = [(ar2.f32(512), f"{pf}bi{i}") for i in range(2)]
        ost = [(ar2.f32(512), f"{pf}o{i}") for i in range(4)]
        n = 0
        for g in range(12):
            c0 = 6144 + g * 512
            wt, wk = wts[g % 2]
            bt_, bk_ = bts[g % 2]
            self.ldc(wt, self.w_in_b[:, c0:c0 + 512].rearrange("(c p) n -> p c n", p=128), w=[wk])
            self.ld(bt_, self.b_in_b[0:1, c0:c0 + 512].partition_broadcast(128).rearrange("p o n -> p (o n)"), w=[bk_])
            for ti, (r0, rw) in enumerate(TILES):
                b = 2 + n % 6
                ob, ok = ost[n % 4]
                n += 1
                for c in range(16):
                    T(lambda e, b=b, c=c, r0=r0, rw=rw, wt=wt: e.matmul(ps[b][:rw, :], lhsT=xT[:, c, r0:r0 + rw], rhs=wt[:, c, :], start=(c == 0), stop=(c == 15)),
                      [wk, pf + "xT"], [psk[b]])
                V(lambda e, b=b, rw=rw, ob=ob, bt_=bt_: e.tensor_tensor(ob[:rw, :], ps[b][:rw, :], bt_[:rw, :], ALU.add), [psk[b], bk_], [ok])
                A(lambda e, rw=rw, ob=ob: e.activation(out=ob[:rw, :], in_=ob[:rw, :], func=AF.Gelu_apprx_tanh), [ok], [ok])
                V(lambda e, rw=rw, ob=ob, ti=ti, g=g: e.bn_stats(stats[:rw, ti, g * 6:(g + 1) * 6], ob[:rw, :]), [ok], [pf + "stats"])
                self.stq(sc["vraw"][r0:r0 + rw, g * 512:(g + 1) * 512], ob[:rw, :], [ok], ["s_vraw"])
        self.S.barrier()
        ar.off = mark1
        ar2.reset()
        vts = [(ar2.f32(6144), pf + "vt0"), (ar.f32(6144), pf + "vt1")]
        lg, lb = ar2.f32(6144), ar2.f32(6144)
        vnb = ar2.bf16(6144)
        mxs = [(ar.f32(768), f"{pf}mx{i}") for i in range(6)]
        wsf = ar.f32(1024).rearrange("p (g t) -> p g t", g=8)
        wsm = ar.bf16(1024).rearrange("p (g t) -> p g t", g=8)
        wsf_s = ar.f32(512).rearrange("p (g t) -> p g t", g=8)
        wsm_s = ar.bf16(512).rearrange("p (g t) -> p g t", g=8)
        bst, bss = ar.f32(8), ar.f32(8)
        self.ld(lg, self.lnv_g.partition_broadcast(128).rearrange("p o n -> p (o n)"), w=[pf + "lg"])
        self.ld(lb, self.lnv_b.partition_broadcast(128).rearrange("p o n -> p (o n)"), w=[pf + "lb"])
        self.ld(wsf, self.wsT, w=[pf + "wsf"])
        self.ld(wsf_s[:64], self.wsT_s, w=[pf + "wsfs"])
        self.ld(bst, self.b_s_t, w=[pf + "bst"])
        self.ld(bss[:64], self.b_s_s, w=[pf + "bss"])
        V(lambda e: e.tensor_tensor(wsm, wsf, self.tri128.unsqueeze(1).broadcast_to([128, 8, 128]), ALU.mult), [pf + "wsf"] + self.CK, [pf + "wsm"])
        V(lambda e: e.tensor_tensor(wsm_s[:64], wsf_s[:64], self.tri[:64, :].unsqueeze(1).broadcast_to([64, 8, 64]), ALU.mult), [pf + "wsfs"] + self.CK, [pf + "wsms"])
        n = 0
        vnb2 = [(vnb, pf + "vnb0"), (ar2.bf16(6144), pf + "vnb1")]
        sm2 = [(ar.f32(2), ar.f32(1), ar.f32(1), f"{pf}sm{i}") for i in range(2)]

        def e3_stage1(ti):
            r0, rw = TILES[ti]
            vt, vtk = vts[ti % 2]
            vb, vbk = vnb2[ti % 2]
            mv, rs, nmr, smk = sm2[ti % 2]
            self.ld(vt[:rw, :], sc["vraw"][r0:r0 + rw, :], w=[vtk] + [f"{vtk}_{c}" for c in range(4)])
            V(lambda e: e.bn_aggr(mv[:rw, :], stats[:rw, ti, :]), [pf + "stats"], [smk])
            A(lambda e: e.activation(out=rs[:rw, :], in_=mv[:rw, 1:2], func=AF.Sqrt, bias=eps_t[:rw, :], scale=1.0), [smk, "eps"], [smk])
            V(lambda e: e.reciprocal(rs[:rw, :], rs[:rw, :]), [smk], [smk])
            V(lambda e: e.scalar_tensor_tensor(out=nmr[:rw, :], in0=mv[:rw, 0:1], scalar=-1.0, in1=rs[:rw, :], op0=ALU.mult, op1=ALU.mult), [smk], [smk])
            for cb4 in range(4):
                cs_ = slice(cb4 * 1536, (cb4 + 1) * 1536)
                kq = f"{vtk}_{cb4}"
                A(lambda e, cs_=cs_: e.activation(out=vt[:rw, cs_], in_=vt[:rw, cs_], func=AF.Identity, bias=nmr[:rw, :], scale=rs[:rw, 0:1]), [vtk, smk], [kq])
            for cb4 in range(4):
                cs_ = slice(cb4 * 1536, (cb4 + 1) * 1536)
                kq = f"{vtk}_{cb4}"
                V(lambda e, cs_=cs_: e.tensor_tensor(vt[:rw, cs_], vt[:rw, cs_], lg[:rw, cs_], ALU.mult), [kq, pf + "lg"], [kq])
                V(lambda e, cs_=cs_: e.tensor_tensor(vt[:rw, cs_], vt[:rw, cs_], lb[:rw, cs_], ALU.add), [kq, pf + "lb"], [kq])
                A(lambda e, cs_=cs_: e.copy(vb[:rw, cs_], vt[:rw, cs_]), [kq], [f"{vbk}_{cb4}"])
            if rw == 64:
                self.stq(self.outs["vs"], vt[:64, :], [f"{vtk}_{c}" for c in range(4)], ["vs"], q="pool")

        def e3_stage2(ti):
            nonlocal n
            r0, rw = TILES[ti]
            samp = rw == 64
            vb, vbk = vnb2[ti % 2]
            wm = wsm_s if samp else wsm
            wmk = pf + ("wsms" if samp else "wsm")
            bs_ = bss if samp else bst
            bsk = pf + ("bss" if samp else "bst")
            for g8 in range(8):
                mx, mk_ = mxs[(ti * 8 + g8) % 6]
                for (c0, cn) in ((0, 512), (512, 256)):
                    b = 2 + n % 6
                    n += 1
                    T(lambda e, b=b, g8=g8, c0=c0, cn=cn: e.matmul(ps[b][:rw, :cn], lhsT=wm[:rw, g8, :rw], rhs=vb[:rw, g8 * 768 + c0:g8 * 768 + c0 + cn],
                                                                   start=True, stop=True), [wmk, f"{vbk}_{g8 // 2}"], [psk[b]])
                    if c0 == 0:
                        V(lambda e, b=b, g8=g8, c0=c0, cn=cn, mx=mx: e.tensor_scalar(mx[:rw, c0:c0 + cn], ps[b][:rw, :cn], bs_[:rw, g8:g8 + 1], None, ALU.add),
                          [psk[b], bsk], [mk_])
                    else:
                        A(lambda e, b=b, g8=g8, c0=c0, cn=cn, mx=mx: e.activation(out=mx[:rw, c0:c0 + cn], in_=ps[b][:rw, :cn], func=AF.Identity, bias=bs_[:rw, g8:g8 + 1], scale=1.0),
                          [psk[b], bsk], [mk_])
                self.stq(sc["mixed"][r0:r0 + rw, g8 * 768:(g8 + 1) * 768], mx[:rw, :], [mk_], ["s_mixed"], q="pool")

        e3_stage1(0)
        for ti in range(len(TILES)):
            if ti + 1 < len(TILES):
                e3_stage1(ti + 1)
            e3_stage2(ti)
        self.S.barrier()
        ar.off = mark1
        wt1s = [(ar.bf16(16 * 512).rearrange("p (c n) -> p c n", c=16), f"{pf}w1_{i}") for i in range(2)]
        bts = [(ar.f32(512), f"{pf}bj{i}") for i in range(2)]
        ust = [(ar.f32(512), f"{pf}u{i}") for i in range(3)]
        mst = [(ar.f32(512), f"{pf}m{i}") for i in range(3)]
        umb = [(ar.bf16(512), f"{pf}um{i}") for i in range(3)]
        n = 0
        pend = []
        for g in range(12):
            c0 = g * 512
            bt_, bk_ = bts[g % 2]
            wt1, w1k = wt1s[g % 2]
            self.ldc(wt1, self.w_in_b[:, c0:c0 + 512].rearrange("(c p) n -> p c n", p=128), w=[w1k])
            self.ld(bt_, self.b_in_b[0:1, c0:c0 + 512].partition_broadcast(128).rearrange("p o n -> p (o n)"), w=[bk_])
            for ti, (r0, rw) in enumerate(TILES):
                b = 2 + n % 6
                ub, uk = ust[n % 3]
                mb_, mk_ = mst[n % 3]
                qb, qk = umb[n % 3]
                n += 1
                self.ld(mb_[:rw, :], sc["mixed"][r0:r0 + rw, c0:c0 + 512], w=[mk_])
                for c in range(16):
                    T(lambda e, b=b, c=c, r0=r0, rw=rw, wt1=wt1: e.matmul(ps[b][:rw, :], lhsT=xT[:, c, r0:r0 + rw], rhs=wt1[:, c, :], start=(c == 0), stop=(c == 15)),
                      [w1k, pf + "xT"], [psk[b]])
                V(lambda e, b=b, rw=rw, ub=ub, bt_=bt_: e.tensor_tensor(ub[:rw, :], ps[b][:rw, :], bt_[:rw, :], ALU.add), [psk[b], bk_], [uk])
                A(lambda e, rw=rw, ub=ub: e.activation(out=ub[:rw, :], in_=ub[:rw, :], func=AF.Gelu_apprx_tanh), [uk], [uk])
                V(lambda e, rw=rw, ub=ub, mb_=mb_, qb=qb: e.tensor_tensor(qb[:rw, :], ub[:rw, :], mb_[:rw, :], ALU.mult), [uk, mk_], [qk])
                def tr_(tb=n % 2, rw=rw, qb=qb, qk=qk, g=g, r0=r0):
                    pt = ps[tb][:, 0:256].bitcast(BF16)
                    for j in range(4):
                        T(lambda e, pt=pt, j=j: e.transpose(pt[:, j * 128:j * 128 + rw], qb[:rw, j * 128:(j + 1) * 128], self.identb[:rw, :rw]),
                          [qk] + self.CK, [psk[tb]])
                    srcv = pt.rearrange("p (a b) -> p a b", a=4)[:, :, :rw]
                    dstv = umT[:, g * 4:(g + 1) * 4, r0:r0 + rw]
                    A(lambda e, d=dstv, s=srcv: e.copy(d, s), [psk[tb]], [pf + "umT"])
                pend.append(tr_)
                if len(pend) > 1:
                    pend.pop(0)()
        while pend:
            pend.pop(0)()
        self.S.barrier()
        ar.off = mark0
        wo = [(ar.bf16(48 * 256).rearrange("p (c n) -> p c n", c=48), f"{pf}wo{i}") for i in range(2)]
        yst = [(ar.f32(256), f"{pf}y{i}") for i in range(4)]
        n = 0
        for cg in range(8):
            wt, wk = wo[cg % 2]
            for k3 in range(3):
                self.ldc(wt[:, k3 * 16:(k3 + 1) * 16, :], self.w_out_b[k3 * 2048:(k3 + 1) * 2048, cg * 256:(cg + 1) * 256].rearrange("(c p) n -> p c n", p=128), w=[wk])
            for ti, (r0, rw) in enumerate(TILES):
                b = 2 + n % 6
                yb, yk = yst[n % 4]
                n += 1
                for c in range(48):
                    T(lambda e, b=b, c=c, r0=r0, rw=rw, wt=wt: e.matmul(ps[b][:rw, 0:256], lhsT=umT[:, c, r0:r0 + rw], rhs=wt[:, c, :], start=(c == 0), stop=(c == 47)),
                      [wk, pf + "umT"], [psk[b]])
                if n % 2:
                    V(lambda e, b=b, rw=rw, yb=yb: e.tensor_copy(yb[:rw, :], ps[b][:rw, 0:256]), [psk[b]], [yk])
                else:
                    A(lambda e, b=b, rw=rw, yb=yb: e.copy(yb[:rw, :], ps[b][:rw, 0:256]), [psk[b]], [yk])
                self.stq(sc["ymix"][r0:r0 + rw, cg * 256:(cg + 1) * 256], yb[:rw, :], [yk], ["s_ymix"])
        self.S.barrier()
        ar.reset()
        eps_t = ar.f32(1)
        self.eps_t = eps_t
        self.G(lambda e: e.memset(eps_t, LN_EPS), [], ["eps"])
        gt, bt = ar.f32(D), ar.f32(D)
        self.ld(gt, self.ln["ln_mix_g"][1:2, :].partition_broadcast(128).rearrange("p o n -> p (o n)"), w=[pf + "g"])
        self.ld(bt, self.ln["ln_mix_b"][1:2, :].partition_broadcast(128).rearrange("p o n -> p (o n)"), w=[pf + "b"])
        xr = [(ar.f32(D), f"{pf}xr{i}") for i in range(2)]
        yr = [(ar.f32(D), f"{pf}yr{i}") for i in range(2)]
        scr = (ar.f32(24), ar.f32(2), ar.f32(1))
        for ti, (r0, rw) in enumerate(TILES):
            xb, xk = xr[ti % 2]
            yb, yk = yr[ti % 2]
            self.ld(xb[:rw, :], src[r0:r0 + rw, :], w=[xk])
            self.ld(yb[:rw, :], sc["ymix"][r0:r0 + rw, :], w=[yk])
            V(lambda e, rw=rw, xb=xb, yb=yb: e.scalar_tensor_tensor(out=yb[:rw, :], in0=xb[:rw, :], scalar=ALPHA, in1=yb[:rw, :], op0=ALU.mult, op1=ALU.add), [xk, yk], [yk])
            self.layer_norm_rows(yb, rw, pf + "g", pf + "b", gt, bt, yb, [yk], yk, scr)
            self.stq(dst[r0:r0 + rw, :], yb[:rw, :], [yk], [self.dname(dst)], q="pool")


    def phase_peer_dense(self, src, li, dst, pf):
        ar, ps, psk = self.ar, self.ps, self.psk
        V, A, T = self.V, self.A, self.T
        NTP = 1152
        G_scr = self.sc["G"]
        ar.reset()
        eps_t = ar.f32(1)
        self.eps_t = eps_t
        self.G(lambda e: e.memset(eps_t, LN_EPS), [], ["eps"])
        xT = ar.bf16(16 * NTP).rearrange("p (c n) -> p c n", c=16)
        xTk = pf + "xT"
        V(lambda e: e.memset(xT[:, :, NT:NTP], 0.0), [], [xTk])
        markP = ar.off
        qT = ar.bf16(16 * NT).rearrange("p (c n) -> p c n", c=16)
        skt = ar.bf16(16 * 128).rearrange("p (c n) -> p c n", c=16)
        self.ldc(skt, self.skT[li], w=[pf + "sk"])
        mark = ar.off
        stg = [(ar.f32(D), f"{pf}xs{i}") for i in range(2)]
        wq = ar.bf16(16 * D).rearrange("p (c n) -> p c n", c=16)
        for q4 in range(4):
            self.ldc(wq[:, :, q4 * 512:(q4 + 1) * 512], self.peer_wq[li, :, q4 * 512:(q4 + 1) * 512].rearrange("(c p) n -> p c n", p=128), w=[f"{pf}wq{q4}"])
        self.load_xT(src, [(r0, rw, r0) for (r0, rw) in TILES], xT, xTk, None, stg)
        n = 0
        for cg in range(16):
            for (t0, tn) in ((0, 512), (512, 512), (1024, 64)):
                b = 2 + n % 6
                n += 1
                for c in range(16):
                    T(lambda e, b=b, c=c, cg=cg, t0=t0, tn=tn: e.matmul(ps[b][:, :tn], lhsT=wq[:, c, cg * 128:(cg + 1) * 128], rhs=xT[:, c, t0:t0 + tn],
                                                                       start=(c == 0), stop=(c == 15)), [f"{pf}wq{cg // 4}", xTk], [psk[b]])
                if n % 2:
                    V(lambda e, b=b, cg=cg, t0=t0, tn=tn: e.tensor_copy(qT[:, cg, t0:t0 + tn], ps[b][:, :tn]), [psk[b]], [pf + "qT"])
                else:
                    A(lambda e, b=b, cg=cg, t0=t0, tn=tn: e.copy(qT[:, cg, t0:t0 + tn], ps[b][:, :tn]), [psk[b]], [pf + "qT"])
        self.S.barrier()
        ar.off = mark
        Sc = ar.f32(2048).rearrange("p (c n) -> p c n", c=16)
        Wk = ar.f32(2048).rearrange("p (c n) -> p c n", c=16)
        sv = ar.f32(256).rearrange("p (c n) -> p c n", c=16)
        si = ar.u32(256).rearrange("p (c n) -> p c n", c=16)
        sif = ar.f32(256).rearrange("p (h t n) -> p h t n", h=8, t=2)
        cand = ar.f32(2048).rearrange("p (h n) -> p h n", h=8)
        cw = ar.f32(2048).rearrange("p (h n) -> p h n", h=8)
        cs = ar.f32(128).rearrange("p (h n) -> p h n", h=8)
        cp = ar.u32(128).rearrange("p (h n) -> p h n", h=8)
        cif = ar.f32(128).rearrange("p (h n) -> p h n", h=8)
        cjf = ar.f32(128).rearrange("p (h n) -> p h n", h=8)
        oh = ar.f32(2048).rearrange("p (h k n) -> p h k n", h=8, k=16)
        n0 = ar.f32(128).rearrange("p (h n) -> p h n", h=8)
        n1 = ar.f32(128).rearrange("p (h n) -> p h n", h=8)
        cpf = ar.f32(128).rearrange("p (h n) -> p h n", h=8)
        gwt = ar.f32(128)
        zz = ar.f32(16)
        itT = ar.f32(384).rearrange("p (a n) -> p a n", a=3)
        itB = ar.bf16(384).rearrange("p (a n) -> p a n", a=3)
        io128b = ar.bf16(128)
        V(lambda e: e.tensor_copy(io128b, self.iota128), self.CK, [pf + "io128b"])
        P0 = ar.bf16(64 * 128).rearrange("p (t n) -> p t n", t=64)
        P1 = ar.bf16(64 * 128).rearrange("p (t n) -> p t n", t=64)
        Gbuf = ar.bf16(128 * 128).rearrange("p (i t) -> p i t", i=128)
        sv4 = sv.rearrange("p (h t) n -> p h t n", t=2)
        V(lambda e: e.memset(Gbuf, 0.0), [], [pf + "Gbuf"])
        ectr = [0]

        def scores(ti):
            r0, rw = TILES[ti]
            for g4 in range(4):
                b = 2 + (ti * 4 + g4) % 6
                for j in range(4):
                    cg = g4 * 4 + j
                    T(lambda e, b=b, j=j, cg=cg, r0=r0, rw=rw: e.matmul(ps[b][:rw, j * 128:(j + 1) * 128], lhsT=qT[:, cg, r0:r0 + rw], rhs=skt[:, cg, :], start=True, stop=True),
                      [pf + "qT", pf + "sk"], [psk[b]])
                dstv = Sc[:rw, g4 * 4:(g4 + 1) * 4, :]
                srcv = ps[b][:rw, :].rearrange("p (a n) -> p a n", a=4)
                wkeys = [f"{pf}S{g4 * 4 + j}" for j in range(4)]
                A(lambda e, d=dstv, s=srcv: e.copy(d, s), [psk[b]], wkeys)

        scores(0)
        for ti, (r0, rw) in enumerate(TILES):
            for cg in range(16):
                V(lambda e, cg=cg, rw=rw: e.max(out=sv[:rw, cg, 0:8], in_=Sc[:rw, cg, :]), [f"{pf}S{cg}"], [f"{pf}sv{cg}"])
            for cg in range(16):
                V(lambda e, cg=cg, rw=rw: e.max_index(out=si[:rw, cg, 0:8], in_max=sv[:rw, cg, 0:8], in_values=Sc[:rw, cg, :]), [f"{pf}S{cg}", f"{pf}sv{cg}"], [f"{pf}si{cg}"])
            for cg in range(16):
                V(lambda e, cg=cg, rw=rw: e.match_replace(out=Wk[:rw, cg, :], in_to_replace=sv[:rw, cg, 0:8], in_values=Sc[:rw, cg, :], imm_value=NEG),
                  [f"{pf}S{cg}", f"{pf}sv{cg}"], [f"{pf}W{cg}"])
            for cg in range(16):
                V(lambda e, cg=cg, rw=rw: e.max(out=sv[:rw, cg, 8:16], in_=Wk[:rw, cg, :]), [f"{pf}W{cg}"], [f"{pf}sv{cg}"])
            for cg in range(16):
                V(lambda e, cg=cg, rw=rw: e.max_index(out=si[:rw, cg, 8:16], in_max=sv[:rw, cg, 8:16], in_values=Wk[:rw, cg, :]), [f"{pf}W{cg}", f"{pf}sv{cg}"], [f"{pf}si{cg}"])
            svk = [f"{pf}sv{cg}" for cg in range(16)]
            sik = [f"{pf}si{cg}" for cg in range(16)]
            V(lambda e, rw=rw: e.tensor_copy(sif[:rw].rearrange("p h t n -> p (h t) n"), si[:rw]), sik, [pf + "sif"])
            V(lambda e, rw=rw: e.tensor_tensor(cand[:rw].rearrange("p h (i j) -> p h i j", i=16),
                                               sv4[:rw, :, 0, :].unsqueeze(3).broadcast_to([rw, 8, 16, 16]),
                                               sv4[:rw, :, 1, :].unsqueeze(2).broadcast_to([rw, 8, 16, 16]), ALU.add), svk, [pf + "cand"])
            ck = [f"{pf}c{h}" for h in range(8)]
            for h in range(8):
                V(lambda e, h=h, rw=rw: e.max(out=cs[:rw, h, 0:8], in_=cand[:rw, h, :]), [pf + "cand"], [ck[h] + "s"])
            for h in range(8):
                V(lambda e, h=h, rw=rw: e.max_index(out=cp[:rw, h, 0:8], in_max=cs[:rw, h, 0:8], in_values=cand[:rw, h, :]), [pf + "cand", ck[h] + "s"], [ck[h] + "p"])
            for h in range(8):
                V(lambda e, h=h, rw=rw: e.match_replace(out=cw[:rw, h, :], in_to_replace=cs[:rw, h, 0:8], in_values=cand[:rw, h, :], imm_value=NEG),
                  [pf + "cand", ck[h] + "s"], [ck[h] + "w"])
            for h in range(8):
                V(lambda e, h=h, rw=rw: e.max(out=cs[:rw, h, 8:16], in_=cw[:rw, h, :]), [ck[h] + "w"], [ck[h] + "s"])
            for h in range(8):
                V(lambda e, h=h, rw=rw: e.max_index(out=cp[:rw, h, 8:16], in_max=cs[:rw, h, 8:16], in_values=cw[:rw, h, :]), [ck[h] + "w", ck[h] + "s"], [ck[h] + "p"])
            csk = [c + "s" for c in ck]
            cpk = [c + "p" for c in ck]
            V(lambda e, rw=rw: e.tensor_copy(cpf[:rw], cp[:rw]), cpk, [pf + "cpf"])
            th4 = self.thr16[:rw, :].unsqueeze(1).unsqueeze(1).broadcast_to([rw, 8, 16, 16])
            V(lambda e, rw=rw, th4=th4: e.tensor_tensor(oh[:rw], cpf[:rw].unsqueeze(3).broadcast_to([rw, 8, 16, 16]), th4, ALU.is_ge), [pf + "cpf"] + self.CK, [pf + "oh"])
            V(lambda e, rw=rw: e.tensor_reduce(cif[:rw], oh[:rw], AX.X, ALU.add), [pf + "oh"], [pf + "cif"])
            V(lambda e, rw=rw: e.scalar_tensor_tensor(out=cjf[:rw], in0=cif[:rw], scalar=-16.0, in1=cpf[:rw], op0=ALU.mult, op1=ALU.add), [pf + "cif", pf + "cpf"], [pf + "cjf"])
            io4 = self.iota16[:rw, :].unsqueeze(1).unsqueeze(1).broadcast_to([rw, 8, 16, 16])
            for (cf, half, nn, kk) in ((cif, 0, n0, "n0"), (cjf, 1, n1, "n1")):
                V(lambda e, cf=cf, rw=rw, io4=io4: e.tensor_tensor(oh[:rw], cf[:rw].unsqueeze(3).broadcast_to([rw, 8, 16, 16]), io4, ALU.is_equal),
                  [pf + "cif", pf + "cjf"] + self.CK, [pf + "oh"])
                V(lambda e, half=half, rw=rw: e.tensor_tensor(oh[:rw], oh[:rw], sif[:rw, :, half, :].unsqueeze(2).broadcast_to([rw, 8, 16, 16]), ALU.mult),
                  [pf + "oh", pf + "sif"], [pf + "oh"])
                V(lambda e, nn=nn, rw=rw: e.tensor_reduce(nn[:rw], oh[:rw], AX.X, ALU.add), [pf + "oh"], [pf + kk])
            g3 = gwt[:rw, :].rearrange("p (h n) -> p h n", h=8)
            gk = pf + "gw"
            V(lambda e, rw=rw, g3=g3: e.tensor_tensor(g3, cs[:rw], cs[:rw, :, 0:1].broadcast_to([rw, 8, 16]), ALU.subtract), csk, [gk])
            A(lambda e, g3=g3: e.activation(out=g3, in_=g3, func=AF.Exp), [gk], [gk])
            V(lambda e, rw=rw, g3=g3: e.tensor_reduce(zz[:rw, 0:8], g3, AX.X, ALU.add), [gk], [pf + "zz"])
            V(lambda e, rw=rw: e.reciprocal(zz[:rw, 0:8], zz[:rw, 0:8]), [pf + "zz"], [pf + "zz"])
            V(lambda e, rw=rw, g3=g3: e.tensor_tensor(g3, g3, zz[:rw, 0:8].unsqueeze(2).broadcast_to([rw, 8, 16]), ALU.mult), [gk, pf + "zz"], [gk])
            if ti + 1 < len(TILES):
                scores(ti + 1)
            for a_, (srcap, kk) in enumerate(((n0, pf + "n0"), (n1, pf + "n1"), (None, gk))):
                sap = gwt[:rw, :] if srcap is None else srcap[:rw].rearrange("p h n -> p (h n)")
                T(lambda e, a_=a_, sap=sap, rw=rw: e.transpose(ps[0][:, a_ * 128:a_ * 128 + rw], sap, self.ident[:rw, :rw]), [kk] + self.CK, [f"{pf}psT{a_}"])
            V(lambda e, rw=rw: e.tensor_copy(itT[:, :, :rw], ps[0][:, 0:384].rearrange("p (a n) -> p a n", a=3)[:, :, :rw]),
              [f"{pf}psT{a_}" for a_ in range(3)], [pf + "itT"])
            for qt in range(rw // 32):
                pb_ = (ti * 4 + qt) % 2
                P0q, P1q = P0[:, pb_ * 32:(pb_ + 1) * 32, :], P1[:, pb_ * 32:(pb_ + 1) * 32, :]
                k0, k1 = f"{pf}P0_{pb_}", f"{pf}P1_{pb_}"
                for tt in range(32):
                    tg = qt * 32 + tt
                    V(lambda e, P0q=P0q, tt=tt, tg=tg: e.tensor_scalar(P0q[:, tt, :], io128b, itT[:, 0, tg:tg + 1], itT[:, 2, tg:tg + 1], ALU.is_equal, ALU.mult),
                      [pf + "itT", pf + "io128b"], [k0])
                    V(lambda e, P1q=P1q, tt=tt, tg=tg: e.tensor_scalar(P1q[:, tt, :], io128b, itT[:, 1, tg:tg + 1], None, ALU.is_equal),
                      [pf + "itT", pf + "io128b"], [k1])
                for t4 in range(8):
                    b = 2 + ectr[0] % 6
                    ectr[0] += 1
                    for k in range(4):
                        tt = t4 * 4 + k
                        T(lambda e, b=b, k=k, tt=tt, P0q=P0q, P1q=P1q: e.matmul(ps[b][:, k * 128:(k + 1) * 128], lhsT=P0q[:, tt, :], rhs=P1q[:, tt, :], start=True, stop=True),
                          [k0, k1], [psk[b]])
                    tg = qt * 32 + t4 * 4
                    A(lambda e, b=b, tg=tg: e.copy(Gbuf[:, :, tg:tg + 4], ps[b][:, :].rearrange("p (t i) -> p i t", t=4)), [psk[b]], [pf + "Gbuf"])
            self.stq(G_scr[ti].rearrange("p c t -> p (c t)"), Gbuf.rearrange("p i t -> p (i t)"), [pf + "Gbuf"], ["s_G"])
        self.S.barrier()
        ar.off = markP
        TB = ((0, 512), (512, 512), (1024, 128))
        acc = ar.f32(9 * D).rearrange("p (a n) -> p a n", a=9)
        UcTs = [(ar.bf16(16 * 128).rearrange("p (c n) -> p c n", c=16), f"{pf}U{i}") for i in range(3)]
        Vg = [(ar.bf16(4 * D).rearrange("p (j n) -> p j n", j=4), f"{pf}V{i}") for i in range(2)]
        Gg = [(ar.bf16(9 * 4 * 128).rearrange("p (a j t) -> p a j t", a=9, j=4), f"{pf}G{i}") for i in range(2)]
        Wg = [(ar.bf16(4 * NTP).rearrange("p (j n) -> p j n", j=4), f"{pf}W{i}") for i in range(2)]
        tmps = [(ar.bf16(512), f"{pf}tmp{i}") for i in range(3)]
        uT = self.peer_u
        v4 = self.peer_v.rearrange("(l i c) d -> l i c d", l=2, c=128)
        vctr = [0]
        actr = [0]

        def vside(grp):
            vg, vgk = Vg[grp % 2]
            wg, wgk = Wg[grp % 2]
            for ti in range(9):
                rw = TILES[ti][1]
                for db in range(4):
                    b = 5 + vctr[0] % 3
                    vctr[0] += 1
                    for j in range(4):
                        T(lambda e, b=b, j=j, ti=ti, db=db, wg=wg, vg=vg: e.matmul(ps[b][:, :], lhsT=wg[:, j, ti * 128:(ti + 1) * 128], rhs=vg[:, j, db * 512:(db + 1) * 512],
                                                                                   start=(j == 0), stop=(j == 3)), [wgk, vgk], [psk[b]])
                    ak = f"{pf}acc{ti}_{db}"
                    if grp == 0:
                        V(lambda e, b=b, ti=ti, db=db, rw=rw: e.tensor_copy(acc[:rw, ti, db * 512:(db + 1) * 512], ps[b][:rw, :]), [psk[b]], [ak])
                    else:
                        V(lambda e, b=b, ti=ti, db=db, rw=rw: e.tensor_tensor(acc[:rw, ti, db * 512:(db + 1) * 512], acc[:rw, ti, db * 512:(db + 1) * 512], ps[b][:rw, :], ALU.add),
                          [psk[b], ak], [ak])

        for grp in range(32):
            gg, ggk = Gg[grp % 2]
            vg, vgk = Vg[grp % 2]
            wg, wgk = Wg[grp % 2]
            self.ld(gg, G_scr[:, :, grp * 4:(grp + 1) * 4, :].rearrange("a p c t -> p a c t"), w=[ggk])
            self.ldc(vg, v4[li, :, grp * 4:(grp + 1) * 4, :], w=[vgk])
            for j in range(4):
                c = grp * 4 + j
                ut, utk = UcTs[c % 3]
                row0 = (li * 128 + c) * 2048
                self.ldc(ut, uT[row0:row0 + 2048, :].rearrange("(dc p) i -> p dc i", p=128), w=[utk])
                for bi, (t0, tn) in enumerate(TB):
                    b = actr[0] % 5
                    tmp, tmk = tmps[actr[0] % 3]
                    actr[0] += 1
                    for dc in range(16):
                        T(lambda e, b=b, dc=dc, t0=t0, tn=tn, ut=ut: e.matmul(ps[b][:, :tn], lhsT=ut[:, dc, :], rhs=xT[:, dc, t0:t0 + tn], start=(dc == 0), stop=(dc == 15)),
                          [utk, xTk], [psk[b]])
                    A(lambda e, b=b, tn=tn, tmp=tmp: e.activation(out=tmp[:, :tn], in_=ps[b][:, :tn], func=AF.Gelu_apprx_tanh), [psk[b]], [tmk])
                    a0, a1 = t0 // 128, (t0 + tn) // 128
                    V(lambda e, wg=wg, gg=gg, j=j, t0=t0, tn=tn, a0=a0, a1=a1, tmp=tmp: e.tensor_tensor(
                        wg[:, j, t0:t0 + tn].rearrange("p (a t) -> p a t", t=128), tmp[:, :tn].rearrange("p (a t) -> p a t", t=128),
                        gg[:, a0:a1, j, :], ALU.mult), [tmk, ggk], [wgk])
                if j == 0 and grp > 0:
                    vside(grp - 1)
        vside(31)
        self.S.barrier()
        ar.off = ar.off - 0
        ar2 = Arena(Vg[0][0].rearrange("p j n -> p (j n)").bitcast(F32))
        gt, bt = ar2.f32(D), None
        ar3 = Arena(Vg[1][0].rearrange("p j n -> p (j n)").bitcast(F32))
        bt = ar3.f32(D)
        ar4 = Arena(Wg[0][0].rearrange("p j n -> p (j n)").bitcast(F32))
        xr = [(ar4.f32(D), f"{pf}xr0")]
        ar5 = Arena(Wg[1][0].rearrange("p j n -> p (j n)").bitcast(F32))
        xr.append((ar5.f32(D), f"{pf}xr1"))
        ar6 = Arena(Gg[0][0].rearrange("p a j t -> p (a j t)").bitcast(F32))
        scr = (ar6.f32(24), ar6.f32(2), ar6.f32(1))
        self.ld(gt, self.ln["ln_ffn_g"][li:li + 1, :].partition_broadcast(128).rearrange("p o n -> p (o n)"), w=[pf + "g"])
        self.ld(bt, self.ln["ln_ffn_b"][li:li + 1, :].partition_broadcast(128).rearrange("p o n -> p (o n)"), w=[pf + "b"])
        for ti, (r0, rw) in enumerate(TILES):
            xb, xk = xr[ti % 2]
            aks = [f"{pf}acc{ti}_{db}" for db in range(4)]
            self.ld(xb[:rw, :], src[r0:r0 + rw, :], w=[xk])
            V(lambda e, rw=rw, xb=xb, ti=ti: e.scalar_tensor_tensor(out=xb[:rw, :], in0=xb[:rw, :], scalar=ALPHA, in1=acc[:rw, ti, :], op0=ALU.mult, op1=ALU.add), [xk] + aks, [xk])
            self.layer_norm_rows(xb, rw, pf + "g", pf + "b", gt, bt, xb, [xk], xk, scr)
            self.stq(dst[r0:r0 + rw, :], xb[:rw, :], [xk], [self.dname(dst)], q="pool")


def make_in_maps(inp):
    f = np.float32
    xp, xs = inp["x_prompt"], inp["x_sample"]
    maps = []
    ws = inp["w_s_b"][0]
    wsT = np.ascontiguousarray(ws.transpose(2, 0, 1))
    wsT_s = np.zeros((64, 8, 64), f)
    for b in range(16):
        wsT_s[b * 4:(b + 1) * 4, :, b * 4:(b + 1) * 4] = ws[:, 0:4, 0:4].transpose(2, 0, 1)
    b_s = inp["b_s_b"][0]
    b_s_t = np.ascontiguousarray(b_s.T)
    b_s_s = np.ascontiguousarray(np.tile(b_s[:, 0:4].T, (16, 1)))
    skT = np.ascontiguousarray(inp["peer_sub_keys"].reshape(2, 16, 128, 128).transpose(0, 3, 1, 2))
    shared = dict(
        consts=CONST_PACK,
        w_in_a=inp["w_in_a"][0], b_gate=inp["b_gate_a"], hn_gain=inp["hn_gain_a"].reshape(1, D),
        w_out_a=inp["w_out_a"][0], ln_mix_g=inp["ln_mix_g"], ln_mix_b=inp["ln_mix_b"],
        ln_ffn_g=inp["ln_ffn_g"], ln_ffn_b=inp["ln_ffn_b"],
        w_in_b=inp["w_in_b"][0], b_in_b=inp["b_in_b"], lnv_g=inp["lnv_g_b"], lnv_b=inp["lnv_b_b"],
        wsT=wsT, wsT_s=wsT_s, b_s_t=b_s_t, b_s_s=b_s_s, w_out_b=inp["w_out_b"][0],
        peer_wq=inp["peer_w_q"], skT=skT,
        peer_u=np.ascontiguousarray(inp["peer_u"].reshape(2, 128, 128, D).transpose(0, 2, 3, 1)).reshape(2 * 128 * D, 128),
        peer_v=inp["peer_v"].reshape(2 * 16384, D),
    )
    for c in range(NCORES):
        b, half = c // 2, c % 2
        m = dict(shared)
        m["x_own"] = np.concatenate([xp[b, half * 1024:(half + 1) * 1024], xs[16 * c:16 * c + 16].reshape(64, D)], axis=0)
        m["x_prev"] = np.ascontiguousarray(xp[b, (1 - half) * 1024:(2 - half) * 1024])
        m["flag"] = np.full((128, 1), float(half), f)
        m["C0s"] = np.ascontiguousarray(inp["state_mlstm_C"][0, 16 * c:16 * c + 16].reshape(16 * 8 * 128, 256))
        m["n0sT"] = np.ascontiguousarray(inp["state_mlstm_n"][0, 16 * c:16 * c + 16].reshape(128, 128).T)
        m["m0sT"] = np.ascontiguousarray(inp["state_mlstm_m"][0, 16 * c:16 * c + 16].T)
        maps.append(m)
    return maps


def assemble(results):
    f = np.float32
    y_p = np.zeros((4, 2048, D), f)
    y_s = np.zeros((128, 4, D), f)
    C_p = np.zeros((1, 4, 8, 128, 256), f)
    n_p = np.zeros((1, 4, 8, 128), f)
    m_p = np.zeros((1, 4, 8), f)
    C_s = np.zeros((1, 128, 8, 128, 256), f)
    n_s = np.zeros((1, 128, 8, 128), f)
    m_s = np.zeros((1, 128, 8), f)
    v_s = np.zeros((1, 128, 4, 6144), f)
    for c in range(NCORES):
        r = results[c]
        b, half = c // 2, c % 2
        y_p[b, half * 1024:(half + 1) * 1024] = r["y_own"][:1024]
        y_s[16 * c:16 * c + 16] = r["y_own"][1024:].reshape(16, 4, D)
        if half == 1:
            C_p[0, b] = r["Cp"].reshape(8, 128, 256)
            n_p[0, b] = r["npT"].T
            m_p[0, b] = r["mpT"][:, 0]
        C_s[0, 16 * c:16 * c + 16] = r["Cs"].reshape(16, 8, 128, 256)
        n_s[0, 16 * c:16 * c + 16] = r["nsT"].T.reshape(16, 8, 128)
        m_s[0, 16 * c:16 * c + 16] = r["msT"].T
        v_s[0, 16 * c:16 * c + 16] = r["vs"].reshape(16, 4, 6144)
    return (y_p, y_s, C_p, n_p, m_p, C_s, n_s, m_s, v_s)


_NC_CACHE = {}


def kernel(**inputs):
    inp = {k: np.asarray(v) for k, v in inputs.items()}
    if "nc" not in _NC_CACHE:
        _NC_CACHE["nc"] = Builder(debug=False).build()
    nc = _NC_CACHE["nc"]
    maps = make_in_maps(inp)
    res = run_bass_kernel_spmd(nc, maps, core_ids=list(range(NCORES)))
    return assemble(res.results)
```
